# Optimizing a Trainium2 kernel written in Bass

```python
import math
import jax, jax.numpy as jnp
from jax import lax
import numpy as np

D_MODEL = 1024
BATCH = 1
SEQ = 16384
DEPTH = 1
DEC_BATCH = 4
DEC_SEQ = 4096
PAST_LEN = 128

GRID_W = 64
N_Q_HEADS = 8
N_KV_HEADS = 2
HEAD_DIM = 64
ATTN_WIDTH = N_Q_HEADS * HEAD_DIM
KV_WIDTH = N_KV_HEADS * HEAD_DIM
Q_BLOCK = 128
ROPE_THETA = 10000.0
GLA_HEADS = 4
GLA_DK = 64
GLA_DV = 128
GLA_KEY_WIDTH = GLA_HEADS * GLA_DK
GLA_WIDTH = GLA_HEADS * GLA_DV
GATE_RANK = 16
GATE_NORMALIZER = 16.0
CHUNK = 64
MIX_WIDTH = ATTN_WIDTH + GLA_WIDTH
SPLIT_SIZES = (ATTN_WIDTH, KV_WIDTH, KV_WIDTH, ATTN_WIDTH,
               GLA_KEY_WIDTH, GLA_KEY_WIDTH, GLA_WIDTH, GLA_WIDTH,
               GATE_RANK, GATE_RANK)
IN_WIDTH = 2848
PLE_DIM = 256
EPS = 1e-6

kernel_name = "hymba_gqa_axialrope_bigla_encoder"


def rms_norm(x, w):
    xf = x.astype(jnp.float32)
    y = xf * lax.rsqrt(jnp.mean(xf * xf, axis=-1, keepdims=True) + EPS)
    return (y * w.astype(jnp.float32)).astype(x.dtype)


def axial_rope_tables(T):
    rows = T // GRID_W
    row_idx = jnp.repeat(jnp.arange(rows, dtype=jnp.float32), GRID_W)
    col_idx = jnp.tile(jnp.arange(GRID_W, dtype=jnp.float32), rows)
    half = HEAD_DIM // 2
    inv_freq = ROPE_THETA ** (-jnp.arange(0, half, 2, dtype=jnp.float32) / half)
    ang_r = row_idx[:, None] * inv_freq[None, :]
    ang_c = col_idx[:, None] * inv_freq[None, :]
    return jnp.cos(ang_r), jnp.sin(ang_r), jnp.cos(ang_c), jnp.sin(ang_c)


def apply_axial_rope(x, tables):
    cr, sr, cc, sc = (t[None, :, None, :] for t in tables)
    x_r1, x_r2, x_c1, x_c2 = jnp.split(x, 4, axis=-1)
    return jnp.concatenate([x_r1 * cr - x_r2 * sr, x_r2 * cr + x_r1 * sr,
                            x_c1 * cc - x_c2 * sc, x_c2 * cc + x_c1 * sc], axis=-1)


def block_attention(q, k, v):
    B, T, _, D = q.shape
    G = N_Q_HEADS // N_KV_HEADS
    nb = T // Q_BLOCK
    scale = D ** -0.5
    qb = q.reshape(B, nb, Q_BLOCK, N_KV_HEADS, G, D).transpose(1, 0, 3, 4, 2, 5)

    def one_block(q_blk):
        s = jnp.einsum('bkgqd,btkd->bkgqt', q_blk, k).astype(jnp.float32) * scale
        p = jax.nn.softmax(s, axis=-1)
        return jnp.einsum('bkgqt,btkd->bkgqd', p.astype(v.dtype), v)

    o = lax.map(one_block, qb)
    return o.transpose(1, 0, 4, 2, 3, 5).reshape(B, T, N_Q_HEADS * D)


def gla_chunked(q, k, v, g, include_diag):
    B, T, H, DK = q.shape
    DV = v.shape[-1]
    N = T // CHUNK

    def to_chunks(a):
        return a.astype(jnp.float32).reshape(B, N, CHUNK, H, a.shape[-1]).transpose(0, 3, 1, 2, 4)

    q, k, v, g = to_chunks(q) * (DK ** -0.5), to_chunks(k), to_chunks(v), to_chunks(g)
    b = jnp.cumsum(g, axis=3)
    b_last = b[:, :, :, -1:, :]
    q_dec = q * jnp.exp(b)
    a = jnp.einsum('bhncd,bhnsd->bhncs', q_dec, k * jnp.exp(-b))
    mask = jnp.tril(jnp.ones((CHUNK, CHUNK), dtype=bool), k=0 if include_diag else -1)
    a = jnp.where(mask, a, 0.0)
    o_intra = jnp.einsum('bhncs,bhnse->bhnce', a, v)
    kv = jnp.einsum('bhncd,bhnce->bhnde', k * jnp.exp(b_last - b), v)
    decay = jnp.exp(b_last[:, :, :, 0, :])

    def step(state, inp):
        dec, kv_c = inp
        return dec[..., None] * state + kv_c, state

    _, s_prev = lax.scan(step, jnp.zeros((B, H, DK, DV), jnp.float32),
                         (jnp.moveaxis(decay, 2, 0), jnp.moveaxis(kv, 2, 0)))
    s_prev = jnp.moveaxis(s_prev, 0, 2)
    o_inter = jnp.einsum('bhncd,bhnde->bhnce', q_dec, s_prev)
    return (o_intra + o_inter).transpose(0, 2, 3, 1, 4).reshape(B, T, H, DV)


def hybrid_layer(h, p, mix_norm, w_in, q_norm, k_norm, w_gate_up_fwd, b_gate_fwd,
                 w_gate_up_bwd, b_gate_bwd, gla_norm, w_out, ple_norm, w_ple_gate, w_ple_proj):
    B, T, _ = h.shape
    xn = rms_norm(h, mix_norm)
    z = xn @ w_in
    idx, acc = [], 0
    for s in SPLIT_SIZES[:-1]:
        acc += s
        idx.append(acc)
    (a_q, a_k, a_v, a_gate, l_q, l_k, l_v, l_gate, lr_f, lr_b) = jnp.split(z, idx, axis=-1)

    tables = axial_rope_tables(T)
    q = rms_norm(a_q.reshape(B, T, N_Q_HEADS, HEAD_DIM).astype(jnp.float32), q_norm)
    k = rms_norm(a_k.reshape(B, T, N_KV_HEADS, HEAD_DIM).astype(jnp.float32), k_norm)
    q = apply_axial_rope(q, tables)
    k = apply_axial_rope(k, tables)
    v = a_v.reshape(B, T, N_KV_HEADS, HEAD_DIM).astype(jnp.float32)
    o_attn = block_attention(q, k, v).astype(h.dtype)

    gq = l_q.reshape(B, T, GLA_HEADS, GLA_DK)
    gk = l_k.reshape(B, T, GLA_HEADS, GLA_DK)
    gv = l_v.reshape(B, T, GLA_HEADS, GLA_DV)
    g_f = jax.nn.log_sigmoid((lr_f @ w_gate_up_fwd + b_gate_fwd).astype(jnp.float32)) / GATE_NORMALIZER
    g_b = jax.nn.log_sigmoid((lr_b @ w_gate_up_bwd + b_gate_bwd).astype(jnp.float32)) / GATE_NORMALIZER
    g_f = g_f.reshape(B, T, GLA_HEADS, GLA_DK)
    g_b = g_b.reshape(B, T, GLA_HEADS, GLA_DK)
    o_fwd = gla_chunked(gq, gk, gv, g_f, include_diag=True)
    flip = lambda a: jnp.flip(a, axis=1)
    o_bwd = flip(gla_chunked(flip(gq), flip(gk), flip(gv), flip(g_b), include_diag=False))
    o_gla = rms_norm(o_fwd + o_bwd, gla_norm).reshape(B, T, GLA_WIDTH).astype(h.dtype)

    mixed = jnp.concatenate([o_attn * jax.nn.silu(a_gate), o_gla * jax.nn.silu(l_gate)], axis=-1)
    h = h + mixed @ w_out

    gate = jax.nn.sigmoid(rms_norm(h, ple_norm) @ w_ple_gate)
    return h + gate * (p @ w_ple_proj)


def setup_inputs(seed: int = 0) -> dict:
    key = jax.random.key(seed)
    ks = jax.random.split(key, 20)
    nrm = lambda k_, shape, s: jax.random.normal(k_, shape, jnp.float32) * s
    gain = lambda k_, shape: 1.0 + 0.05 * jax.random.normal(k_, shape, jnp.float32)
    return {
        "x_prompt": nrm(ks[0], (BATCH, SEQ, D_MODEL), 1.0),
        "x_sample": nrm(ks[1], (DEC_BATCH, DEC_SEQ, D_MODEL), 1.0),
        "p_prompt": nrm(ks[2], (DEPTH, BATCH, SEQ, PLE_DIM), 1.0),
        "p_sample": nrm(ks[3], (DEPTH, DEC_BATCH, DEC_SEQ, PLE_DIM), 1.0),
        "mix_norm": gain(ks[4], (DEPTH, D_MODEL)),
        "w_in": nrm(ks[5], (DEPTH, D_MODEL, IN_WIDTH), D_MODEL ** -0.5),
        "q_norm": gain(ks[6], (DEPTH, HEAD_DIM)),
        "k_norm": gain(ks[7], (DEPTH, HEAD_DIM)),
        "w_gate_up_fwd": nrm(ks[8], (DEPTH, GATE_RANK, GLA_KEY_WIDTH), GATE_RANK ** -0.5),
        "b_gate_fwd": nrm(ks[9], (DEPTH, GLA_KEY_WIDTH), 0.1),
        "w_gate_up_bwd": nrm(ks[10], (DEPTH, GATE_RANK, GLA_KEY_WIDTH), GATE_RANK ** -0.5),
        "b_gate_bwd": nrm(ks[11], (DEPTH, GLA_KEY_WIDTH), 0.1),
        "gla_norm": gain(ks[12], (DEPTH, GLA_DV)),
        "w_out": nrm(ks[13], (DEPTH, MIX_WIDTH, D_MODEL), MIX_WIDTH ** -0.5),
        "ple_norm": gain(ks[14], (DEPTH, D_MODEL)),
        "w_ple_gate": nrm(ks[15], (DEPTH, D_MODEL, D_MODEL), D_MODEL ** -0.5),
        "w_ple_proj": nrm(ks[16], (DEPTH, PLE_DIM, D_MODEL), PLE_DIM ** -0.5),
        "final_norm": gain(ks[17], (D_MODEL,)),
    }


def reference(x_prompt, x_sample, p_prompt, p_sample, mix_norm, w_in, q_norm, k_norm,
              w_gate_up_fwd, b_gate_fwd, w_gate_up_bwd, b_gate_bwd, gla_norm, w_out,
              ple_norm, w_ple_gate, w_ple_proj, final_norm):
    hp, hs = x_prompt, x_sample
    for i in range(DEPTH):
        lw = (mix_norm[i], w_in[i], q_norm[i], k_norm[i], w_gate_up_fwd[i], b_gate_fwd[i],
              w_gate_up_bwd[i], b_gate_bwd[i], gla_norm[i], w_out[i], ple_norm[i],
              w_ple_gate[i], w_ple_proj[i])
        hp = hybrid_layer(hp, p_prompt[i], *lw)
        hs = hybrid_layer(hs, p_sample[i], *lw)
    y_prompt = rms_norm(hp, final_norm)
    y_sample = rms_norm(hs, final_norm)
    return (y_prompt, y_sample)
```

```python
import sys
import numpy as np
import ml_dtypes
from contextlib import ExitStack
import concourse.bass as bass
import concourse.mybir as mybir
from concourse.bass_utils import run_bass_kernel_spmd

F32 = mybir.dt.float32
BF16 = mybir.dt.bfloat16
ALU = mybir.AluOpType
AF = mybir.ActivationFunctionType
AX = mybir.AxisListType

NCORES = 8
D = 1024
TOK_OWN = 4096
NT_OWN = 32
NT_CTX = 160
EPS = 1e-6
ND = 8
MAX_OPS = None


class Buf:
    __slots__ = ("name", "lw", "rd", "excl")

    def __init__(self, name, excl=False):
        self.name = name
        self.excl = excl
        self.lw = None
        self.rd = []


class Op:
    __slots__ = ("eng", "fn", "deps", "signal", "sigval", "dma", "dma_idx", "line")


class Sched:
    ENGS = ("sp", "pe", "dve", "act", "pool")

    def __init__(self):
        self.ops = []
        self.ndma = 0

    def add(self, eng, fn, R=(), W=(), dma=False):
        idx = len(self.ops)
        xr = [b for b in R if b.excl]
        if xr:
            R = [b for b in R if not b.excl]
            W = list(W) + [b for b in xr if b not in W]
        raw, other = set(), set()
        for b in R:
            if b.lw is not None:
                raw.add(b.lw)
        for b in W:
            if b.lw is not None:
                other.add(b.lw)
            for r in b.rd:
                other.add(r)
        deps = set()
        for d in raw | other:
            p = self.ops[d]
            if p.dma or dma:
                deps.add(d)
            elif p.eng != eng:
                deps.add(d)
            elif eng != "pe":
                deps.add(d)
        op = Op()
        op.eng, op.fn, op.deps, op.signal, op.sigval, op.dma = eng, fn, deps, False, 0, dma
        op.dma_idx = -1
        op.line = sys._getframe(2).f_lineno
        if dma:
            op.dma_idx = self.ndma
            self.ndma += 1
        self.ops.append(op)
        for b in R:
            b.rd.append(idx)
        for b in W:
            b.lw = idx
            b.rd = []
        return idx

    def emit(self, nc, stack):
        if MAX_OPS is not None:
            self.ops = self.ops[:MAX_OPS]
            self.ndma = sum(1 for o in self.ops if o.dma)
        ops = self.ops
        for op in ops:
            best = {}
            keep = set()
            for d in op.deps:
                p = ops[d]
                if p.dma:
                    keep.add(d)
                else:
                    if p.eng not in best or best[p.eng] < d:
                        best[p.eng] = d
            keep |= set(best.values())
            op.deps = keep
            for d in keep:
                ops[d].signal = True
        cnt = {e: 0 for e in self.ENGS}
        for op in ops:
            if op.dma:
                op.signal = True
            elif op.signal:
                cnt[op.eng] += 1
                op.sigval = cnt[op.eng]
        sems = {e: stack.enter_context(nc.semaphore("s_" + e)) for e in self.ENGS}
        dsems = [stack.enter_context(nc.semaphore("d%d" % i)) for i in range(ND)]
        block = stack.enter_context(nc.Block())
        ndma = self.ndma

        def run(engname):
            def body(e):
                waited = {}

                def wait(sem, val):
                    k = id(sem)
                    if waited.get(k, 0) < val:
                        e.wait_ge(sem, val)
                        waited[k] = val

                for op in ops:
                    if op.eng != engname:
                        continue
                    for d in sorted(op.deps):
                        p = ops[d]
                        if p.dma:
                            wait(dsems[p.dma_idx % ND], 16 * (p.dma_idx // ND + 1))
                        else:
                            wait(sems[p.eng], p.sigval)
                    if op.dma:
                        j = op.dma_idx
                        if j >= ND:
                            wait(dsems[j % ND], 16 * (j // ND))
                        op.fn(e).then_inc(dsems[j % ND], 16)
                    else:
                        ins = op.fn(e)
                        if op.signal:
                            ins.then_inc(sems[engname], 1)
                if engname == "sp":
                    for i in range(ND):
                        n = (ndma - i + ND - 1) // ND if ndma > i else 0
                        if n > 0:
                            wait(dsems[i], 16 * n)
            return body

        block.sync(run("sp"))
        block.tensor(run("pe"))
        block.vector(run("dve"))
        block.scalar(run("act"))
        block.gpsimd(run("pool"))


def build_program(segs=((128, 16), (32, 16))):
    NT_CTX = sum(a for a, _ in segs)
    NT_OWN = sum(b for _, b in segs)
    TOK_OWN = NT_OWN * 128
    MAXC = max(a for a, _ in segs)
    nc = bass.Bass("TRN2", target_bir_lowering=False)
    S = Sched()
    st = ExitStack()

    def din(name, shape, dt=F32):
        return nc.dram_tensor(name, list(shape), dt, kind="ExternalInput").ap()

    x_all = din("x_all", [NT_CTX * 128, D])
    x_own = din("x_own", [TOK_OWN, D])
    p_own = din("p_own", [TOK_OWN, 256])
    tab_all = din("tab_all", [NT_CTX * 128, 128])
    tab_own = din("tab_own", [TOK_OWN, 128])
    masks_d = din("masks", [128, NT_CTX * 2])
    w_in = din("w_in", [D, 2848])
    w_out = din("w_out", [D, D])
    w_pg = din("w_pg", [D, D])
    w_pp = din("w_pp", [256, D])
    wup_d = [din("wup_f", [16, 256]), din("wup_b", [16, 256])]
    bg_d = [din("bg_f", [1, 256]), din("bg_b", [1, 256])]
    mixn_d = din("mixn", [128, 8])
    plen_d = din("plen", [128, 8])
    qn_d = din("qn", [128, 64])
    kn_d = din("kn", [128, 64])
    gn_d = din("gn", [128, 128])
    fn_d = din("fn", [128, D])
    ident_d = din("ident", [128, 128], BF16)
    tri_d = din("tri", [128, 4 * 128])
    y_out = nc.dram_tensor("y", [TOK_OWN, D], F32, kind="ExternalOutput").ap()
    obwd = nc.dram_tensor("obwd", [TOK_OWN, 512], F32)

    bufs = {}

    def sb(name, shape, dt=F32):
        t = st.enter_context(nc.sbuf_tensor("sb_" + name, list(shape), dt))
        bufs[name] = Buf(name)
        return t, bufs[name]

    Win, bWin = sb("Win", [128, 8, 2848], BF16)
    Wout, bWout = sb("Wout", [128, 8, D], BF16)
    Wpg, bWpg = sb("Wpg", [128, 8, D], BF16)
    Wpp, bWpp = sb("Wpp", [128, 2, D], BF16)
    Wup, bWup = sb("Wup", [16, 2, 256], BF16)
    bgt, bbg = sb("bgt", [1, 2, 256], BF16)
    ident, bident = sb("ident", [128, 128], BF16)
    tri, btri = sb("tri", [128, 4, 128], F32)
    ones_bf, bones_bf = sb("ones_bf", [128, 128], BF16)
    ones_f, bones_f = sb("ones_f", [128, 64], F32)
    qn, bqn = sb("qn", [128, 64])
    kn, bkn = sb("kn", [128, 64])
    gn, bgn = sb("gn", [128, 128])
    masks, bmasks = sb("masks", [128, NT_CTX * 2])
    mixn, bmixn = sb("mixn", [128, 8])
    plen, bplen = sb("plen", [128, 8])
    KT, bKT = sb("KT", [128, MAXC * 128], BF16)
    VO, bVO = sb("VO", [128, MAXC, 2, 66], BF16)
    T = []
    bT = []
    for i in range(5):
        t, b = sb("T%d" % i, [128, 1024], F32)
        T.append(t)
        bT.append(b)
    xt, bxt = sb("xt", [128, D])
    xs, bxs = sb("xs", [128, D], BF16)
    xT, bxT = sb("xT", [128, 8, 128], BF16)
    tab, btab = sb("tab", [128, 128])
    pt, bpt = sb("pt", [128, 256])
    pb, bpb = sb("pb", [128, 256], BF16)
    pT, bpT = sb("pT", [128, 2, 128], BF16)
    kr, bkr = sb("kr", [128, 128], BF16)
    lkA, blkA, vbA, bvbA, lrTA, blrTA = [], [], [], [], [], []
    for i in range(2):
        t, b = sb("lkA%d" % i, [128, 256]); lkA.append(t); blkA.append(b)
        t, b = sb("vbA%d" % i, [128, 512], BF16); vbA.append(t); bvbA.append(b)
        t, b = sb("lrTA%d" % i, [16, 2, 128], BF16); lrTA.append(t); blrTA.append(b)
    dcy2, bdcy2, acf2, bacf2 = [], [], [], []
    for i in range(2):
        t, b = sb("dcyA%d" % i, [128, 2]); dcy2.append(t); bdcy2.append(b)
        t, b = sb("acfA%d" % i, [128, 2]); acf2.append(t); bacf2.append(b)
    vb, bvb = sb("vb", [128, 512], BF16)
    lrT, blrT = sb("lrT", [16, 2, 128], BF16)
    gq, bgq = sb("gq", [128, 3, 256], BF16)
    qkT, bqkT = sb("qkT", [128, 4, 128], BF16)
    AT, bAT = sb("AT", [128, 4, 128], BF16)
    Sbf, bSbf = sb("Sbf", [128, 2, 2, 128], BF16)
    qdz, bqdz = sb("qdz", [128, 2, 2, 128], BF16)
    Sf, bSf = sb("Sf", [128, 2, 128])
    Lb, bLb = sb("Lb", [128, 2, 128])
    Pb, bPb = sb("Pb", [128, 2])
    dcy, bdcy = sb("dcy", [128, 2])
    acf, bacf = sb("acf", [128, 2])
    st1, bst1 = sb("st1", [128, 8])
    st2, bst2 = sb("st2", [128, 8])
    st3, bst3 = sb("st3", [128, 8])
    st4, bst4 = sb("st4", [128, 8])
    stA, bstA = [], []
    for i in range(2):
        t, b = sb("stA%d" % i, [128, 2]); stA.append(t); bstA.append(b)
    st5, bst5 = sb("st5", [128, 8])
    mg, bmg = sb("mg", [128, 512], BF16)
    mixT, bmixT = sb("mixT", [128, 8, 128], BF16)
    qr, bqr = sb("qr", [128, 4, 2, 64], BF16)
    QTz, bQTz = sb("QTz", [128, 2, 4, 128], BF16)
    PTb = []
    bPTb = []
    for i in range(2):
        t, b = sb("PTb%d" % i, [128, 2, 512], BF16)
        PTb.append(t)
        bPTb.append(b)

    PS = st.enter_context(nc.psum_tensor("PS", [128, 6, 512], F32))
    PT = st.enter_context(nc.psum_tensor("PT", [128, 2, 1024], BF16))
    bPS = [Buf("PS%d" % i, True) for i in range(6)]
    bPTs = [Buf("PT%d" % i, True) for i in range(2)]
    rr = {"ps": 0, "pt": 0, "dq": 0}

    def nps():
        i = rr["ps"]
        rr["ps"] = (i + 1) % 6
        return i

    def npt():
        i = rr["pt"]
        rr["pt"] = (i + 1) % 2
        return i

    def dma(out, in_, R, W):
        S.add("sp", lambda e, o=out, i=in_: e.dma_start(out=o, in_=i), R, W, dma=True)

    def mm(out, lhsT, rhs, start, stop, R, W):
        S.add("pe", lambda e, o=out, l=lhsT, r=rhs, s0=start, s1=stop:
              e.matmul(o, lhsT=l, rhs=r, start=s0, stop=s1), R, W)

    def tr(out, in_, R, W):
        S.add("pe", lambda e, o=out, i=in_: e.transpose(o, i, ident[:, :]), list(R) + [bident], W)

    def act(out, in_, func, R, W, scale=1.0, bias=0.0):
        S.add("act", lambda e, o=out, i=in_, f=func, s=scale, b=bias:
              e.activation(out=o, in_=i, func=f, bias=b, scale=s), R, W)

    def tt(out, in0, in1, op, R, W, eng="dve"):
        S.add(eng, lambda e, o=out, a=in0, b=in1, p=op: e.tensor_tensor(out=o, in0=a, in1=b, op=p), R, W)

    def ts(out, in0, s1, s2, op0, op1, R, W, eng="dve"):
        if s2 is None:
            S.add(eng, lambda e, o=out, a=in0, x=s1, p0=op0:
                  e.tensor_scalar(out=o, in0=a, scalar1=x, scalar2=None, op0=p0), R, W)
        else:
            S.add(eng, lambda e, o=out, a=in0, x=s1, y=s2, p0=op0, p1=op1:
                  e.tensor_scalar(out=o, in0=a, scalar1=x, scalar2=y, op0=p0, op1=p1), R, W)

    def stt(out, in0, scalar, in1, op0, op1, R, W, accum=None):
        if accum is None:
            S.add("dve", lambda e, o=out, a=in0, s=scalar, b=in1, p0=op0, p1=op1:
                  e.scalar_tensor_tensor(out=o, in0=a, scalar=s, in1=b, op0=p0, op1=p1), R, W)
        else:
            S.add("dve", lambda e, o=out, a=in0, s=scalar, b=in1, p0=op0, p1=op1, ac=accum:
                  e.scalar_tensor_tensor(out=o, in0=a, scalar=s, in1=b, op0=p0, op1=p1, accum_out=ac), R, W)

    def cp(out, in_, R, W, eng="dve"):
        if eng == "act":
            S.add("act", lambda e, o=out, i=in_: e.copy(out=o, in_=i), R, W)
        else:
            S.add(eng, lambda e, o=out, i=in_: e.tensor_copy(out=o, in_=i), R, W)

    def red(out, in_, R, W):
        S.add("dve", lambda e, o=out, i=in_: e.tensor_reduce(out=o, in_=i, axis=AX.X, op=ALU.add), R, W)

    def recip(out, in_, R, W):
        S.add("dve", lambda e, o=out, i=in_: e.reciprocal(out=o, in_=i), R, W)

    def memset(ap, val, W, eng="dve"):
        S.add(eng, lambda e, a=ap, v=val: e.memset(a, v), (), W)

    def rstd_from_ss(ss_ap, out_ap, n, bss, bout):
        act(out_ap, ss_ap, AF.Ln, [bss], [bout], scale=1.0 / n, bias=EPS)
        act(out_ap, out_ap, AF.Exp, [bout], [bout], scale=-0.5)

    dma(ident[:, :], ident_d[:, :], [], [bident])
    dma(tri[:, :, :], tri_d.rearrange("p (a b) -> p a b", a=4), [], [btri])
    dma(qn[:, :], qn_d[:, :], [], [bqn])
    dma(kn[:, :], kn_d[:, :], [], [bkn])
    dma(gn[:, :], gn_d[:, :], [], [bgn])
    dma(masks[:, :], masks_d[:, :], [], [bmasks])
    dma(mixn[:, :], mixn_d[:, :], [], [bmixn])
    dma(plen[:, :], plen_d[:, :], [], [bplen])
    memset(ones_bf[:, :], 1.0, [bones_bf])
    memset(ones_f[:, :], 1.0, [bones_f])
    memset(VO[:, :, :, 64:66], 1.0, [bVO])
    memset(QTz[:, :, :, :], 0.0, [bQTz])
    memset(Sbf[:, :, :, :], 0.0, [bSbf])
    memset(qdz[:, :, :, :], 0.0, [bqdz])

    k = 0
    for kc in range(8):
        for (c0, c1) in ((0, 1024), (1024, 2048), (2048, 2848)):
            n = c1 - c0
            ti = k % 2
            k += 1
            dma(T[ti][:, 0:n], w_in[kc * 128:(kc + 1) * 128, c0:c1], [], [bT[ti]])
            ts(Win[:, kc, c0:c1], T[ti][:, 0:n], mixn[:, kc:kc + 1], None, ALU.mult, None,
               [bT[ti], bmixn], [bWin])
    for kc in range(8):
        ti = k % 2
        k += 1
        dma(T[ti][:, :], w_pg[kc * 128:(kc + 1) * 128, :], [], [bT[ti]])
        ts(Wpg[:, kc, :], T[ti][:, :], plen[:, kc:kc + 1], None, ALU.mult, None, [bT[ti], bplen], [bWpg])
    for kc in range(8):
        ti = k % 2
        k += 1
        dma(T[ti][:, :], w_out[kc * 128:(kc + 1) * 128, :], [], [bT[ti]])
        cp(Wout[:, kc, :], T[ti][:, :], [bT[ti]], [bWout])
    for kc in range(2):
        ti = k % 2
        k += 1
        dma(T[ti][:, :], w_pp[kc * 128:(kc + 1) * 128, :], [], [bT[ti]])
        cp(Wpp[:, kc, :], T[ti][:, :], [bT[ti]], [bWpp])
    for d in range(2):
        dma(T[2][0:16, d * 256:(d + 1) * 256], wup_d[d][:, :], [], [bT[2]])
        dma(T[3][0:1, d * 256:(d + 1) * 256], bg_d[d][:, :], [], [bT[3]])
    cp(Wup[:, :, :], T[2][0:16, 0:512].rearrange("p (a b) -> p a b", a=2), [bT[2]], [bWup])
    cp(bgt[:, :, :], T[3][0:1, 0:512].rearrange("p (a b) -> p a b", a=2), [bT[3]], [bbg])

    def front_g(x_rows_ap, load=True):
        if load:
            dma(xt[:, :], x_rows_ap, [], [bxt])
        stt(xs[:, :], xt[:, :], 1.0, xt[:, :], ALU.mult, ALU.mult, [bxt], [bxs, bst1], accum=st1[:, 0:1])
        yield
        rstd_from_ss(st1[:, 0:1], st1[:, 1:2], float(D), bst1, bst1)
        act(xs[:, :], xt[:, :], AF.Copy, [bxt, bst1], [bxs], scale=st1[:, 1:2])
        yield
        transpose8(xs, bxs, xT, bxT)
        yield

    def front(x_rows_ap):
        for _ in front_g(x_rows_ap):
            pass

    def transpose8(src, bsrc, dst, bdst):
        b = npt()
        for c in range(8):
            tr(PT[:, b, c * 128:(c + 1) * 128], src[:, c * 128:(c + 1) * 128], [bsrc], [bPTs[b]])
        cp(dst[:, :, :], PT[:, b, :].rearrange("p (a b) -> p a b", a=8), [bPTs[b]], [bdst], eng="act")

    def proj_tok(bank, off, c0, c1):
        n = c1 - c0
        for kc in range(8):
            mm(PS[:, bank, off:off + n], xT[:, kc, :], Win[:, kc, c0:c1], kc == 0, kc == 7,
               [bxT, bWin], [bPS[bank]])

    def proj_feat(bank, off, c0, c1, p0=0):
        m = c1 - c0
        for kc in range(8):
            mm(PS[p0:p0 + m, bank, off:off + 128], Win[:, kc, c0:c1], xT[:, kc, :], kc == 0, kc == 7,
               [bxT, bWin], [bPS[bank]])

    def norm_rope_g(src, bsrc, nh, wn, bwn, dst, bdst, tmp, btmp, gj=None, tabs=None):
        n = nh * 64
        tab_t, btab_t = tabs if tabs is not None else (tab, btab)
        sq = tmp[:, 0:n]
        yv = tmp[:, n:2 * n]
        tt(sq, src, src, ALU.mult, [bsrc], [btmp])
        red(st2[:, 0:nh], sq.rearrange("p (h d) -> p h d", h=nh), [btmp], [bst2])
        yield
        rstd_from_ss(st2[:, 0:nh], st3[:, 0:nh], 64.0, bst2, bst3)
        yield
        y3 = yv.rearrange("p (h d) -> p h d", h=nh)
        tt(y3, src.rearrange("p (h d) -> p h d", h=nh),
           st3[:, 0:nh].unsqueeze(2).to_broadcast([128, nh, 64]), ALU.mult, [bsrc, bst3], [btmp])
        tt(y3, y3, wn[:, :].unsqueeze(1).to_broadcast([128, nh, 64]), ALU.mult, [btmp, bwn], [btmp])
        yield
        y5 = yv.rearrange("p (h a b c) -> p h a b c", h=nh, a=2, b=2)
        s5 = sq.rearrange("p (h a b c) -> p h a b c", h=nh, a=2, b=2)
        cos4 = tab_t[:, 0:64].rearrange("p (a b c) -> p a b c", a=2, b=2)
        sin4 = tab_t[:, 64:128].rearrange("p (a b c) -> p a b c", a=2, b=2)
        for hf in range(2):
            tt(s5[:, :, :, hf, :], y5[:, :, :, 1 - hf, :],
               sin4[:, :, hf, :].unsqueeze(1).to_broadcast([128, nh, 2, 16]), ALU.mult,
               [btmp, btab_t], [btmp])
        tt(y3, y3, tab_t[:, 0:64].unsqueeze(1).to_broadcast([128, nh, 64]), ALU.mult, [btmp, btab_t], [btmp])
        if gj is None:
            tt(dst, y3, sq.rearrange("p (h d) -> p h d", h=nh), ALU.add, [btmp], [bdst])
        else:
            g_, j_ = gj
            tt(dst, yv.rearrange("p (g j d) -> p g j d", g=g_, j=j_),
               sq.rearrange("p (g j d) -> p g j d", g=g_, j=j_), ALU.add, [btmp], [bdst])

    def norm_rope(*a, **k):
        for _ in norm_rope_g(*a, **k):
            pass

    TRI_CS = (0, 2)
    TRI_REM = (1, 3)
    TRI_MASK = (0, 1)

    def kvblk(bk, p, hf):
        return PS[hf * 64:(hf + 1) * 64, bk, p * 256 + hf * 128:p * 256 + hf * 128 + 128]

    def gla_full_g(d, St, bSt, E, bE, lk_ap, blk, lq_ap, blq, vb_t, bvb_t, lr_ap, blr):
        bx, bc, ba = 3, 4, 5
        for r in range(2):
            pr = slice(r * 64, (r + 1) * 64)
            cp(Sbf[pr, :, r, :], St[pr, :, :], [bSt], [bSbf])
        mm(PS[:, bx, 0:256], lr_ap, Wup[0:16, d, :], True, False, [blr, bWup], [bPS[bx]])
        mm(PS[:, bx, 0:256], ones_bf[0:1, 0:128], bgt[0:1, d, :], False, True, [bones_bf, bbg], [bPS[bx]])
        yield
        act(E[:, 0:256], PS[:, bx, 0:256], AF.Exp, [bPS[bx]], [bE], scale=-1.0)
        act(E[:, 0:256], E[:, 0:256], AF.Ln, [bE], [bE], bias=1.0)
        yield
        mm(PS[:, bc, 0:256], tri[:, TRI_CS[d], :], E[:, 0:256], True, True, [btri, bE], [bPS[bc]])
        mm(PS[:, bc, 256:512], tri[:, TRI_REM[d], :], E[:, 0:256], True, True, [btri, bE], [bPS[bc]])
        for p in range(2):
            mm(PS[:, bx, 256 + p:257 + p], E[:, p * 128:(p + 1) * 128], ones_f[:, 0:1], True, True,
               [bE, bones_f], [bPS[bx]])
        yield
        act(E[:, 256:512], PS[:, bc, 0:256], AF.Exp, [bPS[bc]], [bE], scale=-1.0 / 16)
        act(E[:, 768:1024], PS[:, bc, 0:256], AF.Exp, [bPS[bc]], [bE], scale=1.0 / 16)
        act(E[:, 512:768], PS[:, bc, 256:512], AF.Exp, [bPS[bc]], [bE], scale=-1.0 / 16)
        act(dcy[:, 0:2], PS[:, bx, 256:258], AF.Exp, [bPS[bx]], [bdcy], scale=-1.0 / 16)
        yield
        stt(gq[:, 0, :], lq_ap, 0.125, E[:, 256:512], ALU.mult, ALU.mult, [blq, bE], [bgq])
        tt(gq[:, 1, :], lk_ap, E[:, 768:1024], ALU.mult, [blk, bE], [bgq])
        tt(gq[:, 2, :], lk_ap, E[:, 512:768], ALU.mult, [blk, bE], [bgq])
        yield
        b = npt()
        for i in range(2):
            for p in range(2):
                tr(PT[:, b, (i * 2 + p) * 128:(i * 2 + p + 1) * 128], gq[:, i, p * 128:(p + 1) * 128],
                   [bgq], [bPTs[b]])
        bk = bx
        for p in range(2):
            mm(PS[:, bk, p * 256:(p + 1) * 256], gq[:, 2, p * 128:(p + 1) * 128], vb_t[:, p * 256:(p + 1) * 256],
               True, True, [bgq, bvb_t], [bPS[bk]])
        yield
        cp(qkT[:, :, :], PT[:, b, 0:512].rearrange("p (a b) -> p a b", a=4), [bPTs[b]], [bqkT], eng="act")
        for r in range(2):
            pr = slice(r * 64, (r + 1) * 64)
            cp(qdz[pr, :, r, :], PT[pr, b, 0:256].rearrange("p (a b) -> p a b", a=2), [bPTs[b]], [bqdz])
        yield
        for p in range(2):
            mm(PS[:, ba, p * 256:(p + 1) * 256], qkT[:, 2 + p, :],
               qdz[:, p, :, :].rearrange("p r c -> p (r c)"), True, True, [bqkT, bqdz], [bPS[ba]])
        yield
        tt(AT[:, :, :], PS[:, ba, :].rearrange("p (a b) -> p a b", a=4),
           tri[:, TRI_MASK[d], :].unsqueeze(1).to_broadcast([128, 4, 128]), ALU.mult,
           [bPS[ba], btri], [bAT])
        yield
        bo = bc
        for p in range(2):
            mm(PS[:, bo, p * 256:(p + 1) * 256], qkT[:, p, :], Sbf[:, p, :, :].rearrange("p r c -> p (r c)"),
               True, False, [bqkT, bSbf], [bPS[bo]])
            for r in range(2):
                h = 2 * p + r
                mm(PS[:, bo, h * 128:(h + 1) * 128], AT[:, h, :], vb_t[:, h * 128:(h + 1) * 128], False, r == 1,
                   [bAT, bvb_t], [bPS[bo]])
        yield
        for p in range(2):
            for hf in range(2):
                rows = slice(hf * 64, (hf + 1) * 64)
                stt(St[rows, p, :], St[rows, p, :], dcy[rows, p:p + 1], kvblk(bk, p, hf), ALU.mult, ALU.add,
                    [bSt, bdcy, bPS[bk]], [bSt])
        yield

    def a_pre(seg, n, par):
        xt2, bxt2 = ((xt, bxt), (T[4], bT[4]))[par]
        rows = slice(n * 128, (n + 1) * 128)
        dma(xt2[:, :], x_all[rows, :], [], [bxt2])
        stt(T[1][:, :], xt2[:, :], 1.0, xt2[:, :], ALU.mult, ALU.mult, [bxt2], [bT[1], bstA[par]],
            accum=stA[par][:, 0:1])
        yield
        rstd_from_ss(stA[par][:, 0:1], stA[par][:, 1:2], float(D), bstA[par], bstA[par])
        yield

    def a_headA(seg, n, par):
        xt2, bxt2 = ((xt, bxt), (T[4], bT[4]))[par]
        tb2, btb2 = ((tab, btab), (pt, bpt))[par]
        rows = slice(n * 128, (n + 1) * 128)
        dma(tb2[:, 0:128], tab_all[rows, :], [], [btb2])
        act(xs[:, :], xt2[:, :], AF.Copy, [bxt2, bstA[par]], [bxs], scale=stA[par][:, 1:2])
        yield
        transpose8(xs, bxs, xT, bxT)
        yield
        ba, bv, bl = 0, 1, 2
        proj_tok(ba, 0, 512, 768)
        proj_tok(ba, 256, 1536, 1792)
        yield
        proj_tok(bv, 0, 1792, 2304)
        yield
        proj_feat(bl, 0, 2816, 2832)
        proj_feat(bl, 128, 2832, 2848)
        yield

    def a_headB(seg, n, par):
        n0 = sum(a for a, _ in segs[:seg])
        j = n - n0
        tb2, btb2 = ((tab, btab), (pt, bpt))[par]
        ba, bv, bl = 0, 1, 2
        cp(T[0][:, 0:128], PS[:, ba, 0:128], [bPS[ba]], [bT[0]], eng="act")
        cp(VO[:, j, :, 0:64], PS[:, ba, 128:256].rearrange("p (g d) -> p g d", g=2), [bPS[ba]], [bVO],
           eng="act")
        cp(lkA[par][:, :], PS[:, ba, 256:512], [bPS[ba]], [blkA[par]])
        yield
        cp(vbA[par][:, :], PS[:, bv, :], [bPS[bv]], [bvbA[par]], eng="act")
        cp(lrTA[par][:, :, :], PS[0:16, bl, 0:256].rearrange("p (a b) -> p a b", a=2), [bPS[bl]], [blrTA[par]])
        yield
        yield from norm_rope_g(T[0][:, 0:128], bT[0], 2, kn, bkn, kr[:, :].rearrange("p (h d) -> p h d", h=2),
                               bkr, T[0][:, 256:768], bT[0], tabs=(tb2, btb2))
        yield
        b = npt()
        tr(PT[:, b, 0:128], kr[:, :], [bkr], [bPTs[b]])
        cp(KT[:, j * 128:(j + 1) * 128], PT[:, b, 0:128], [bPTs[b]], [bKT])
        yield

    def a_tail(seg, n, par):
        E2 = (T[2], T[3])
        bE2 = (bT[2], bT[3])
        lk_ap, blk = lkA[par][:, :], blkA[par]
        mcol = [masks[:, 2 * n + d:2 * n + d + 1] for d in range(2)]
        bx = [3, 4]
        bc = 5
        for d in range(2):
            mm(PS[:, bx[d], 0:256], lrTA[par][0:16, d, :], Wup[0:16, d, :], True, False,
               [blrTA[par], bWup], [bPS[bx[d]]])
            mm(PS[:, bx[d], 0:256], ones_bf[0:1, 0:128], bgt[0:1, d, :], False, True, [bones_bf, bbg],
               [bPS[bx[d]]])
        yield
        for d in range(2):
            act(E2[d][:, 0:256], PS[:, bx[d], 0:256], AF.Exp, [bPS[bx[d]]], [bE2[d]], scale=-1.0)
        yield
        for d in range(2):
            act(E2[d][:, 0:256], E2[d][:, 0:256], AF.Ln, [bE2[d]], [bE2[d]], bias=1.0)
        yield
        for d in range(2):
            mm(PS[:, bc, d * 256:(d + 1) * 256], tri[:, TRI_REM[d], :], E2[d][:, 0:256], True, True,
               [btri, bE2[d]], [bPS[bc]])
            for p in range(2):
                mm(PS[:, bx[d], 256 + p:257 + p], E2[d][:, p * 128:(p + 1) * 128], ones_f[:, 0:1], True, True,
                   [bE2[d], bones_f], [bPS[bx[d]]])
        yield
        for d in range(2):
            act(E2[d][:, 512:768], PS[:, bc, d * 256:(d + 1) * 256], AF.Exp, [bPS[bc]], [bE2[d]], scale=-1.0 / 16)
            act(dcy2[d][:, 0:2], PS[:, bx[d], 256:258], AF.Exp, [bPS[bx[d]]], [bdcy2[d]], scale=-1.0 / 16)
        yield
        for d in range(2):
            stt(gq[:, d, :], lk_ap, mcol[d], E2[d][:, 512:768], ALU.mult, ALU.mult, [blk, bE2[d], bmasks], [bgq])
            ts(acf2[d][:, :], dcy2[d][:, :], -1.0, mcol[d], ALU.add, ALU.mult, [bdcy2[d], bmasks], [bacf2[d]])
            ts(acf2[d][:, :], acf2[d][:, :], 1.0, None, ALU.add, None, [bacf2[d]], [bacf2[d]])
        yield
        bk = bx
        for d in range(2):
            for p in range(2):
                mm(PS[:, bk[d], p * 256:(p + 1) * 256], gq[:, d, p * 128:(p + 1) * 128],
                   vbA[par][:, p * 256:(p + 1) * 256], True, True, [bgq, bvbA[par]], [bPS[bk[d]]])
        yield
        for d in range(2):
            for p in range(2):
                for hf in range(2):
                    r = slice(hf * 64, (hf + 1) * 64)
                    if d == 0:
                        stt(Sf[r, p, :], Sf[r, p, :], acf2[0][r, p:p + 1], kvblk(bk[0], p, hf), ALU.mult, ALU.add,
                            [bSf, bacf2[0], bPS[bk[0]]], [bSf])
                    else:
                        stt(Lb[r, p, :], kvblk(bk[1], p, hf), Pb[r, p:p + 1], Lb[r, p, :], ALU.mult, ALU.add,
                            [bLb, bPb, bPS[bk[1]]], [bLb])
            yield
        tt(Pb[:, :], Pb[:, :], acf2[1][:, :], ALU.mult, [bPb, bacf2[1]], [bPb])
        yield

    def interleave(*gens):
        gens = [g for g in gens if g is not None]
        while gens:
            for g in list(gens):
                try:
                    next(g)
                except StopIteration:
                    gens.remove(g)

    def phase_a(seg):
        n0 = sum(a for a, _ in segs[:seg])
        n1 = n0 + segs[seg][0]
        memset(Sf[:, :, :], 0.0, [bSf])
        memset(Lb[:, :, :], 0.0, [bLb])
        memset(Pb[:, :], 1.0, [bPb])
        tiles = list(range(n0, n1))
        N_ = len(tiles)

        def G(fn, k):
            return fn(seg, tiles[k], k % 2) if 0 <= k < N_ else None
        for k in range(-3, N_):
            interleave(G(a_headA, k + 2), G(a_headB, k + 1), G(a_tail, k), G(a_pre, k + 3))

    LQK = (T[1], T[4])
    bLQK = (bT[1], bT[4])

    def b1_head(t, par):
        rows = slice(t * 128, (t + 1) * 128)
        yield from front_g(x_own[rows, :])
        ba, bv, bl = 0, 1, 2
        proj_tok(ba, 0, 1280, 1792)
        yield
        proj_tok(bv, 0, 1792, 2304)
        proj_feat(bl, 128, 2832, 2848)
        yield
        cp(LQK[par][:, 0:512], PS[:, ba, :], [bPS[ba]], [bLQK[par]])
        cp(vbA[par][:, :], PS[:, bv, :], [bPS[bv]], [bvbA[par]], eng="act")
        cp(lrTA[par][:, 1, :], PS[0:16, bl, 128:256], [bPS[bl]], [blrTA[par]])
        yield

    def b1_tail(t, par):
        rows = slice(t * 128, (t + 1) * 128)
        yield from gla_full_g(1, Lb, bLb, T[2], bT[2], LQK[par][:, 256:512], bLQK[par], LQK[par][:, 0:256],
                              bLQK[par], vbA[par], bvbA[par], lrTA[par][0:16, 1, :], blrTA[par])
        hs_ = slice(par * 512, (par + 1) * 512)
        cp(T[0][:, hs_], PS[:, 4, :], [bPS[4]], [bT[0]])
        dma(obwd[rows, :], T[0][:, hs_], [bT[0]], [bufs.setdefault("obwd", Buf("obwd"))])
        yield

    def phase_b1(seg):
        t0 = sum(b for _, b in segs[:seg])
        tiles = list(range(t0 + segs[seg][1] - 1, t0 - 1, -1))
        interleave(b1_head(tiles[0], 0))
        for i, t in enumerate(tiles):
            h = b1_head(tiles[i + 1], (i + 1) % 2) if i + 1 < len(tiles) else None
            interleave(h, b1_tail(t, i % 2))

    def phase_b2(seg):
        t0 = sum(b for _, b in segs[:seg])
        nkb = segs[seg][0]
        bob = bufs.setdefault("obwd", Buf("obwd"))
        for t in range(t0, t0 + segs[seg][1]):
            rows = slice(t * 128, (t + 1) * 128)
            for _ in front_g(x_own[rows, :], load=(t == t0)):
                pass
            dma(tab[:, :], tab_own[rows, :], [], [btab])
            dma(pt[:, :], p_own[rows, :], [], [bpt])
            bq = nps()
            proj_tok(bq, 0, 0, 512)
            ba = nps()
            proj_tok(ba, 0, 1280, 1792)
            bv = nps()
            proj_tok(bv, 0, 1792, 2304)
            bg = nps()
            proj_tok(bg, 0, 2304, 2816)
            bl = nps()
            proj_feat(bl, 0, 2816, 2832)
            bag = nps()
            for c in range(4):
                proj_feat(bag, c * 128, 768 + c * 128, 768 + (c + 1) * 128)
            cp(T[0][:, 0:512], PS[:, bq, :], [bPS[bq]], [bT[0]], eng="act")
            cp(T[1][:, 0:512], PS[:, ba, :], [bPS[ba]], [bT[1]])
            cp(vb[:, :], PS[:, bv, :], [bPS[bv]], [bvb], eng="act")
            cp(lrT[:, 0, :], PS[0:16, bl, 0:128], [bPS[bl]], [blrT])
            act(T[4][:, 0:512], PS[:, bg, :], AF.Tanh, [bPS[bg]], [bT[4]], scale=0.5)
            ts(T[4][:, 0:512], T[4][:, 0:512], 0.5, 0.5, ALU.mult, ALU.add, [bT[4]], [bT[4]])
            tt(T[1][:, 512:1024], PS[:, bg, :], T[4][:, 0:512], ALU.mult, [bPS[bg], bT[4]], [bT[1]])
            act(T[4][:, 512:1024], PS[:, bag, :], AF.Tanh, [bPS[bag]], [bT[4]], scale=0.5)
            ts(T[4][:, 512:1024], T[4][:, 512:1024], 0.5, 0.5, ALU.mult, ALU.add, [bT[4]], [bT[4]])
            aTd = T[3][0:64, :].rearrange("p (h q) -> p h q", h=8)
            for r in range(2):
                prow = slice(r * 64, (r + 1) * 64)
                tt(aTd[:, r::2, :], PS[prow, bag, :].rearrange("p (c q) -> p c q", c=4),
                   T[4][prow, 512:1024].rearrange("p (c q) -> p c q", c=4), ALU.mult,
                   [bPS[bag], bT[4]], [bT[3]])
            def q_chain():
                yield from norm_rope_g(T[0][:, 0:512], bT[0], 8, qn, bqn,
                                       qr[:, :, :, :].rearrange("p j g d -> p g j d"), bqr,
                                       T[2][:, :], bT[2], gj=(2, 4))
                b = npt()
                for j in range(4):
                    tr(PT[:, b, j * 128:(j + 1) * 128], qr[:, j, :, :].rearrange("p g d -> p (g d)"), [bqr],
                       [bPTs[b]])
                yield
                for g in range(2):
                    prow = slice(g * 64, (g + 1) * 64)
                    cp(QTz[prow, g, :, :], PT[prow, b, 0:512].rearrange("p (j q) -> p j q", j=4),
                       [bPTs[b]], [bQTz])
                yield

            def gla_chain():
                dma(T[0][:, 512:1024], obwd[rows, :], [bob], [bT[0]])
                yield from gla_full_g(0, Sf, bSf, T[4], bT[4], T[1][:, 256:512], bT[1], T[1][:, 0:256], bT[1],
                                      vb, bvb, lrT[0:16, 0, :], blrT)
                osum = T[0][:, 512:1024]
                tt(osum, PS[:, 4, :], osum, ALU.add, [bPS[4], bT[0]], [bT[0]])
                yield
                tt(T[4][:, 0:512], osum, osum, ALU.mult, [bT[0]], [bT[4]])
                red(st4[:, 0:4], T[4][:, 0:512].rearrange("p (h d) -> p h d", h=4), [bT[4]], [bst4])
                yield
                rstd_from_ss(st4[:, 0:4], st5[:, 0:4], 128.0, bst4, bst5)
                yield
                o3 = osum.rearrange("p (h d) -> p h d", h=4)
                tt(o3, o3, st5[:, 0:4].unsqueeze(2).to_broadcast([128, 4, 128]), ALU.mult, [bT[0], bst5], [bT[0]])
                tt(o3, o3, gn[:, :].unsqueeze(1).to_broadcast([128, 4, 128]), ALU.mult, [bT[0], bgn], [bT[0]])
                tt(mg[:, :], osum, T[1][:, 512:1024], ALU.mult, [bT[0], bT[1]], [bmg])
                yield
                b = npt()
                for c in range(4):
                    tr(PT[:, b, c * 128:(c + 1) * 128], mg[:, c * 128:(c + 1) * 128], [bmg], [bPTs[b]])
                cp(mixT[:, 4:8, :], PT[:, b, 0:512].rearrange("p (a b) -> p a b", a=4), [bPTs[b]], [bmixT],
                   eng="act")
                yield

            interleave(q_chain(), gla_chain())
            dma(T[0][:, :], x_own[rows, :], [], [bT[0]])
            if t + 1 < t0 + segs[seg][1]:
                nrows = slice((t + 1) * 128, (t + 2) * 128)
                dma(xt[:, :], x_own[nrows, :], [], [bxt])
            def s_mm(kb):
                sb_ = kb % 2
                for g in range(2):
                    mm(PS[:, 2 * sb_ + g, :], KT[:, kb * 128:(kb + 1) * 128],
                       QTz[:, g, :, :].rearrange("p j q -> p (j q)"), True, True,
                       [bKT, bQTz], [bPS[2 * sb_ + g]])
            s_mm(0)
            if nkb > 1:
                s_mm(1)
            for kb in range(nkb):
                sb_ = kb % 2
                act(PTb[sb_][:, :, :], PS[:, 2 * sb_:2 * sb_ + 2, :], AF.Exp,
                    [bPS[2 * sb_], bPS[2 * sb_ + 1]], [bPTb[sb_]], scale=0.125)
                for g in range(2):
                    mm(PS[0:65, 4 + g, :], VO[:, kb, g, 0:65], PTb[sb_][:, g, :], kb == 0, kb == nkb - 1,
                       [bVO, bPTb[sb_]], [bPS[4 + g]])
                if kb + 2 < nkb:
                    s_mm(kb + 2)
            Oe = T[2]
            cp(Oe[0:65, :].rearrange("p (g n) -> p g n", g=2), PS[0:65, 4:6, :], [bPS[4], bPS[5]], [bT[2]],
               eng="act")
            act(Oe[64:65, :], Oe[64:65, :], AF.Ln, [bT[2]], [bT[2]])
            act(Oe[64:65, :], Oe[64:65, :], AF.Exp, [bT[2]], [bT[2]], scale=-1.0)
            for g in range(2):
                bb = nps()
                while bb >= 4:
                    bb = nps()
                mm(PS[0:64, bb, :], ones_f[64:65, 0:64], Oe[64:65, g * 512:(g + 1) * 512], True, True,
                   [bones_f, bT[2]], [bPS[bb]])
                tt(Oe[0:64, g * 512:(g + 1) * 512], Oe[0:64, g * 512:(g + 1) * 512], PS[0:64, bb, :], ALU.mult,
                   [bT[2], bPS[bb]], [bT[2]])
                On = Oe[0:64, g * 512:(g + 1) * 512].rearrange("p (j q) -> p j q", j=4)
                for r in range(2):
                    tt(mixT[r * 64:(r + 1) * 64, 2 * g:2 * g + 2, :], On[:, r::2, :],
                       aTd[:, 4 * g + r:4 * g + 4:2, :], ALU.mult, [bT[2], bT[3]], [bmixT])
            dma(T[3][:, :], fn_d[:, :], [], [bT[3]])
            bh = [nps(), nps()]
            for hh in range(2):
                for c in range(8):
                    mm(PS[:, bh[hh], :], mixT[:, c, :], Wout[:, c, hh * 512:(hh + 1) * 512], c == 0, c == 7,
                       [bmixT, bWout], [bPS[bh[hh]]])
            h2 = T[0]
            for hh in range(2):
                tt(h2[:, hh * 512:(hh + 1) * 512], PS[:, bh[hh], :], h2[:, hh * 512:(hh + 1) * 512], ALU.add,
                   [bPS[bh[hh]], bT[0]], [bT[0]])
            stt(xs[:, :], h2[:, :], 1.0, h2[:, :], ALU.mult, ALU.mult, [bT[0]], [bxs, bst1], accum=st1[:, 2:3])
            rstd_from_ss(st1[:, 2:3], st1[:, 3:4], float(D), bst1, bst1)
            ts(xs[:, :], h2[:, :], st1[:, 3:4], None, ALU.mult, None, [bT[0], bst1], [bxs])
            transpose8(xs, bxs, xT, bxT)
            cp(pb[:, :], pt[:, :], [bpt], [bpb])
            b = npt()
            for c in range(2):
                tr(PT[:, b, c * 128:(c + 1) * 128], pb[:, c * 128:(c + 1) * 128], [bpb], [bPTs[b]])
            cp(pT[:, :, :], PT[:, b, 0:256].rearrange("p (a b) -> p a b", a=2), [bPTs[b]], [bpT], eng="act")
            h3 = T[1]
            for hh in range(2):
                cs_ = slice(hh * 512, (hh + 1) * 512)
                bgp = nps()
                for c in range(8):
                    mm(PS[:, bgp, :], xT[:, c, :], Wpg[:, c, cs_], c == 0, c == 7, [bxT, bWpg], [bPS[bgp]])
                bpp = nps()
                for c in range(2):
                    mm(PS[:, bpp, :], pT[:, c, :], Wpp[:, c, cs_], c == 0, c == 1, [bpT, bWpp], [bPS[bpp]])
                act(T[4][:, cs_], PS[:, bgp, :], AF.Tanh, [bPS[bgp]], [bT[4]], scale=0.5)
                ts(T[4][:, cs_], T[4][:, cs_], 0.5, 0.5, ALU.mult, ALU.add, [bT[4]], [bT[4]])
                tt(T[4][:, cs_], T[4][:, cs_], PS[:, bpp, :], ALU.mult, [bT[4], bPS[bpp]], [bT[4]])
                tt(h3[:, cs_], T[4][:, cs_], h2[:, cs_], ALU.add, [bT[4], bT[0]], [bT[1]])
            stt(T[4][:, :], h3[:, :], 1.0, h3[:, :], ALU.mult, ALU.mult, [bT[1]], [bT[4], bst1], accum=st1[:, 4:5])
            rstd_from_ss(st1[:, 4:5], st1[:, 5:6], float(D), bst1, bst1)
            stt(T[4][:, :], h3[:, :], st1[:, 5:6], T[3][:, :], ALU.mult, ALU.mult, [bT[1], bst1, bT[3]], [bT[4]])
            dma(y_out[rows, :], T[4][:, :], [bT[4]], [Buf("yout")])

    for seg in range(len(segs)):
        phase_a(seg)
        phase_b1(seg)
        phase_b2(seg)

    S.emit(nc, st)
    st.close()
    return nc


def _rope_tab(pos):
    half = 32
    inv = (10000.0 ** (-np.arange(0, half, 2, dtype=np.float32) / half)).astype(np.float32)
    row = (pos // 64).astype(np.float32)
    col = (pos % 64).astype(np.float32)
    ar = row[:, None] * inv[None, :]
    ac = col[:, None] * inv[None, :]
    cr, sr, cc, sc = np.cos(ar), np.sin(ar), np.cos(ac), np.sin(ac)
    cos = np.concatenate([cr, cr, cc, cc], axis=1)
    sin = np.concatenate([-sr, sr, -sc, sc], axis=1)
    return np.concatenate([cos, sin], axis=1).astype(np.float32)


_NC_CACHE = {}


def kernel(x_prompt, x_sample, p_prompt, p_sample, mix_norm, w_in, q_norm, k_norm,
           w_gate_up_fwd, b_gate_fwd, w_gate_up_bwd, b_gate_bwd, gla_norm, w_out,
           ple_norm, w_ple_gate, w_ple_proj, final_norm):
    f32 = np.float32
    xp = np.asarray(x_prompt, f32)[0]
    xsm = np.asarray(x_sample, f32)
    pp = np.asarray(p_prompt, f32)[0, 0]
    psm = np.asarray(p_sample, f32)[0]
    tab_p = _rope_tab(np.arange(16384))
    tab_s = _rope_tab(np.arange(4096))
    rep = lambda v, n: np.ascontiguousarray(np.broadcast_to(np.asarray(v, f32).reshape(1, -1), (128, n)))
    tri = np.zeros((128, 4, 128), f32)
    s_, t_ = np.meshgrid(np.arange(128), np.arange(128), indexing="ij")
    tri[:, 0] = (s_ <= t_)
    tri[:, 1] = (s_ > t_)
    tri[:, 2] = (s_ >= t_)
    tri[:, 3] = (s_ < t_)
    common = {
        "w_in": np.ascontiguousarray(np.asarray(w_in, f32)[0]),
        "w_out": np.ascontiguousarray(np.asarray(w_out, f32)[0]),
        "w_pg": np.ascontiguousarray(np.asarray(w_ple_gate, f32)[0]),
        "w_pp": np.ascontiguousarray(np.asarray(w_ple_proj, f32)[0]),
        "wup_f": np.ascontiguousarray(np.asarray(w_gate_up_fwd, f32)[0]),
        "wup_b": np.ascontiguousarray(np.asarray(w_gate_up_bwd, f32)[0]),
        "bg_f": np.ascontiguousarray(np.asarray(b_gate_fwd, f32)[0].reshape(1, 256)),
        "bg_b": np.ascontiguousarray(np.asarray(b_gate_bwd, f32)[0].reshape(1, 256)),
        "mixn": np.ascontiguousarray(np.asarray(mix_norm, f32)[0].reshape(8, 128).T),
        "plen": np.ascontiguousarray(np.asarray(ple_norm, f32)[0].reshape(8, 128).T),
        "qn": rep(np.asarray(q_norm)[0], 64),
        "kn": rep(np.asarray(k_norm)[0], 64),
        "gn": rep(np.asarray(gla_norm)[0], 128),
        "fn": rep(np.asarray(final_norm), 1024),
        "ident": np.eye(128, dtype=f32).astype(ml_dtypes.bfloat16),
        "tri": np.ascontiguousarray(tri.reshape(128, 512)),
    }
    in_maps = []
    for c in range(NCORES):
        sq, hf = c // 2, c % 2
        own_p = slice(2048 * c, 2048 * (c + 1))
        own_s = slice(2048 * hf, 2048 * (hf + 1))
        m = np.zeros((NT_CTX, 2), f32)
        m[0:16 * c, 0] = 1.0
        m[16 * (c + 1):128, 1] = 1.0
        if hf == 1:
            m[128:144, 0] = 1.0
        else:
            m[144:160, 1] = 1.0
        d = dict(common)
        d["x_all"] = np.ascontiguousarray(np.concatenate([xp, xsm[sq]], axis=0))
        d["x_own"] = np.ascontiguousarray(np.concatenate([xp[own_p], xsm[sq][own_s]], axis=0))
        d["p_own"] = np.ascontiguousarray(np.concatenate([pp[own_p], psm[sq][own_s]], axis=0))
        d["tab_all"] = np.ascontiguousarray(np.concatenate([tab_p, tab_s], axis=0))
        d["tab_own"] = np.ascontiguousarray(np.concatenate([tab_p[own_p], tab_s[own_s]], axis=0))
        d["masks"] = np.ascontiguousarray(np.broadcast_to(m.reshape(1, -1), (128, NT_CTX * 2)))
        in_maps.append(d)
    if "nc" not in _NC_CACHE:
        _NC_CACHE["nc"] = build_program()
    res = run_bass_kernel_spmd(_NC_CACHE["nc"], in_maps, core_ids=list(range(NCORES)))
    ys = [np.asarray(r["y"], f32) for r in res.results]
    y_prompt = np.concatenate([y[0:2048] for y in ys], axis=0)[None]
    y_sample = np.stack([np.concatenate([ys[2 * s][2048:], ys[2 * s + 1][2048:]], axis=0) for s in range(4)])
    return (y_prompt.astype(f32), y_sample.astype(f32))
```

```python
import sys
import numpy as np
import ml_dtypes
from contextlib import ExitStack
import concourse.bass as bass
import concourse.mybir as mybir
from concourse.bass_utils import run_bass_kernel_spmd

F32 = mybir.dt.float32
BF16 = mybir.dt.bfloat16
ALU = mybir.AluOpType
AF = mybir.ActivationFunctionType
AX = mybir.AxisListType

NCORES = 8
D = 1024
TOK_OWN = 4096
NT_OWN = 32
NT_CTX = 160
EPS = 1e-6
ND = 8
MAX_OPS = None
DMA_SCRATCH = 512


class Buf:
    __slots__ = ("name", "lw", "rd", "excl")

    def __init__(self, name, excl=False):
        self.name = name
        self.excl = excl
        self.lw = None
        self.rd = []


class Op:
    __slots__ = ("eng", "fn", "deps", "signal", "sigval", "dma", "dma_idx", "line")


class Sched:
    ENGS = ("sp", "pe", "dve", "act", "pool")

    def __init__(self):
        self.ops = []
        self.ndma = 0

    def add(self, eng, fn, R=(), W=(), dma=False):
        idx = len(self.ops)
        xr = [b for b in R if b.excl]
        if xr:
            R = [b for b in R if not b.excl]
            W = list(W) + [b for b in xr if b not in W]
        raw, other = set(), set()
        for b in R:
            if b.lw is not None:
                raw.add(b.lw)
        for b in W:
            if b.lw is not None:
                other.add(b.lw)
            for r in b.rd:
                other.add(r)
        deps = set()
        for d in raw | other:
            p = self.ops[d]
            if p.dma or dma:
                deps.add(d)
            elif p.eng != eng:
                deps.add(d)
            elif eng != "pe":
                deps.add(d)
        op = Op()
        op.eng, op.fn, op.deps, op.signal, op.sigval, op.dma = eng, fn, deps, False, 0, dma
        op.dma_idx = -1
        op.line = sys._getframe(2).f_lineno
        if dma:
            op.dma_idx = self.ndma
            self.ndma += 1
        self.ops.append(op)
        for b in R:
            b.rd.append(idx)
        for b in W:
            b.lw = idx
            b.rd = []
        return idx

    def emit(self, nc, stack):
        if MAX_OPS is not None:
            self.ops = self.ops[:MAX_OPS]
            self.ndma = sum(1 for o in self.ops if o.dma)
        ops = self.ops
        for op in ops:
            best = {}
            keep = set()
            for d in op.deps:
                p = ops[d]
                if p.dma:
                    keep.add(d)
                else:
                    if p.eng not in best or best[p.eng] < d:
                        best[p.eng] = d
            keep |= set(best.values())
            op.deps = keep
            for d in keep:
                ops[d].signal = True
        cnt = {e: 0 for e in self.ENGS}
        for op in ops:
            if op.dma:
                op.signal = True
            elif op.signal:
                cnt[op.eng] += 1
                op.sigval = cnt[op.eng]
        sems = {e: stack.enter_context(nc.semaphore("s_" + e)) for e in self.ENGS}
        dsems = [stack.enter_context(nc.semaphore("d%d" % i)) for i in range(ND)]
        block = stack.enter_context(nc.Block())
        ndma = self.ndma

        def run(engname):
            def body(e):
                waited = {}

                def wait(sem, val):
                    k = id(sem)
                    if waited.get(k, 0) < val:
                        e.wait_ge(sem, val)
                        waited[k] = val

                for op in ops:
                    if op.eng != engname:
                        continue
                    for d in sorted(op.deps):
                        p = ops[d]
                        if p.dma:
                            wait(dsems[p.dma_idx % ND], 16 * (p.dma_idx // ND + 1))
                        else:
                            wait(sems[p.eng], p.sigval)
                    if op.dma:
                        j = op.dma_idx
                        if j >= ND:
                            wait(dsems[j % ND], 16 * (j // ND))
                        op.fn(e).then_inc(dsems[j % ND], 16)
                    else:
                        ins = op.fn(e)
                        if op.signal:
                            ins.then_inc(sems[engname], 1)
                if engname == "sp":
                    for i in range(ND):
                        n = (ndma - i + ND - 1) // ND if ndma > i else 0
                        if n > 0:
                            wait(dsems[i], 16 * n)
            return body

        block.sync(run("sp"))
        block.tensor(run("pe"))
        block.vector(run("dve"))
        block.scalar(run("act"))
        block.gpsimd(run("pool"))


def build_program(segs=((128, 16), (32, 16))):
    NT_CTX = sum(a for a, _ in segs)
    NT_OWN = sum(b for _, b in segs)
    TOK_OWN = NT_OWN * 128
    MAXC = max(a for a, _ in segs)
    nc = bass.Bass("TRN2", target_bir_lowering=False, dynamic_dma_scratch_size=DMA_SCRATCH)
    S = Sched()
    st = ExitStack()

    def din(name, shape, dt=F32):
        return nc.dram_tensor(name, list(shape), dt, kind="ExternalInput").ap()

    x_all = din("x_all", [NT_CTX * 128, D])
    x_own = din("x_own", [TOK_OWN, D])
    p_own = din("p_own", [TOK_OWN, 256])
    tab_all = din("tab_all", [NT_CTX * 128, 128])
    tab_own = din("tab_own", [TOK_OWN, 128])
    masks_d = din("masks", [128, NT_CTX * 2])
    w_in = din("w_in", [D, 2848])
    w_out = din("w_out", [D, D])
    w_pg = din("w_pg", [D, D])
    w_pp = din("w_pp", [256, D])
    wup_d = [din("wup_f", [16, 256]), din("wup_b", [16, 256])]
    bg_d = [din("bg_f", [1, 256]), din("bg_b", [1, 256])]
    mixn_d = din("mixn", [128, 8])
    plen_d = din("plen", [128, 8])
    qn_d = din("qn", [128, 64])
    kn_d = din("kn", [128, 64])
    gn_d = din("gn", [128, 128])
    fn_d = din("fn", [128, D])
    ident_d = din("ident", [128, 128], BF16)
    tri_d = din("tri", [128, 4 * 128])
    y_out = nc.dram_tensor("y", [TOK_OWN, D], F32, kind="ExternalOutput").ap()
    obwd = nc.dram_tensor("obwd", [TOK_OWN, 512], F32)

    bufs = {}

    def sb(name, shape, dt=F32):
        t = st.enter_context(nc.sbuf_tensor("sb_" + name, list(shape), dt))
        bufs[name] = Buf(name)
        return t, bufs[name]

    Win, bWin = sb("Win", [128, 8, 2848], BF16)
    Wout, bWout = sb("Wout", [128, 8, D], BF16)
    Wpg, bWpg = sb("Wpg", [128, 8, D], BF16)
    Wpp, bWpp = sb("Wpp", [128, 2, D], BF16)
    Wup, bWup = sb("Wup", [16, 2, 256], BF16)
    bgt, bbg = sb("bgt", [1, 2, 256], BF16)
    ident, bident = sb("ident", [128, 128], BF16)
    tri, btri = sb("tri", [128, 4, 128], F32)
    ones_bf, bones_bf = sb("ones_bf", [128, 128], BF16)
    ones_f, bones_f = sb("ones_f", [128, 64], F32)
    qn, bqn = sb("qn", [128, 64])
    kn, bkn = sb("kn", [128, 64])
    gn, bgn = sb("gn", [128, 128])
    masks, bmasks = sb("masks", [128, NT_CTX * 2])
    mixn, bmixn = sb("mixn", [128, 8])
    plen, bplen = sb("plen", [128, 8])
    KT, bKT = sb("KT", [128, MAXC * 128], BF16)
    VO, bVO = sb("VO", [128, MAXC, 2, 66], BF16)
    T = []
    bT = []
    for i in range(5):
        t, b = sb("T%d" % i, [128, 1024], F32)
        T.append(t)
        bT.append(b)
    xt, bxt = sb("xt", [128, D])
    xs, bxs = sb("xs", [128, D], BF16)
    xT, bxT = sb("xT", [128, 8, 128], BF16)
    tab, btab = sb("tab", [128, 128])
    pt, bpt = sb("pt", [128, 256])
    pb, bpb = sb("pb", [128, 256], BF16)
    pT, bpT = sb("pT", [128, 2, 128], BF16)
    kr, bkr = sb("kr", [128, 128], BF16)
    lkA, blkA, vbA, bvbA, lrTA, blrTA = [], [], [], [], [], []
    for i in range(2):
        t, b = sb("lkA%d" % i, [128, 256]); lkA.append(t); blkA.append(b)
        t, b = sb("vbA%d" % i, [128, 512], BF16); vbA.append(t); bvbA.append(b)
        t, b = sb("lrTA%d" % i, [16, 2, 128], BF16); lrTA.append(t); blrTA.append(b)
    dcy2, bdcy2, acf2, bacf2 = [], [], [], []
    for i in range(2):
        t, b = sb("dcyA%d" % i, [128, 2]); dcy2.append(t); bdcy2.append(b)
        t, b = sb("acfA%d" % i, [128, 2]); acf2.append(t); bacf2.append(b)
    vb, bvb = sb("vb", [128, 512], BF16)
    lrT, blrT = sb("lrT", [16, 2, 128], BF16)
    gq, bgq = sb("gq", [128, 3, 256], BF16)
    qkT, bqkT = sb("qkT", [128, 4, 128], BF16)
    AT, bAT = sb("AT", [128, 4, 128], BF16)
    Sbf, bSbf = sb("Sbf", [128, 2, 2, 128], BF16)
    qdz, bqdz = sb("qdz", [128, 2, 2, 128], BF16)
    Sf, bSf = sb("Sf", [128, 2, 128])
    Lb, bLb = sb("Lb", [128, 2, 128])
    Pb, bPb = sb("Pb", [128, 2])
    dcy, bdcy = sb("dcy", [128, 2])
    acf, bacf = sb("acf", [128, 2])
    st1, bst1 = sb("st1", [128, 8])
    st2, bst2 = sb("st2", [128, 8])
    st3, bst3 = sb("st3", [128, 8])
    st4, bst4 = sb("st4", [128, 8])
    stA, bstA = [], []
    for i in range(2):
        t, b = sb("stA%d" % i, [128, 2]); stA.append(t); bstA.append(b)
    st5, bst5 = sb("st5", [128, 8])
    mg, bmg = sb("mg", [128, 512], BF16)
    mixT, bmixT = sb("mixT", [128, 8, 128], BF16)
    qr, bqr = sb("qr", [128, 4, 2, 64], BF16)
    QTz, bQTz = sb("QTz", [128, 2, 4, 128], BF16)
    PTb = []
    bPTb = []
    for i in range(2):
        t, b = sb("PTb%d" % i, [128, 2, 512], BF16)
        PTb.append(t)
        bPTb.append(b)

    PS = st.enter_context(nc.psum_tensor("PS", [128, 6, 512], F32))
    PT = st.enter_context(nc.psum_tensor("PT", [128, 2, 1024], BF16))
    bPS = [Buf("PS%d" % i, True) for i in range(6)]
    bPTs = [Buf("PT%d" % i, True) for i in range(2)]
    rr = {"ps": 0, "pt": 0, "dq": 0}

    def nps():
        i = rr["ps"]
        rr["ps"] = (i + 1) % 6
        return i

    def npt():
        i = rr["pt"]
        rr["pt"] = (i + 1) % 2
        return i

    def dma(out, in_, R, W):
        S.add("sp", lambda e, o=out, i=in_: e.dma_start(out=o, in_=i), R, W, dma=True)

    def mm(out, lhsT, rhs, start, stop, R, W):
        S.add("pe", lambda e, o=out, l=lhsT, r=rhs, s0=start, s1=stop:
              e.matmul(o, lhsT=l, rhs=r, start=s0, stop=s1), R, W)

    def tr(out, in_, R, W):
        S.add("pe", lambda e, o=out, i=in_: e.transpose(o, i, ident[:, :]), list(R) + [bident], W)

    def act(out, in_, func, R, W, scale=1.0, bias=0.0):
        S.add("act", lambda e, o=out, i=in_, f=func, s=scale, b=bias:
              e.activation(out=o, in_=i, func=f, bias=b, scale=s), R, W)

    def tt(out, in0, in1, op, R, W, eng="dve"):
        S.add(eng, lambda e, o=out, a=in0, b=in1, p=op: e.tensor_tensor(out=o, in0=a, in1=b, op=p), R, W)

    def ts(out, in0, s1, s2, op0, op1, R, W, eng="dve"):
        if s2 is None:
            S.add(eng, lambda e, o=out, a=in0, x=s1, p0=op0:
                  e.tensor_scalar(out=o, in0=a, scalar1=x, scalar2=None, op0=p0), R, W)
        else:
            S.add(eng, lambda e, o=out, a=in0, x=s1, y=s2, p0=op0, p1=op1:
                  e.tensor_scalar(out=o, in0=a, scalar1=x, scalar2=y, op0=p0, op1=p1), R, W)

    def stt(out, in0, scalar, in1, op0, op1, R, W, accum=None):
        if accum is None:
            S.add("dve", lambda e, o=out, a=in0, s=scalar, b=in1, p0=op0, p1=op1:
                  e.scalar_tensor_tensor(out=o, in0=a, scalar=s, in1=b, op0=p0, op1=p1), R, W)
        else:
            S.add("dve", lambda e, o=out, a=in0, s=scalar, b=in1, p0=op0, p1=op1, ac=accum:
                  e.scalar_tensor_tensor(out=o, in0=a, scalar=s, in1=b, op0=p0, op1=p1, accum_out=ac), R, W)

    def cp(out, in_, R, W, eng="dve"):
        if eng == "act":
            S.add("act", lambda e, o=out, i=in_: e.copy(out=o, in_=i), R, W)
        else:
            S.add(eng, lambda e, o=out, i=in_: e.tensor_copy(out=o, in_=i), R, W)

    def red(out, in_, R, W):
        S.add("dve", lambda e, o=out, i=in_: e.tensor_reduce(out=o, in_=i, axis=AX.X, op=ALU.add), R, W)

    def recip(out, in_, R, W):
        S.add("dve", lambda e, o=out, i=in_: e.reciprocal(out=o, in_=i), R, W)

    def memset(ap, val, W, eng="dve"):
        S.add(eng, lambda e, a=ap, v=val: e.memset(a, v), (), W)

    def rstd_from_ss(ss_ap, out_ap, n, bss, bout):
        act(out_ap, ss_ap, AF.Ln, [bss], [bout], scale=1.0 / n, bias=EPS)
        act(out_ap, out_ap, AF.Exp, [bout], [bout], scale=-0.5)

    dma(ident[:, :], ident_d[:, :], [], [bident])
    dma(tri[:, :, :], tri_d.rearrange("p (a b) -> p a b", a=4), [], [btri])
    dma(qn[:, :], qn_d[:, :], [], [bqn])
    dma(kn[:, :], kn_d[:, :], [], [bkn])
    dma(gn[:, :], gn_d[:, :], [], [bgn])
    dma(masks[:, :], masks_d[:, :], [], [bmasks])
    dma(mixn[:, :], mixn_d[:, :], [], [bmixn])
    dma(plen[:, :], plen_d[:, :], [], [bplen])
    memset(ones_bf[:, :], 1.0, [bones_bf])
    memset(ones_f[:, :], 1.0, [bones_f])
    memset(VO[:, :, :, 64:66], 1.0, [bVO])
    memset(QTz[:, :, :, :], 0.0, [bQTz])
    memset(Sbf[:, :, :, :], 0.0, [bSbf])
    memset(qdz[:, :, :, :], 0.0, [bqdz])

    k = 0
    for kc in range(8):
        for (c0, c1) in ((0, 1024), (1024, 2048), (2048, 2848)):
            n = c1 - c0
            ti = k % 2
            k += 1
            dma(T[ti][:, 0:n], w_in[kc * 128:(kc + 1) * 128, c0:c1], [], [bT[ti]])
            ts(Win[:, kc, c0:c1], T[ti][:, 0:n], mixn[:, kc:kc + 1], None, ALU.mult, None,
               [bT[ti], bmixn], [bWin])
    for kc in range(8):
        ti = k % 2
        k += 1
        dma(T[ti][:, :], w_pg[kc * 128:(kc + 1) * 128, :], [], [bT[ti]])
        ts(Wpg[:, kc, :], T[ti][:, :], plen[:, kc:kc + 1], None, ALU.mult, None, [bT[ti], bplen], [bWpg])
    for kc in range(8):
        ti = k % 2
        k += 1
        dma(T[ti][:, :], w_out[kc * 128:(kc + 1) * 128, :], [], [bT[ti]])
        cp(Wout[:, kc, :], T[ti][:, :], [bT[ti]], [bWout])
    for kc in range(2):
        ti = k % 2
        k += 1
        dma(T[ti][:, :], w_pp[kc * 128:(kc + 1) * 128, :], [], [bT[ti]])
        cp(Wpp[:, kc, :], T[ti][:, :], [bT[ti]], [bWpp])
    for d in range(2):
        dma(T[2][0:16, d * 256:(d + 1) * 256], wup_d[d][:, :], [], [bT[2]])
        dma(T[3][0:1, d * 256:(d + 1) * 256], bg_d[d][:, :], [], [bT[3]])
    cp(Wup[:, :, :], T[2][0:16, 0:512].rearrange("p (a b) -> p a b", a=2), [bT[2]], [bWup])
    cp(bgt[:, :, :], T[3][0:1, 0:512].rearrange("p (a b) -> p a b", a=2), [bT[3]], [bbg])

    def front_g(x_rows_ap, load=True):
        if load:
            dma(xt[:, :], x_rows_ap, [], [bxt])
        stt(xs[:, :], xt[:, :], 1.0, xt[:, :], ALU.mult, ALU.mult, [bxt], [bxs, bst1], accum=st1[:, 0:1])
        yield
        rstd_from_ss(st1[:, 0:1], st1[:, 1:2], float(D), bst1, bst1)
        act(xs[:, :], xt[:, :], AF.Copy, [bxt, bst1], [bxs], scale=st1[:, 1:2])
        yield
        transpose8(xs, bxs, xT, bxT)
        yield

    def front(x_rows_ap):
        for _ in front_g(x_rows_ap):
            pass

    def transpose8(src, bsrc, dst, bdst):
        b = npt()
        for c in range(8):
            tr(PT[:, b, c * 128:(c + 1) * 128], src[:, c * 128:(c + 1) * 128], [bsrc], [bPTs[b]])
        cp(dst[:, :, :], PT[:, b, :].rearrange("p (a b) -> p a b", a=8), [bPTs[b]], [bdst], eng="act")

    def proj_tok(bank, off, c0, c1):
        n = c1 - c0
        for kc in range(8):
            mm(PS[:, bank, off:off + n], xT[:, kc, :], Win[:, kc, c0:c1], kc == 0, kc == 7,
               [bxT, bWin], [bPS[bank]])

    def proj_feat(bank, off, c0, c1, p0=0):
        m = c1 - c0
        for kc in range(8):
            mm(PS[p0:p0 + m, bank, off:off + 128], Win[:, kc, c0:c1], xT[:, kc, :], kc == 0, kc == 7,
               [bxT, bWin], [bPS[bank]])

    def norm_rope_g(src, bsrc, nh, wn, bwn, dst, bdst, tmp, btmp, gj=None, tabs=None):
        n = nh * 64
        tab_t, btab_t = tabs if tabs is not None else (tab, btab)
        sq = tmp[:, 0:n]
        yv = tmp[:, n:2 * n]
        tt(sq, src, src, ALU.mult, [bsrc], [btmp])
        red(st2[:, 0:nh], sq.rearrange("p (h d) -> p h d", h=nh), [btmp], [bst2])
        yield
        rstd_from_ss(st2[:, 0:nh], st3[:, 0:nh], 64.0, bst2, bst3)
        yield
        y3 = yv.rearrange("p (h d) -> p h d", h=nh)
        tt(y3, src.rearrange("p (h d) -> p h d", h=nh),
           st3[:, 0:nh].unsqueeze(2).to_broadcast([128, nh, 64]), ALU.mult, [bsrc, bst3], [btmp])
        tt(y3, y3, wn[:, :].unsqueeze(1).to_broadcast([128, nh, 64]), ALU.mult, [btmp, bwn], [btmp])
        yield
        y5 = yv.rearrange("p (h a b c) -> p h a b c", h=nh, a=2, b=2)
        s5 = sq.rearrange("p (h a b c) -> p h a b c", h=nh, a=2, b=2)
        cos4 = tab_t[:, 0:64].rearrange("p (a b c) -> p a b c", a=2, b=2)
        sin4 = tab_t[:, 64:128].rearrange("p (a b c) -> p a b c", a=2, b=2)
        for hf in range(2):
            tt(s5[:, :, :, hf, :], y5[:, :, :, 1 - hf, :],
               sin4[:, :, hf, :].unsqueeze(1).to_broadcast([128, nh, 2, 16]), ALU.mult,
               [btmp, btab_t], [btmp])
        tt(y3, y3, tab_t[:, 0:64].unsqueeze(1).to_broadcast([128, nh, 64]), ALU.mult, [btmp, btab_t], [btmp])
        if gj is None:
            tt(dst, y3, sq.rearrange("p (h d) -> p h d", h=nh), ALU.add, [btmp], [bdst])
        else:
            g_, j_ = gj
            tt(dst, yv.rearrange("p (g j d) -> p g j d", g=g_, j=j_),
               sq.rearrange("p (g j d) -> p g j d", g=g_, j=j_), ALU.add, [btmp], [bdst])

    def norm_rope(*a, **k):
        for _ in norm_rope_g(*a, **k):
            pass

    TRI_CS = (0, 2)
    TRI_REM = (1, 3)
    TRI_MASK = (0, 1)

    def kvblk(bk, p, hf):
        return PS[hf * 64:(hf + 1) * 64, bk, p * 256 + hf * 128:p * 256 + hf * 128 + 128]

    def gla_full_g(d, St, bSt, E, bE, lk_ap, blk, lq_ap, blq, vb_t, bvb_t, lr_ap, blr):
        bx, bc, ba = 3, 4, 5
        for r in range(2):
            pr = slice(r * 64, (r + 1) * 64)
            cp(Sbf[pr, :, r, :], St[pr, :, :], [bSt], [bSbf])
        mm(PS[:, bx, 0:256], lr_ap, Wup[0:16, d, :], True, False, [blr, bWup], [bPS[bx]])
        mm(PS[:, bx, 0:256], ones_bf[0:1, 0:128], bgt[0:1, d, :], False, True, [bones_bf, bbg], [bPS[bx]])
        yield
        act(E[:, 0:256], PS[:, bx, 0:256], AF.Exp, [bPS[bx]], [bE], scale=-1.0)
        act(E[:, 0:256], E[:, 0:256], AF.Ln, [bE], [bE], bias=1.0)
        yield
        mm(PS[:, bc, 0:256], tri[:, TRI_CS[d], :], E[:, 0:256], True, True, [btri, bE], [bPS[bc]])
        mm(PS[:, bc, 256:512], tri[:, TRI_REM[d], :], E[:, 0:256], True, True, [btri, bE], [bPS[bc]])
        for p in range(2):
            mm(PS[:, bx, 256 + p:257 + p], E[:, p * 128:(p + 1) * 128], ones_f[:, 0:1], True, True,
               [bE, bones_f], [bPS[bx]])
        yield
        act(E[:, 256:512], PS[:, bc, 0:256], AF.Exp, [bPS[bc]], [bE], scale=-1.0 / 16)
        act(E[:, 768:1024], PS[:, bc, 0:256], AF.Exp, [bPS[bc]], [bE], scale=1.0 / 16)
        act(E[:, 512:768], PS[:, bc, 256:512], AF.Exp, [bPS[bc]], [bE], scale=-1.0 / 16)
        act(dcy[:, 0:2], PS[:, bx, 256:258], AF.Exp, [bPS[bx]], [bdcy], scale=-1.0 / 16)
        yield
        stt(gq[:, 0, :], lq_ap, 0.125, E[:, 256:512], ALU.mult, ALU.mult, [blq, bE], [bgq])
        tt(gq[:, 1, :], lk_ap, E[:, 768:1024], ALU.mult, [blk, bE], [bgq])
        tt(gq[:, 2, :], lk_ap, E[:, 512:768], ALU.mult, [blk, bE], [bgq])
        yield
        b = npt()
        for i in range(2):
            for p in range(2):
                tr(PT[:, b, (i * 2 + p) * 128:(i * 2 + p + 1) * 128], gq[:, i, p * 128:(p + 1) * 128],
                   [bgq], [bPTs[b]])
        bk = bx
        for p in range(2):
            mm(PS[:, bk, p * 256:(p + 1) * 256], gq[:, 2, p * 128:(p + 1) * 128], vb_t[:, p * 256:(p + 1) * 256],
               True, True, [bgq, bvb_t], [bPS[bk]])
        yield
        cp(qkT[:, :, :], PT[:, b, 0:512].rearrange("p (a b) -> p a b", a=4), [bPTs[b]], [bqkT], eng="act")
        for r in range(2):
            pr = slice(r * 64, (r + 1) * 64)
            cp(qdz[pr, :, r, :], PT[pr, b, 0:256].rearrange("p (a b) -> p a b", a=2), [bPTs[b]], [bqdz])
        yield
        for p in range(2):
            mm(PS[:, ba, p * 256:(p + 1) * 256], qkT[:, 2 + p, :],
               qdz[:, p, :, :].rearrange("p r c -> p (r c)"), True, True, [bqkT, bqdz], [bPS[ba]])
        yield
        tt(AT[:, :, :], PS[:, ba, :].rearrange("p (a b) -> p a b", a=4),
           tri[:, TRI_MASK[d], :].unsqueeze(1).to_broadcast([128, 4, 128]), ALU.mult,
           [bPS[ba], btri], [bAT])
        yield
        bo = bc
        for p in range(2):
            mm(PS[:, bo, p * 256:(p + 1) * 256], qkT[:, p, :], Sbf[:, p, :, :].rearrange("p r c -> p (r c)"),
               True, False, [bqkT, bSbf], [bPS[bo]])
            for r in range(2):
                h = 2 * p + r
                mm(PS[:, bo, h * 128:(h + 1) * 128], AT[:, h, :], vb_t[:, h * 128:(h + 1) * 128], False, r == 1,
                   [bAT, bvb_t], [bPS[bo]])
        yield
        for p in range(2):
            for hf in range(2):
                rows = slice(hf * 64, (hf + 1) * 64)
                stt(St[rows, p, :], St[rows, p, :], dcy[rows, p:p + 1], kvblk(bk, p, hf), ALU.mult, ALU.add,
                    [bSt, bdcy, bPS[bk]], [bSt])
        yield

    def a_pre(seg, n, par):
        xt2, bxt2 = ((xt, bxt), (T[4], bT[4]))[par]
        rows = slice(n * 128, (n + 1) * 128)
        dma(xt2[:, :], x_all[rows, :], [], [bxt2])
        stt(T[1][:, :], xt2[:, :], 1.0, xt2[:, :], ALU.mult, ALU.mult, [bxt2], [bT[1], bstA[par]],
            accum=stA[par][:, 0:1])
        yield
        rstd_from_ss(stA[par][:, 0:1], stA[par][:, 1:2], float(D), bstA[par], bstA[par])
        yield

    def a_headA(seg, n, par):
        xt2, bxt2 = ((xt, bxt), (T[4], bT[4]))[par]
        tb2, btb2 = ((tab, btab), (pt, bpt))[par]
        rows = slice(n * 128, (n + 1) * 128)
        dma(tb2[:, 0:128], tab_all[rows, :], [], [btb2])
        act(xs[:, :], xt2[:, :], AF.Copy, [bxt2, bstA[par]], [bxs], scale=stA[par][:, 1:2])
        yield
        transpose8(xs, bxs, xT, bxT)
        yield
        ba, bv, bl = 0, 1, 2
        proj_tok(ba, 0, 512, 768)
        proj_tok(ba, 256, 1536, 1792)
        yield
        proj_tok(bv, 0, 1792, 2304)
        yield
        proj_feat(bl, 0, 2816, 2832)
        proj_feat(bl, 128, 2832, 2848)
        yield

    def a_headB(seg, n, par):
        n0 = sum(a for a, _ in segs[:seg])
        j = n - n0
        tb2, btb2 = ((tab, btab), (pt, bpt))[par]
        ba, bv, bl = 0, 1, 2
        cp(T[0][:, 0:128], PS[:, ba, 0:128], [bPS[ba]], [bT[0]], eng="act")
        cp(VO[:, j, :, 0:64], PS[:, ba, 128:256].rearrange("p (g d) -> p g d", g=2), [bPS[ba]], [bVO],
           eng="act")
        cp(lkA[par][:, :], PS[:, ba, 256:512], [bPS[ba]], [blkA[par]])
        yield
        cp(vbA[par][:, :], PS[:, bv, :], [bPS[bv]], [bvbA[par]], eng="act")
        cp(lrTA[par][:, :, :], PS[0:16, bl, 0:256].rearrange("p (a b) -> p a b", a=2), [bPS[bl]], [blrTA[par]])
        yield
        yield from norm_rope_g(T[0][:, 0:128], bT[0], 2, kn, bkn, kr[:, :].rearrange("p (h d) -> p h d", h=2),
                               bkr, T[0][:, 256:768], bT[0], tabs=(tb2, btb2))
        yield
        b = npt()
        tr(PT[:, b, 0:128], kr[:, :], [bkr], [bPTs[b]])
        cp(KT[:, j * 128:(j + 1) * 128], PT[:, b, 0:128], [bPTs[b]], [bKT])
        yield

    def a_tail(seg, n, par):
        E2 = (T[2], T[3])
        bE2 = (bT[2], bT[3])
        lk_ap, blk = lkA[par][:, :], blkA[par]
        mcol = [masks[:, 2 * n + d:2 * n + d + 1] for d in range(2)]
        bx = [3, 4]
        bc = 5
        for d in range(2):
            mm(PS[:, bx[d], 0:256], lrTA[par][0:16, d, :], Wup[0:16, d, :], True, False,
               [blrTA[par], bWup], [bPS[bx[d]]])
            mm(PS[:, bx[d], 0:256], ones_bf[0:1, 0:128], bgt[0:1, d, :], False, True, [bones_bf, bbg],
               [bPS[bx[d]]])
        yield
        for d in range(2):
            act(E2[d][:, 0:256], PS[:, bx[d], 0:256], AF.Exp, [bPS[bx[d]]], [bE2[d]], scale=-1.0)
        yield
        for d in range(2):
            act(E2[d][:, 0:256], E2[d][:, 0:256], AF.Ln, [bE2[d]], [bE2[d]], bias=1.0)
        yield
        for d in range(2):
            mm(PS[:, bc, d * 256:(d + 1) * 256], tri[:, TRI_REM[d], :], E2[d][:, 0:256], True, True,
               [btri, bE2[d]], [bPS[bc]])
            for p in range(2):
                mm(PS[:, bx[d], 256 + p:257 + p], E2[d][:, p * 128:(p + 1) * 128], ones_f[:, 0:1], True, True,
                   [bE2[d], bones_f], [bPS[bx[d]]])
        yield
        for d in range(2):
            act(E2[d][:, 512:768], PS[:, bc, d * 256:(d + 1) * 256], AF.Exp, [bPS[bc]], [bE2[d]], scale=-1.0 / 16)
            act(dcy2[d][:, 0:2], PS[:, bx[d], 256:258], AF.Exp, [bPS[bx[d]]], [bdcy2[d]], scale=-1.0 / 16)
        yield
        for d in range(2):
            stt(gq[:, d, :], lk_ap, mcol[d], E2[d][:, 512:768], ALU.mult, ALU.mult, [blk, bE2[d], bmasks], [bgq])
            ts(acf2[d][:, :], dcy2[d][:, :], -1.0, mcol[d], ALU.add, ALU.mult, [bdcy2[d], bmasks], [bacf2[d]])
            ts(acf2[d][:, :], acf2[d][:, :], 1.0, None, ALU.add, None, [bacf2[d]], [bacf2[d]])
        yield
        bk = bx
        for d in range(2):
            for p in range(2):
                mm(PS[:, bk[d], p * 256:(p + 1) * 256], gq[:, d, p * 128:(p + 1) * 128],
                   vbA[par][:, p * 256:(p + 1) * 256], True, True, [bgq, bvbA[par]], [bPS[bk[d]]])
        yield
        for d in range(2):
            for p in range(2):
                for hf in range(2):
                    r = slice(hf * 64, (hf + 1) * 64)
                    if d == 0:
                        stt(Sf[r, p, :], Sf[r, p, :], acf2[0][r, p:p + 1], kvblk(bk[0], p, hf), ALU.mult, ALU.add,
                            [bSf, bacf2[0], bPS[bk[0]]], [bSf])
                    else:
                        stt(Lb[r, p, :], kvblk(bk[1], p, hf), Pb[r, p:p + 1], Lb[r, p, :], ALU.mult, ALU.add,
                            [bLb, bPb, bPS[bk[1]]], [bLb])
            yield
        tt(Pb[:, :], Pb[:, :], acf2[1][:, :], ALU.mult, [bPb, bacf2[1]], [bPb])
        yield

    def interleave(*gens):
        gens = [g for g in gens if g is not None]
        while gens:
            for g in list(gens):
                try:
                    next(g)
                except StopIteration:
                    gens.remove(g)

    def phase_a(seg):
        n0 = sum(a for a, _ in segs[:seg])
        n1 = n0 + segs[seg][0]
        memset(Sf[:, :, :], 0.0, [bSf])
        memset(Lb[:, :, :], 0.0, [bLb])
        memset(Pb[:, :], 1.0, [bPb])
        tiles = list(range(n0, n1))
        N_ = len(tiles)

        def G(fn, k):
            return fn(seg, tiles[k], k % 2) if 0 <= k < N_ else None
        for k in range(-3, N_):
            interleave(G(a_headA, k + 2), G(a_headB, k + 1), G(a_tail, k), G(a_pre, k + 3))

    LQK = (T[1], T[4])
    bLQK = (bT[1], bT[4])

    def b1_head(t, par):
        rows = slice(t * 128, (t + 1) * 128)
        yield from front_g(x_own[rows, :])
        ba, bv, bl = 0, 1, 2
        proj_tok(ba, 0, 1280, 1792)
        yield
        proj_tok(bv, 0, 1792, 2304)
        proj_feat(bl, 128, 2832, 2848)
        yield
        cp(LQK[par][:, 0:512], PS[:, ba, :], [bPS[ba]], [bLQK[par]])
        cp(vbA[par][:, :], PS[:, bv, :], [bPS[bv]], [bvbA[par]], eng="act")
        cp(lrTA[par][:, 1, :], PS[0:16, bl, 128:256], [bPS[bl]], [blrTA[par]])
        yield

    def b1_tail(t, par):
        rows = slice(t * 128, (t + 1) * 128)
        yield from gla_full_g(1, Lb, bLb, T[2], bT[2], LQK[par][:, 256:512], bLQK[par], LQK[par][:, 0:256],
                              bLQK[par], vbA[par], bvbA[par], lrTA[par][0:16, 1, :], blrTA[par])
        hs_ = slice(par * 512, (par + 1) * 512)
        cp(T[0][:, hs_], PS[:, 4, :], [bPS[4]], [bT[0]])
        dma(obwd[rows, :], T[0][:, hs_], [bT[0]], [bufs.setdefault("obwd", Buf("obwd"))])
        yield

    def phase_b1(seg):
        t0 = sum(b for _, b in segs[:seg])
        tiles = list(range(t0 + segs[seg][1] - 1, t0 - 1, -1))
        interleave(b1_head(tiles[0], 0))
        for i, t in enumerate(tiles):
            h = b1_head(tiles[i + 1], (i + 1) % 2) if i + 1 < len(tiles) else None
            interleave(h, b1_tail(t, i % 2))

    def phase_b2(seg):
        t0 = sum(b for _, b in segs[:seg])
        nkb = segs[seg][0]
        bob = bufs.setdefault("obwd", Buf("obwd"))
        for t in range(t0, t0 + segs[seg][1]):
            rows = slice(t * 128, (t + 1) * 128)
            for _ in front_g(x_own[rows, :], load=(t == t0)):
                pass
            dma(tab[:, :], tab_own[rows, :], [], [btab])
            dma(pt[:, :], p_own[rows, :], [], [bpt])
            bq = nps()
            proj_tok(bq, 0, 0, 512)
            ba = nps()
            proj_tok(ba, 0, 1280, 1792)
            bv = nps()
            proj_tok(bv, 0, 1792, 2304)
            bg = nps()
            proj_tok(bg, 0, 2304, 2816)
            bl = nps()
            proj_feat(bl, 0, 2816, 2832)
            bag = nps()
            for c in range(4):
                proj_feat(bag, c * 128, 768 + c * 128, 768 + (c + 1) * 128)
            cp(T[0][:, 0:512], PS[:, bq, :], [bPS[bq]], [bT[0]], eng="act")
            cp(T[1][:, 0:512], PS[:, ba, :], [bPS[ba]], [bT[1]])
            cp(vb[:, :], PS[:, bv, :], [bPS[bv]], [bvb], eng="act")
            cp(lrT[:, 0, :], PS[0:16, bl, 0:128], [bPS[bl]], [blrT])
            act(T[4][:, 0:512], PS[:, bg, :], AF.Tanh, [bPS[bg]], [bT[4]], scale=0.5)
            ts(T[4][:, 0:512], T[4][:, 0:512], 0.5, 0.5, ALU.mult, ALU.add, [bT[4]], [bT[4]])
            tt(T[1][:, 512:1024], PS[:, bg, :], T[4][:, 0:512], ALU.mult, [bPS[bg], bT[4]], [bT[1]])
            act(T[4][:, 512:1024], PS[:, bag, :], AF.Tanh, [bPS[bag]], [bT[4]], scale=0.5)
            ts(T[4][:, 512:1024], T[4][:, 512:1024], 0.5, 0.5, ALU.mult, ALU.add, [bT[4]], [bT[4]])
            aTd = T[3][0:64, :].rearrange("p (h q) -> p h q", h=8)
            for r in range(2):
                prow = slice(r * 64, (r + 1) * 64)
                tt(aTd[:, r::2, :], PS[prow, bag, :].rearrange("p (c q) -> p c q", c=4),
                   T[4][prow, 512:1024].rearrange("p (c q) -> p c q", c=4), ALU.mult,
                   [bPS[bag], bT[4]], [bT[3]])
            def q_chain():
                yield from norm_rope_g(T[0][:, 0:512], bT[0], 8, qn, bqn,
                                       qr[:, :, :, :].rearrange("p j g d -> p g j d"), bqr,
                                       T[2][:, :], bT[2], gj=(2, 4))
                b = npt()
                for j in range(4):
                    tr(PT[:, b, j * 128:(j + 1) * 128], qr[:, j, :, :].rearrange("p g d -> p (g d)"), [bqr],
                       [bPTs[b]])
                yield
                for g in range(2):
                    prow = slice(g * 64, (g + 1) * 64)
                    cp(QTz[prow, g, :, :], PT[prow, b, 0:512].rearrange("p (j q) -> p j q", j=4),
                       [bPTs[b]], [bQTz])
                yield

            def gla_chain():
                dma(T[0][:, 512:1024], obwd[rows, :], [bob], [bT[0]])
                yield from gla_full_g(0, Sf, bSf, T[4], bT[4], T[1][:, 256:512], bT[1], T[1][:, 0:256], bT[1],
                                      vb, bvb, lrT[0:16, 0, :], blrT)
                osum = T[0][:, 512:1024]
                tt(osum, PS[:, 4, :], osum, ALU.add, [bPS[4], bT[0]], [bT[0]])
                yield
                tt(T[4][:, 0:512], osum, osum, ALU.mult, [bT[0]], [bT[4]])
                red(st4[:, 0:4], T[4][:, 0:512].rearrange("p (h d) -> p h d", h=4), [bT[4]], [bst4])
                yield
                rstd_from_ss(st4[:, 0:4], st5[:, 0:4], 128.0, bst4, bst5)
                yield
                o3 = osum.rearrange("p (h d) -> p h d", h=4)
                tt(o3, o3, st5[:, 0:4].unsqueeze(2).to_broadcast([128, 4, 128]), ALU.mult, [bT[0], bst5], [bT[0]])
                tt(o3, o3, gn[:, :].unsqueeze(1).to_broadcast([128, 4, 128]), ALU.mult, [bT[0], bgn], [bT[0]])
                tt(mg[:, :], osum, T[1][:, 512:1024], ALU.mult, [bT[0], bT[1]], [bmg])
                yield
                b = npt()
                for c in range(4):
                    tr(PT[:, b, c * 128:(c + 1) * 128], mg[:, c * 128:(c + 1) * 128], [bmg], [bPTs[b]])
                cp(mixT[:, 4:8, :], PT[:, b, 0:512].rearrange("p (a b) -> p a b", a=4), [bPTs[b]], [bmixT],
                   eng="act")
                yield

            interleave(q_chain(), gla_chain())
            dma(T[0][:, :], x_own[rows, :], [], [bT[0]])
            if t + 1 < t0 + segs[seg][1]:
                nrows = slice((t + 1) * 128, (t + 2) * 128)
                dma(xt[:, :], x_own[nrows, :], [], [bxt])
            def s_mm(kb):
                sb_ = kb % 2
                for g in range(2):
                    mm(PS[:, 2 * sb_ + g, :], KT[:, kb * 128:(kb + 1) * 128],
                       QTz[:, g, :, :].rearrange("p j q -> p (j q)"), True, True,
                       [bKT, bQTz], [bPS[2 * sb_ + g]])
            s_mm(0)
            if nkb > 1:
                s_mm(1)
            for kb in range(nkb):
                sb_ = kb % 2
                act(PTb[sb_][:, :, :], PS[:, 2 * sb_:2 * sb_ + 2, :], AF.Exp,
                    [bPS[2 * sb_], bPS[2 * sb_ + 1]], [bPTb[sb_]], scale=0.125)
                for g in range(2):
                    mm(PS[0:65, 4 + g, :], VO[:, kb, g, 0:65], PTb[sb_][:, g, :], kb == 0, kb == nkb - 1,
                       [bVO, bPTb[sb_]], [bPS[4 + g]])
                if kb + 2 < nkb:
                    s_mm(kb + 2)
            Oe = T[2]
            cp(Oe[0:65, :].rearrange("p (g n) -> p g n", g=2), PS[0:65, 4:6, :], [bPS[4], bPS[5]], [bT[2]],
               eng="act")
            act(Oe[64:65, :], Oe[64:65, :], AF.Ln, [bT[2]], [bT[2]])
            act(Oe[64:65, :], Oe[64:65, :], AF.Exp, [bT[2]], [bT[2]], scale=-1.0)
            for g in range(2):
                bb = nps()
                while bb >= 4:
                    bb = nps()
                mm(PS[0:64, bb, :], ones_f[64:65, 0:64], Oe[64:65, g * 512:(g + 1) * 512], True, True,
                   [bones_f, bT[2]], [bPS[bb]])
                tt(Oe[0:64, g * 512:(g + 1) * 512], Oe[0:64, g * 512:(g + 1) * 512], PS[0:64, bb, :], ALU.mult,
                   [bT[2], bPS[bb]], [bT[2]])
                On = Oe[0:64, g * 512:(g + 1) * 512].rearrange("p (j q) -> p j q", j=4)
                for r in range(2):
                    tt(mixT[r * 64:(r + 1) * 64, 2 * g:2 * g + 2, :], On[:, r::2, :],
                       aTd[:, 4 * g + r:4 * g + 4:2, :], ALU.mult, [bT[2], bT[3]], [bmixT])
            dma(T[3][:, :], fn_d[:, :], [], [bT[3]])
            bh = [nps(), nps()]
            for hh in range(2):
                for c in range(8):
                    mm(PS[:, bh[hh], :], mixT[:, c, :], Wout[:, c, hh * 512:(hh + 1) * 512], c == 0, c == 7,
                       [bmixT, bWout], [bPS[bh[hh]]])
            h2 = T[0]
            for hh in range(2):
                tt(h2[:, hh * 512:(hh + 1) * 512], PS[:, bh[hh], :], h2[:, hh * 512:(hh + 1) * 512], ALU.add,
                   [bPS[bh[hh]], bT[0]], [bT[0]])
            stt(xs[:, :], h2[:, :], 1.0, h2[:, :], ALU.mult, ALU.mult, [bT[0]], [bxs, bst1], accum=st1[:, 2:3])
            rstd_from_ss(st1[:, 2:3], st1[:, 3:4], float(D), bst1, bst1)
            ts(xs[:, :], h2[:, :], st1[:, 3:4], None, ALU.mult, None, [bT[0], bst1], [bxs])
            transpose8(xs, bxs, xT, bxT)
            cp(pb[:, :], pt[:, :], [bpt], [bpb])
            b = npt()
            for c in range(2):
                tr(PT[:, b, c * 128:(c + 1) * 128], pb[:, c * 128:(c + 1) * 128], [bpb], [bPTs[b]])
            cp(pT[:, :, :], PT[:, b, 0:256].rearrange("p (a b) -> p a b", a=2), [bPTs[b]], [bpT], eng="act")
            h3 = T[1]
            for hh in range(2):
                cs_ = slice(hh * 512, (hh + 1) * 512)
                bgp = nps()
                for c in range(8):
                    mm(PS[:, bgp, :], xT[:, c, :], Wpg[:, c, cs_], c == 0, c == 7, [bxT, bWpg], [bPS[bgp]])
                bpp = nps()
                for c in range(2):
                    mm(PS[:, bpp, :], pT[:, c, :], Wpp[:, c, cs_], c == 0, c == 1, [bpT, bWpp], [bPS[bpp]])
                act(T[4][:, cs_], PS[:, bgp, :], AF.Tanh, [bPS[bgp]], [bT[4]], scale=0.5)
                ts(T[4][:, cs_], T[4][:, cs_], 0.5, 0.5, ALU.mult, ALU.add, [bT[4]], [bT[4]])
                tt(T[4][:, cs_], T[4][:, cs_], PS[:, bpp, :], ALU.mult, [bT[4], bPS[bpp]], [bT[4]])
                tt(h3[:, cs_], T[4][:, cs_], h2[:, cs_], ALU.add, [bT[4], bT[0]], [bT[1]])
            stt(T[4][:, :], h3[:, :], 1.0, h3[:, :], ALU.mult, ALU.mult, [bT[1]], [bT[4], bst1], accum=st1[:, 4:5])
            rstd_from_ss(st1[:, 4:5], st1[:, 5:6], float(D), bst1, bst1)
            stt(T[4][:, :], h3[:, :], st1[:, 5:6], T[3][:, :], ALU.mult, ALU.mult, [bT[1], bst1, bT[3]], [bT[4]])
            dma(y_out[rows, :], T[4][:, :], [bT[4]], [Buf("yout")])

    for seg in range(len(segs)):
        phase_a(seg)
        phase_b1(seg)
        phase_b2(seg)

    S.emit(nc, st)
    st.close()
    return nc


def _rope_tab(pos):
    half = 32
    inv = (10000.0 ** (-np.arange(0, half, 2, dtype=np.float32) / half)).astype(np.float32)
    row = (pos // 64).astype(np.float32)
    col = (pos % 64).astype(np.float32)
    ar = row[:, None] * inv[None, :]
    ac = col[:, None] * inv[None, :]
    cr, sr, cc, sc = np.cos(ar), np.sin(ar), np.cos(ac), np.sin(ac)
    cos = np.concatenate([cr, cr, cc, cc], axis=1)
    sin = np.concatenate([-sr, sr, -sc, sc], axis=1)
    return np.concatenate([cos, sin], axis=1).astype(np.float32)


_NC_CACHE = {}


def kernel(x_prompt, x_sample, p_prompt, p_sample, mix_norm, w_in, q_norm, k_norm,
           w_gate_up_fwd, b_gate_fwd, w_gate_up_bwd, b_gate_bwd, gla_norm, w_out,
           ple_norm, w_ple_gate, w_ple_proj, final_norm):
    f32 = np.float32
    xp = np.asarray(x_prompt, f32)[0]
    xsm = np.asarray(x_sample, f32)
    pp = np.asarray(p_prompt, f32)[0, 0]
    psm = np.asarray(p_sample, f32)[0]
    tab_p = _rope_tab(np.arange(16384))
    tab_s = _rope_tab(np.arange(4096))
    rep = lambda v, n: np.ascontiguousarray(np.broadcast_to(np.asarray(v, f32).reshape(1, -1), (128, n)))
    tri = np.zeros((128, 4, 128), f32)
    s_, t_ = np.meshgrid(np.arange(128), np.arange(128), indexing="ij")
    tri[:, 0] = (s_ <= t_)
    tri[:, 1] = (s_ > t_)
    tri[:, 2] = (s_ >= t_)
    tri[:, 3] = (s_ < t_)
    common = {
        "w_in": np.ascontiguousarray(np.asarray(w_in, f32)[0]),
        "w_out": np.ascontiguousarray(np.asarray(w_out, f32)[0]),
        "w_pg": np.ascontiguousarray(np.asarray(w_ple_gate, f32)[0]),
        "w_pp": np.ascontiguousarray(np.asarray(w_ple_proj, f32)[0]),
        "wup_f": np.ascontiguousarray(np.asarray(w_gate_up_fwd, f32)[0]),
        "wup_b": np.ascontiguousarray(np.asarray(w_gate_up_bwd, f32)[0]),
        "bg_f": np.ascontiguousarray(np.asarray(b_gate_fwd, f32)[0].reshape(1, 256)),
        "bg_b": np.ascontiguousarray(np.asarray(b_gate_bwd, f32)[0].reshape(1, 256)),
        "mixn": np.ascontiguousarray(np.asarray(mix_norm, f32)[0].reshape(8, 128).T),
        "plen": np.ascontiguousarray(np.asarray(ple_norm, f32)[0].reshape(8, 128).T),
        "qn": rep(np.asarray(q_norm)[0], 64),
        "kn": rep(np.asarray(k_norm)[0], 64),
        "gn": rep(np.asarray(gla_norm)[0], 128),
        "fn": rep(np.asarray(final_norm), 1024),
        "ident": np.eye(128, dtype=f32).astype(ml_dtypes.bfloat16),
        "tri": np.ascontiguousarray(tri.reshape(128, 512)),
    }
    in_maps = []
    for c in range(NCORES):
        sq, hf = c // 2, c % 2
        own_p = slice(2048 * c, 2048 * (c + 1))
        own_s = slice(2048 * hf, 2048 * (hf + 1))
        m = np.zeros((NT_CTX, 2), f32)
        m[0:16 * c, 0] = 1.0
        m[16 * (c + 1):128, 1] = 1.0
        if hf == 1:
            m[128:144, 0] = 1.0
        else:
            m[144:160, 1] = 1.0
        d = dict(common)
        d["x_all"] = np.ascontiguousarray(np.concatenate([xp, xsm[sq]], axis=0))
        d["x_own"] = np.ascontiguousarray(np.concatenate([xp[own_p], xsm[sq][own_s]], axis=0))
        d["p_own"] = np.ascontiguousarray(np.concatenate([pp[own_p], psm[sq][own_s]], axis=0))
        d["tab_all"] = np.ascontiguousarray(np.concatenate([tab_p, tab_s], axis=0))
        d["tab_own"] = np.ascontiguousarray(np.concatenate([tab_p[own_p], tab_s[own_s]], axis=0))
        d["masks"] = np.ascontiguousarray(np.broadcast_to(m.reshape(1, -1), (128, NT_CTX * 2)))
        in_maps.append(d)
    if "nc" not in _NC_CACHE:
        _NC_CACHE["nc"] = build_program()
    res = run_bass_kernel_spmd(_NC_CACHE["nc"], in_maps, core_ids=list(range(NCORES)))
    ys = [np.asarray(r["y"], f32) for r in res.results]
    y_prompt = np.concatenate([y[0:2048] for y in ys], axis=0)[None]
    y_sample = np.stack([np.concatenate([ys[2 * s][2048:], ys[2 * s + 1][2048:]], axis=0) for s in range(4)])
    return (y_prompt.astype(f32), y_sample.astype(f32))
```

```python
import sys
import numpy as np
import ml_dtypes
from contextlib import ExitStack
import concourse.bass as bass
import concourse.mybir as mybir
from concourse.bass_utils import run_bass_kernel_spmd

F32 = mybir.dt.float32
BF16 = mybir.dt.bfloat16
ALU = mybir.AluOpType
AF = mybir.ActivationFunctionType
AX = mybir.AxisListType

NCORES = 8
D = 1024
TOK_OWN = 4096
NT_OWN = 32
NT_CTX = 160
EPS = 1e-6
ND = 8
MAX_OPS = None
DMA_SCRATCH = 512


class Buf:
    __slots__ = ("name", "lw", "rd", "excl")

    def __init__(self, name, excl=False):
        self.name = name
        self.excl = excl
        self.lw = None
        self.rd = []


class Op:
    __slots__ = ("eng", "fn", "deps", "signal", "sigval", "dma", "dma_idx", "line")


class Sched:
    ENGS = ("sp", "pe", "dve", "act", "pool")

    def __init__(self):
        self.ops = []
        self.ndma = 0

    def add(self, eng, fn, R=(), W=(), dma=False):
        idx = len(self.ops)
        xr = [b for b in R if b.excl]
        if xr:
            R = [b for b in R if not b.excl]
            W = list(W) + [b for b in xr if b not in W]
        raw, other = set(), set()
        for b in R:
            if b.lw is not None:
                raw.add(b.lw)
        for b in W:
            if b.lw is not None:
                other.add(b.lw)
            for r in b.rd:
                other.add(r)
        deps = set()
        for d in raw | other:
            p = self.ops[d]
            if p.dma or dma:
                deps.add(d)
            elif p.eng != eng:
                deps.add(d)
            elif eng != "pe":
                deps.add(d)
        op = Op()
        op.eng, op.fn, op.deps, op.signal, op.sigval, op.dma = eng, fn, deps, False, 0, dma
        op.dma_idx = -1
        op.line = sys._getframe(2).f_lineno
        if dma:
            op.dma_idx = self.ndma
            self.ndma += 1
        self.ops.append(op)
        for b in R:
            b.rd.append(idx)
        for b in W:
            b.lw = idx
            b.rd = []
        return idx

    def emit(self, nc, stack):
        if MAX_OPS is not None:
            self.ops = self.ops[:MAX_OPS]
            self.ndma = sum(1 for o in self.ops if o.dma)
        ops = self.ops
        for op in ops:
            best = {}
            keep = set()
            for d in op.deps:
                p = ops[d]
                if p.dma:
                    keep.add(d)
                else:
                    if p.eng not in best or best[p.eng] < d:
                        best[p.eng] = d
            keep |= set(best.values())
            op.deps = keep
            for d in keep:
                ops[d].signal = True
        cnt = {e: 0 for e in self.ENGS}
        for op in ops:
            if op.dma:
                op.signal = True
            elif op.signal:
                cnt[op.eng] += 1
                op.sigval = cnt[op.eng]
        sems = {e: stack.enter_context(nc.semaphore("s_" + e)) for e in self.ENGS}
        dsems = [stack.enter_context(nc.semaphore("d%d" % i)) for i in range(ND)]
        block = stack.enter_context(nc.Block())
        ndma = self.ndma

        def run(engname):
            def body(e):
                waited = {}

                def wait(sem, val):
                    k = id(sem)
                    if waited.get(k, 0) < val:
                        e.wait_ge(sem, val)
                        waited[k] = val

                for op in ops:
                    if op.eng != engname:
                        continue
                    for d in sorted(op.deps):
                        p = ops[d]
                        if p.dma:
                            wait(dsems[p.dma_idx % ND], 16 * (p.dma_idx // ND + 1))
                        else:
                            wait(sems[p.eng], p.sigval)
                    if op.dma:
                        j = op.dma_idx
                        if j >= ND:
                            wait(dsems[j % ND], 16 * (j // ND))
                        op.fn(e).then_inc(dsems[j % ND], 16)
                    else:
                        ins = op.fn(e)
                        if op.signal:
                            ins.then_inc(sems[engname], 1)
                if engname == "sp":
                    for i in range(ND):
                        n = (ndma - i + ND - 1) // ND if ndma > i else 0
                        if n > 0:
                            wait(dsems[i], 16 * n)
            return body

        block.sync(run("sp"))
        block.tensor(run("pe"))
        block.vector(run("dve"))
        block.scalar(run("act"))
        block.gpsimd(run("pool"))


def build_program(segs=((128, 16), (32, 16))):
    NT_CTX = sum(a for a, _ in segs)
    NT_OWN = sum(b for _, b in segs)
    TOK_OWN = NT_OWN * 128
    MAXC = max(a for a, _ in segs)
    nc = bass.Bass("TRN2", target_bir_lowering=False, dynamic_dma_scratch_size=DMA_SCRATCH)
    S = Sched()
    st = ExitStack()

    def din(name, shape, dt=F32):
        return nc.dram_tensor(name, list(shape), dt, kind="ExternalInput").ap()

    x_all = din("x_all", [NT_CTX * 128, D])
    x_own = din("x_own", [TOK_OWN, D])
    p_own = din("p_own", [TOK_OWN, 256])
    tab_all = din("tab_all", [NT_CTX * 128, 128])
    tab_own = din("tab_own", [TOK_OWN, 128])
    masks_d = din("masks", [128, NT_CTX * 2])
    w_in = din("w_in", [D, 2848])
    w_out = din("w_out", [D, D])
    w_pg = din("w_pg", [D, D])
    w_pp = din("w_pp", [256, D])
    wup_d = [din("wup_f", [16, 256]), din("wup_b", [16, 256])]
    bg_d = [din("bg_f", [1, 256]), din("bg_b", [1, 256])]
    mixn_d = din("mixn", [128, 8])
    plen_d = din("plen", [128, 8])
    qn_d = din("qn", [128, 64])
    kn_d = din("kn", [128, 64])
    gn_d = din("gn", [128, 128])
    fn_d = din("fn", [128, D])
    ident_d = din("ident", [128, 128], BF16)
    tri_d = din("tri", [128, 4 * 128])
    y_out = nc.dram_tensor("y", [TOK_OWN, D], F32, kind="ExternalOutput").ap()
    obwd = nc.dram_tensor("obwd", [TOK_OWN, 512], F32)

    bufs = {}

    def sb(name, shape, dt=F32):
        t = st.enter_context(nc.sbuf_tensor("sb_" + name, list(shape), dt))
        bufs[name] = Buf(name)
        return t, bufs[name]

    Win, bWin = sb("Win", [128, 8, 2848], BF16)
    Wout, bWout = sb("Wout", [128, 8, D], BF16)
    Wpg, bWpg = sb("Wpg", [128, 8, D], BF16)
    Wpp, bWpp = sb("Wpp", [128, 2, D], BF16)
    Wup, bWup = sb("Wup", [17, 2, 256], BF16)
    ident, bident = sb("ident", [128, 128], BF16)
    tri, btri = sb("tri", [128, 4, 128], F32)
    ones_bf, bones_bf = sb("ones_bf", [128, 128], BF16)
    ones_f, bones_f = sb("ones_f", [128, 64], F32)
    qn, bqn = sb("qn", [128, 64])
    kn, bkn = sb("kn", [128, 64])
    gn, bgn = sb("gn", [128, 128])
    masks, bmasks = sb("masks", [128, NT_CTX * 2])
    mixn, bmixn = sb("mixn", [128, 8])
    plen, bplen = sb("plen", [128, 8])
    KT, bKT = sb("KT", [128, MAXC * 128], BF16)
    VO, bVO = sb("VO", [128, MAXC, 2, 66], BF16)
    T = []
    bT = []
    for i in range(5):
        t, b = sb("T%d" % i, [128, 1024], F32)
        T.append(t)
        bT.append(b)
    xt, bxt = sb("xt", [128, D])
    xs, bxs = sb("xs", [128, D], BF16)
    xT, bxT = sb("xT", [128, 8, 128], BF16)
    tab, btab = sb("tab", [128, 128])
    pt, bpt = sb("pt", [128, 256])
    pb, bpb = sb("pb", [128, 256], BF16)
    pT, bpT = sb("pT", [128, 2, 128], BF16)
    kr, bkr = sb("kr", [128, 128], BF16)
    lkA, blkA, vbA, bvbA, lrTA, blrTA = [], [], [], [], [], []
    for i in range(2):
        t, b = sb("lkA%d" % i, [128, 256]); lkA.append(t); blkA.append(b)
        t, b = sb("vbA%d" % i, [128, 512], BF16); vbA.append(t); bvbA.append(b)
        t, b = sb("lrTA%d" % i, [17, 2, 128], BF16); lrTA.append(t); blrTA.append(b)
    dcy2, bdcy2, acf2, bacf2 = [], [], [], []
    for i in range(2):
        t, b = sb("dcyA%d" % i, [128, 2]); dcy2.append(t); bdcy2.append(b)
        t, b = sb("acfA%d" % i, [128, 2]); acf2.append(t); bacf2.append(b)
    vb, bvb = sb("vb", [128, 512], BF16)
    lrT, blrT = sb("lrT", [17, 2, 128], BF16)
    gq, bgq = sb("gq", [128, 3, 256], BF16)
    qkT, bqkT = sb("qkT", [128, 4, 128], BF16)
    AT, bAT = sb("AT", [128, 4, 128], BF16)
    Sbf, bSbf = sb("Sbf", [128, 2, 2, 128], BF16)
    qdz, bqdz = sb("qdz", [128, 2, 2, 128], BF16)
    Sf, bSf = sb("Sf", [128, 2, 128])
    Lb, bLb = sb("Lb", [128, 2, 128])
    Pb, bPb = sb("Pb", [128, 2])
    dcy, bdcy = sb("dcy", [128, 2])
    acf, bacf = sb("acf", [128, 2])
    st1, bst1 = sb("st1", [128, 8])
    st2, bst2 = sb("st2", [128, 8])
    st3, bst3 = sb("st3", [128, 8])
    st4, bst4 = sb("st4", [128, 8])
    stA, bstA = [], []
    for i in range(2):
        t, b = sb("stA%d" % i, [128, 2]); stA.append(t); bstA.append(b)
    st5, bst5 = sb("st5", [128, 8])
    mg, bmg = sb("mg", [128, 512], BF16)
    mixT, bmixT = sb("mixT", [128, 8, 128], BF16)
    qr, bqr = sb("qr", [128, 4, 2, 64], BF16)
    QTz, bQTz = sb("QTz", [128, 2, 4, 128], BF16)
    H2, bH2 = sb("H2", [128, D])
    FNt, bFNt = sb("FNt", [128, D])
    G2, bG2 = sb("G2", [128, D])
    HT, bHT = sb("HT", [128, 8, 128], BF16)
    mixTb, bmixTb = sb("mixTb", [128, 8, 128], BF16)
    pTb, bpTb = sb("pTb", [128, 2, 128], BF16)
    stP, bstP = sb("stP", [128, 8])
    PTb = []
    bPTb = []
    for i in range(2):
        t, b = sb("PTb%d" % i, [128, 2, 512], BF16)
        PTb.append(t)
        bPTb.append(b)

    PS = st.enter_context(nc.psum_tensor("PS", [128, 6, 512], F32))
    PT = st.enter_context(nc.psum_tensor("PT", [128, 2, 1024], BF16))
    bPS = [Buf("PS%d" % i, True) for i in range(6)]
    bPTs = [Buf("PT%d" % i, True) for i in range(2)]
    rr = {"ps": 0, "pt": 0, "dq": 0}

    def nps():
        i = rr["ps"]
        rr["ps"] = (i + 1) % 6
        return i

    def npt():
        i = rr["pt"]
        rr["pt"] = (i + 1) % 2
        return i

    def dma(out, in_, R, W):
        S.add("sp", lambda e, o=out, i=in_: e.dma_start(out=o, in_=i), R, W, dma=True)

    def mm(out, lhsT, rhs, start, stop, R, W):
        S.add("pe", lambda e, o=out, l=lhsT, r=rhs, s0=start, s1=stop:
              e.matmul(o, lhsT=l, rhs=r, start=s0, stop=s1), R, W)

    def tr(out, in_, R, W):
        S.add("pe", lambda e, o=out, i=in_: e.transpose(o, i, ident[:, :]), list(R) + [bident], W)

    def act(out, in_, func, R, W, scale=1.0, bias=0.0):
        S.add("act", lambda e, o=out, i=in_, f=func, s=scale, b=bias:
              e.activation(out=o, in_=i, func=f, bias=b, scale=s), R, W)

    def tt(out, in0, in1, op, R, W, eng="dve"):
        S.add(eng, lambda e, o=out, a=in0, b=in1, p=op: e.tensor_tensor(out=o, in0=a, in1=b, op=p), R, W)

    def ts(out, in0, s1, s2, op0, op1, R, W, eng="dve"):
        if s2 is None:
            S.add(eng, lambda e, o=out, a=in0, x=s1, p0=op0:
                  e.tensor_scalar(out=o, in0=a, scalar1=x, scalar2=None, op0=p0), R, W)
        else:
            S.add(eng, lambda e, o=out, a=in0, x=s1, y=s2, p0=op0, p1=op1:
                  e.tensor_scalar(out=o, in0=a, scalar1=x, scalar2=y, op0=p0, op1=p1), R, W)

    def stt(out, in0, scalar, in1, op0, op1, R, W, accum=None):
        if accum is None:
            S.add("dve", lambda e, o=out, a=in0, s=scalar, b=in1, p0=op0, p1=op1:
                  e.scalar_tensor_tensor(out=o, in0=a, scalar=s, in1=b, op0=p0, op1=p1), R, W)
        else:
            S.add("dve", lambda e, o=out, a=in0, s=scalar, b=in1, p0=op0, p1=op1, ac=accum:
                  e.scalar_tensor_tensor(out=o, in0=a, scalar=s, in1=b, op0=p0, op1=p1, accum_out=ac), R, W)

    def cp(out, in_, R, W, eng="dve"):
        if eng == "act":
            S.add("act", lambda e, o=out, i=in_: e.copy(out=o, in_=i), R, W)
        else:
            S.add(eng, lambda e, o=out, i=in_: e.tensor_copy(out=o, in_=i), R, W)

    def red(out, in_, R, W):
        S.add("dve", lambda e, o=out, i=in_: e.tensor_reduce(out=o, in_=i, axis=AX.X, op=ALU.add), R, W)

    def recip(out, in_, R, W):
        S.add("dve", lambda e, o=out, i=in_: e.reciprocal(out=o, in_=i), R, W)

    def memset(ap, val, W, eng="dve"):
        S.add(eng, lambda e, a=ap, v=val: e.memset(a, v), (), W)

    def rstd_from_ss(ss_ap, out_ap, n, bss, bout):
        act(out_ap, ss_ap, AF.Ln, [bss], [bout], scale=1.0 / n, bias=EPS)
        act(out_ap, out_ap, AF.Exp, [bout], [bout], scale=-0.5)

    dma(ident[:, :], ident_d[:, :], [], [bident])
    dma(tri[:, :, :], tri_d.rearrange("p (a b) -> p a b", a=4), [], [btri])
    dma(qn[:, :], qn_d[:, :], [], [bqn])
    dma(kn[:, :], kn_d[:, :], [], [bkn])
    dma(gn[:, :], gn_d[:, :], [], [bgn])
    dma(masks[:, :], masks_d[:, :], [], [bmasks])
    dma(mixn[:, :], mixn_d[:, :], [], [bmixn])
    dma(plen[:, :], plen_d[:, :], [], [bplen])
    dma(FNt[:, :], fn_d[:, :], [], [bFNt])
    memset(ones_bf[:, :], 1.0, [bones_bf])
    memset(ones_f[:, :], 1.0, [bones_f])
    memset(VO[:, :, :, 64:66], 1.0, [bVO])
    memset(QTz[:, :, :, :], 0.0, [bQTz])
    memset(Sbf[:, :, :, :], 0.0, [bSbf])
    memset(qdz[:, :, :, :], 0.0, [bqdz])

    k = 0
    for kc in range(8):
        for (c0, c1) in ((0, 1024), (1024, 2048), (2048, 2848)):
            n = c1 - c0
            ti = k % 2
            k += 1
            dma(T[ti][:, 0:n], w_in[kc * 128:(kc + 1) * 128, c0:c1], [], [bT[ti]])
            ts(Win[:, kc, c0:c1], T[ti][:, 0:n], mixn[:, kc:kc + 1], None, ALU.mult, None,
               [bT[ti], bmixn], [bWin])
    for kc in range(8):
        ti = k % 2
        k += 1
        dma(T[ti][:, :], w_pg[kc * 128:(kc + 1) * 128, :], [], [bT[ti]])
        ts(Wpg[:, kc, :], T[ti][:, :], plen[:, kc:kc + 1], None, ALU.mult, None, [bT[ti], bplen], [bWpg])
    for kc in range(8):
        ti = k % 2
        k += 1
        dma(T[ti][:, :], w_out[kc * 128:(kc + 1) * 128, :], [], [bT[ti]])
        cp(Wout[:, kc, :], T[ti][:, :], [bT[ti]], [bWout])
    for kc in range(2):
        ti = k % 2
        k += 1
        dma(T[ti][:, :], w_pp[kc * 128:(kc + 1) * 128, :], [], [bT[ti]])
        cp(Wpp[:, kc, :], T[ti][:, :], [bT[ti]], [bWpp])
    for d in range(2):
        dma(T[2][0:16, d * 256:(d + 1) * 256], wup_d[d][:, :], [], [bT[2]])
        dma(T[2][16:17, d * 256:(d + 1) * 256], bg_d[d][:, :], [], [bT[2]])
    cp(Wup[:, :, :], T[2][0:17, 0:512].rearrange("p (a b) -> p a b", a=2), [bT[2]], [bWup])
    memset(lrT[:, :, :], 1.0, [blrT])
    for i in range(2):
        memset(lrTA[i][:, :, :], 1.0, [blrTA[i]])

    def front_g(x_rows_ap, load=True):
        if load:
            dma(xt[:, :], x_rows_ap, [], [bxt])
        stt(xs[:, :], xt[:, :], 1.0, xt[:, :], ALU.mult, ALU.mult, [bxt], [bxs, bst1], accum=st1[:, 0:1])
        yield
        rstd_from_ss(st1[:, 0:1], st1[:, 1:2], float(D), bst1, bst1)
        act(xs[:, :], xt[:, :], AF.Copy, [bxt, bst1], [bxs], scale=st1[:, 1:2])
        yield
        transpose8(xs, bxs, xT, bxT)
        yield

    def front(x_rows_ap):
        for _ in front_g(x_rows_ap):
            pass

    def transpose8(src, bsrc, dst, bdst):
        b = npt()
        for c in range(8):
            tr(PT[:, b, c * 128:(c + 1) * 128], src[:, c * 128:(c + 1) * 128], [bsrc], [bPTs[b]])
        cp(dst[:, :, :], PT[:, b, :].rearrange("p (a b) -> p a b", a=8), [bPTs[b]], [bdst], eng="act")

    def proj_tok(bank, off, c0, c1):
        n = c1 - c0
        for kc in range(8):
            mm(PS[:, bank, off:off + n], xT[:, kc, :], Win[:, kc, c0:c1], kc == 0, kc == 7,
               [bxT, bWin], [bPS[bank]])

    def proj_feat(bank, off, c0, c1, p0=0):
        m = c1 - c0
        for kc in range(8):
            mm(PS[p0:p0 + m, bank, off:off + 128], Win[:, kc, c0:c1], xT[:, kc, :], kc == 0, kc == 7,
               [bxT, bWin], [bPS[bank]])

    def norm_rope_g(src, bsrc, nh, wn, bwn, dst, bdst, tmp, btmp, gj=None, tabs=None):
        n = nh * 64
        tab_t, btab_t = tabs if tabs is not None else (tab, btab)
        sq = tmp[:, 0:n]
        yv = tmp[:, n:2 * n]
        tt(sq, src, src, ALU.mult, [bsrc], [btmp])
        red(st2[:, 0:nh], sq.rearrange("p (h d) -> p h d", h=nh), [btmp], [bst2])
        yield
        rstd_from_ss(st2[:, 0:nh], st3[:, 0:nh], 64.0, bst2, bst3)
        yield
        y3 = yv.rearrange("p (h d) -> p h d", h=nh)
        tt(y3, src.rearrange("p (h d) -> p h d", h=nh),
           st3[:, 0:nh].unsqueeze(2).to_broadcast([128, nh, 64]), ALU.mult, [bsrc, bst3], [btmp])
        tt(y3, y3, wn[:, :].unsqueeze(1).to_broadcast([128, nh, 64]), ALU.mult, [btmp, bwn], [btmp])
        yield
        y5 = yv.rearrange("p (h a b c) -> p h a b c", h=nh, a=2, b=2)
        s5 = sq.rearrange("p (h a b c) -> p h a b c", h=nh, a=2, b=2)
        cos4 = tab_t[:, 0:64].rearrange("p (a b c) -> p a b c", a=2, b=2)
        sin4 = tab_t[:, 64:128].rearrange("p (a b c) -> p a b c", a=2, b=2)
        for hf in range(2):
            tt(s5[:, :, :, hf, :], y5[:, :, :, 1 - hf, :],
               sin4[:, :, hf, :].unsqueeze(1).to_broadcast([128, nh, 2, 16]), ALU.mult,
               [btmp, btab_t], [btmp])
        tt(y3, y3, tab_t[:, 0:64].unsqueeze(1).to_broadcast([128, nh, 64]), ALU.mult, [btmp, btab_t], [btmp])
        if gj is None:
            tt(dst, y3, sq.rearrange("p (h d) -> p h d", h=nh), ALU.add, [btmp], [bdst])
        else:
            g_, j_ = gj
            tt(dst, yv.rearrange("p (g j d) -> p g j d", g=g_, j=j_),
               sq.rearrange("p (g j d) -> p g j d", g=g_, j=j_), ALU.add, [btmp], [bdst])

    def norm_rope(*a, **k):
        for _ in norm_rope_g(*a, **k):
            pass

    TRI_CS = (0, 2)
    TRI_REM = (1, 3)
    TRI_MASK = (0, 1)

    def kvblk(bk, p, hf):
        return PS[hf * 64:(hf + 1) * 64, bk, p * 256 + hf * 128:p * 256 + hf * 128 + 128]

    def gla_full_g(d, St, bSt, E, bE, lk_ap, blk, lq_ap, blq, vb_t, bvb_t, lr_ap, blr):
        bx, bc, ba = 3, 4, 5
        for r in range(2):
            pr = slice(r * 64, (r + 1) * 64)
            cp(Sbf[pr, :, r, :], St[pr, :, :], [bSt], [bSbf])
        mm(PS[:, bx, 0:256], lr_ap, Wup[0:17, d, :], True, True, [blr, bWup], [bPS[bx]])
        yield
        act(E[:, 0:256], PS[:, bx, 0:256], AF.Exp, [bPS[bx]], [bE], scale=-1.0)
        act(E[:, 0:256], E[:, 0:256], AF.Ln, [bE], [bE], bias=1.0)
        yield
        mm(PS[:, bc, 0:256], tri[:, TRI_CS[d], :], E[:, 0:256], True, True, [btri, bE], [bPS[bc]])
        mm(PS[:, bc, 256:512], tri[:, TRI_REM[d], :], E[:, 0:256], True, True, [btri, bE], [bPS[bc]])
        for p in range(2):
            mm(PS[:, bx, 256 + p:257 + p], E[:, p * 128:(p + 1) * 128], ones_f[:, 0:1], True, True,
               [bE, bones_f], [bPS[bx]])
        yield
        act(E[:, 256:512], PS[:, bc, 0:256], AF.Exp, [bPS[bc]], [bE], scale=-1.0 / 16)
        act(E[:, 768:1024], PS[:, bc, 0:256], AF.Exp, [bPS[bc]], [bE], scale=1.0 / 16)
        act(E[:, 512:768], PS[:, bc, 256:512], AF.Exp, [bPS[bc]], [bE], scale=-1.0 / 16)
        act(dcy[:, 0:2], PS[:, bx, 256:258], AF.Exp, [bPS[bx]], [bdcy], scale=-1.0 / 16)
        yield
        stt(gq[:, 0, :], lq_ap, 0.125, E[:, 256:512], ALU.mult, ALU.mult, [blq, bE], [bgq])
        tt(gq[:, 1, :], lk_ap, E[:, 768:1024], ALU.mult, [blk, bE], [bgq])
        tt(gq[:, 2, :], lk_ap, E[:, 512:768], ALU.mult, [blk, bE], [bgq])
        yield
        b = npt()
        for i in range(2):
            for p in range(2):
                tr(PT[:, b, (i * 2 + p) * 128:(i * 2 + p + 1) * 128], gq[:, i, p * 128:(p + 1) * 128],
                   [bgq], [bPTs[b]])
        bk = bx
        for p in range(2):
            mm(PS[:, bk, p * 256:(p + 1) * 256], gq[:, 2, p * 128:(p + 1) * 128], vb_t[:, p * 256:(p + 1) * 256],
               True, True, [bgq, bvb_t], [bPS[bk]])
        yield
        cp(qkT[:, :, :], PT[:, b, 0:512].rearrange("p (a b) -> p a b", a=4), [bPTs[b]], [bqkT], eng="act")
        for r in range(2):
            pr = slice(r * 64, (r + 1) * 64)
            cp(qdz[pr, :, r, :], PT[pr, b, 0:256].rearrange("p (a b) -> p a b", a=2), [bPTs[b]], [bqdz])
        yield
        for p in range(2):
            mm(PS[:, ba, p * 256:(p + 1) * 256], qkT[:, 2 + p, :],
               qdz[:, p, :, :].rearrange("p r c -> p (r c)"), True, True, [bqkT, bqdz], [bPS[ba]])
        yield
        tt(AT[:, :, :], PS[:, ba, :].rearrange("p (a b) -> p a b", a=4),
           tri[:, TRI_MASK[d], :].unsqueeze(1).to_broadcast([128, 4, 128]), ALU.mult,
           [bPS[ba], btri], [bAT])
        yield
        bo = bc
        for p in range(2):
            mm(PS[:, bo, p * 256:(p + 1) * 256], qkT[:, p, :], Sbf[:, p, :, :].rearrange("p r c -> p (r c)"),
               True, False, [bqkT, bSbf], [bPS[bo]])
            for r in range(2):
                h = 2 * p + r
                mm(PS[:, bo, h * 128:(h + 1) * 128], AT[:, h, :], vb_t[:, h * 128:(h + 1) * 128], False, r == 1,
                   [bAT, bvb_t], [bPS[bo]])
        yield
        for p in range(2):
            for hf in range(2):
                rows = slice(hf * 64, (hf + 1) * 64)
                stt(St[rows, p, :], St[rows, p, :], dcy[rows, p:p + 1], kvblk(bk, p, hf), ALU.mult, ALU.add,
                    [bSt, bdcy, bPS[bk]], [bSt])
        yield

    def a_pre(seg, n, par):
        xt2, bxt2 = ((xt, bxt), (T[4], bT[4]))[par]
        rows = slice(n * 128, (n + 1) * 128)
        dma(xt2[:, :], x_all[rows, :], [], [bxt2])
        stt(T[1][:, :], xt2[:, :], 1.0, xt2[:, :], ALU.mult, ALU.mult, [bxt2], [bT[1], bstA[par]],
            accum=stA[par][:, 0:1])
        yield
        rstd_from_ss(stA[par][:, 0:1], stA[par][:, 1:2], float(D), bstA[par], bstA[par])
        yield

    def a_headA(seg, n, par):
        xt2, bxt2 = ((xt, bxt), (T[4], bT[4]))[par]
        tb2, btb2 = ((tab, btab), (pt, bpt))[par]
        rows = slice(n * 128, (n + 1) * 128)
        dma(tb2[:, 0:128], tab_all[rows, :], [], [btb2])
        act(xs[:, :], xt2[:, :], AF.Copy, [bxt2, bstA[par]], [bxs], scale=stA[par][:, 1:2])
        yield
        transpose8(xs, bxs, xT, bxT)
        yield
        ba, bv, bl = 0, 1, 2
        proj_tok(ba, 0, 512, 768)
        proj_tok(ba, 256, 1536, 1792)
        yield
        proj_tok(bv, 0, 1792, 2304)
        yield
        proj_feat(bl, 0, 2816, 2832)
        proj_feat(bl, 128, 2832, 2848)
        yield

    def a_headB(seg, n, par):
        n0 = sum(a for a, _ in segs[:seg])
        j = n - n0
        tb2, btb2 = ((tab, btab), (pt, bpt))[par]
        ba, bv, bl = 0, 1, 2
        cp(T[0][:, 0:128], PS[:, ba, 0:128], [bPS[ba]], [bT[0]], eng="act")
        cp(VO[:, j, :, 0:64], PS[:, ba, 128:256].rearrange("p (g d) -> p g d", g=2), [bPS[ba]], [bVO],
           eng="act")
        cp(lkA[par][:, :], PS[:, ba, 256:512], [bPS[ba]], [blkA[par]])
        yield
        cp(vbA[par][:, :], PS[:, bv, :], [bPS[bv]], [bvbA[par]], eng="act")
        cp(lrTA[par][0:16, :, :], PS[0:16, bl, 0:256].rearrange("p (a b) -> p a b", a=2), [bPS[bl]], [blrTA[par]])
        yield
        yield from norm_rope_g(T[0][:, 0:128], bT[0], 2, kn, bkn, kr[:, :].rearrange("p (h d) -> p h d", h=2),
                               bkr, T[0][:, 256:768], bT[0], tabs=(tb2, btb2))
        yield
        b = npt()
        tr(PT[:, b, 0:128], kr[:, :], [bkr], [bPTs[b]])
        cp(KT[:, j * 128:(j + 1) * 128], PT[:, b, 0:128], [bPTs[b]], [bKT])
        yield

    def a_tail(seg, n, par):
        E2 = (T[2], T[3])
        bE2 = (bT[2], bT[3])
        lk_ap, blk = lkA[par][:, :], blkA[par]
        mcol = [masks[:, 2 * n + d:2 * n + d + 1] for d in range(2)]
        bx = [3, 4]
        bc = 5
        for d in range(2):
            mm(PS[:, bx[d], 0:256], lrTA[par][0:17, d, :], Wup[0:17, d, :], True, True,
               [blrTA[par], bWup], [bPS[bx[d]]])
        yield
        for d in range(2):
            act(E2[d][:, 0:256], PS[:, bx[d], 0:256], AF.Exp, [bPS[bx[d]]], [bE2[d]], scale=-1.0)
        yield
        for d in range(2):
            act(E2[d][:, 0:256], E2[d][:, 0:256], AF.Ln, [bE2[d]], [bE2[d]], bias=1.0)
        yield
        for d in range(2):
            mm(PS[:, bc, d * 256:(d + 1) * 256], tri[:, TRI_REM[d], :], E2[d][:, 0:256], True, True,
               [btri, bE2[d]], [bPS[bc]])
            for p in range(2):
                mm(PS[:, bx[d], 256 + p:257 + p], E2[d][:, p * 128:(p + 1) * 128], ones_f[:, 0:1], True, True,
                   [bE2[d], bones_f], [bPS[bx[d]]])
        yield
        for d in range(2):
            act(E2[d][:, 512:768], PS[:, bc, d * 256:(d + 1) * 256], AF.Exp, [bPS[bc]], [bE2[d]], scale=-1.0 / 16)
            act(dcy2[d][:, 0:2], PS[:, bx[d], 256:258], AF.Exp, [bPS[bx[d]]], [bdcy2[d]], scale=-1.0 / 16)
        yield
        for d in range(2):
            stt(gq[:, d, :], lk_ap, mcol[d], E2[d][:, 512:768], ALU.mult, ALU.mult, [blk, bE2[d], bmasks], [bgq])
            ts(acf2[d][:, :], dcy2[d][:, :], -1.0, mcol[d], ALU.add, ALU.mult, [bdcy2[d], bmasks], [bacf2[d]])
            ts(acf2[d][:, :], acf2[d][:, :], 1.0, None, ALU.add, None, [bacf2[d]], [bacf2[d]])
        yield
        bk = bx
        for d in range(2):
            for p in range(2):
                mm(PS[:, bk[d], p * 256:(p + 1) * 256], gq[:, d, p * 128:(p + 1) * 128],
                   vbA[par][:, p * 256:(p + 1) * 256], True, True, [bgq, bvbA[par]], [bPS[bk[d]]])
        yield
        for d in range(2):
            for p in range(2):
                for hf in range(2):
                    r = slice(hf * 64, (hf + 1) * 64)
                    if d == 0:
                        stt(Sf[r, p, :], Sf[r, p, :], acf2[0][r, p:p + 1], kvblk(bk[0], p, hf), ALU.mult, ALU.add,
                            [bSf, bacf2[0], bPS[bk[0]]], [bSf])
                    else:
                        stt(Lb[r, p, :], kvblk(bk[1], p, hf), Pb[r, p:p + 1], Lb[r, p, :], ALU.mult, ALU.add,
                            [bLb, bPb, bPS[bk[1]]], [bLb])
            yield
        tt(Pb[:, :], Pb[:, :], acf2[1][:, :], ALU.mult, [bPb, bacf2[1]], [bPb])
        yield

    def interleave(*gens):
        gens = [g for g in gens if g is not None]
        while gens:
            for g in list(gens):
                try:
                    next(g)
                except StopIteration:
                    gens.remove(g)

    def phase_a(seg):
        n0 = sum(a for a, _ in segs[:seg])
        n1 = n0 + segs[seg][0]
        memset(Sf[:, :, :], 0.0, [bSf])
        memset(Lb[:, :, :], 0.0, [bLb])
        memset(Pb[:, :], 1.0, [bPb])
        tiles = list(range(n0, n1))
        N_ = len(tiles)

        def G(fn, k):
            return fn(seg, tiles[k], k % 2) if 0 <= k < N_ else None
        for k in range(-3, N_):
            interleave(G(a_headA, k + 2), G(a_headB, k + 1), G(a_tail, k), G(a_pre, k + 3))

    LQK = (T[1], T[4])
    bLQK = (bT[1], bT[4])

    def b1_head(t, par):
        rows = slice(t * 128, (t + 1) * 128)
        yield from front_g(x_own[rows, :])
        ba, bv, bl = 0, 1, 2
        proj_tok(ba, 0, 1280, 1792)
        yield
        proj_tok(bv, 0, 1792, 2304)
        proj_feat(bl, 128, 2832, 2848)
        yield
        cp(LQK[par][:, 0:512], PS[:, ba, :], [bPS[ba]], [bLQK[par]])
        cp(vbA[par][:, :], PS[:, bv, :], [bPS[bv]], [bvbA[par]], eng="act")
        cp(lrTA[par][0:16, 1, :], PS[0:16, bl, 128:256], [bPS[bl]], [blrTA[par]])
        yield

    def b1_tail(t, par):
        rows = slice(t * 128, (t + 1) * 128)
        yield from gla_full_g(1, Lb, bLb, T[2], bT[2], LQK[par][:, 256:512], bLQK[par], LQK[par][:, 0:256],
                              bLQK[par], vbA[par], bvbA[par], lrTA[par][0:17, 1, :], blrTA[par])
        hs_ = slice(par * 512, (par + 1) * 512)
        cp(T[0][:, hs_], PS[:, 4, :], [bPS[4]], [bT[0]])
        dma(obwd[rows, :], T[0][:, hs_], [bT[0]], [bufs.setdefault("obwd", Buf("obwd"))])
        yield

    def phase_b1(seg):
        t0 = sum(b for _, b in segs[:seg])
        tiles = list(range(t0 + segs[seg][1] - 1, t0 - 1, -1))
        interleave(b1_head(tiles[0], 0))
        for i, t in enumerate(tiles):
            h = b1_head(tiles[i + 1], (i + 1) % 2) if i + 1 < len(tiles) else None
            interleave(h, b1_tail(t, i % 2))

    def interleave_g(*gens):
        gens = [g for g in gens if g is not None]
        while gens:
            for g in list(gens):
                try:
                    next(g)
                except StopIteration:
                    gens.remove(g)
            yield

    def b2_pre(seg, t, par, first):
        bob = bufs.setdefault("obwd", Buf("obwd"))
        mixT_t, bmixT_t = ((mixT, bmixT), (mixTb, bmixTb))[par]
        pT_t, bpT_t = ((pT, bpT), (pTb, bpTb))[par]
        rows = slice(t * 128, (t + 1) * 128)
        dma(tab[:, :], tab_own[rows, :], [], [btab])
        dma(pt[:, :], p_own[rows, :], [], [bpt])
        yield from front_g(x_own[rows, :], load=first)
        bq = nps()
        proj_tok(bq, 0, 0, 512)
        yield
        ba = nps()
        proj_tok(ba, 0, 1280, 1792)
        yield
        cp(T[0][:, 0:512], PS[:, bq, :], [bPS[bq]], [bT[0]], eng="act")
        bv = nps()
        proj_tok(bv, 0, 1792, 2304)
        yield
        cp(T[1][:, 0:512], PS[:, ba, :], [bPS[ba]], [bT[1]])
        bl = nps()
        proj_feat(bl, 0, 2816, 2832)
        yield
        cp(vb[:, :], PS[:, bv, :], [bPS[bv]], [bvb], eng="act")
        cp(lrT[0:16, 0, :], PS[0:16, bl, 0:128], [bPS[bl]], [blrT])
        bg = nps()
        proj_tok(bg, 0, 2304, 2816)
        yield
        bag = nps()
        for c in range(4):
            proj_feat(bag, c * 128, 768 + c * 128, 768 + (c + 1) * 128)
        yield
        cp(pb[:, :], pt[:, :], [bpt], [bpb])
        b = npt()
        for c in range(2):
            tr(PT[:, b, c * 128:(c + 1) * 128], pb[:, c * 128:(c + 1) * 128], [bpb], [bPTs[b]])
        cp(pT_t[:, :, :], PT[:, b, 0:256].rearrange("p (a b) -> p a b", a=2), [bPTs[b]], [bpT_t], eng="act")
        yield
        act(T[4][:, 0:512], PS[:, bg, :], AF.Tanh, [bPS[bg]], [bT[4]], scale=0.5)
        act(T[4][:, 512:1024], PS[:, bag, :], AF.Tanh, [bPS[bag]], [bT[4]], scale=0.5)
        yield
        ts(T[4][:, 0:512], T[4][:, 0:512], 0.5, 0.5, ALU.mult, ALU.add, [bT[4]], [bT[4]])
        tt(T[1][:, 512:1024], PS[:, bg, :], T[4][:, 0:512], ALU.mult, [bPS[bg], bT[4]], [bT[1]])
        yield
        ts(T[4][:, 512:1024], T[4][:, 512:1024], 0.5, 0.5, ALU.mult, ALU.add, [bT[4]], [bT[4]])
        aTd = T[3][0:64, :].rearrange("p (h q) -> p h q", h=8)
        for r in range(2):
            prow = slice(r * 64, (r + 1) * 64)
            tt(aTd[:, r::2, :], PS[prow, bag, :].rearrange("p (c q) -> p c q", c=4),
               T[4][prow, 512:1024].rearrange("p (c q) -> p c q", c=4), ALU.mult,
               [bPS[bag], bT[4]], [bT[3]])
        yield

        def q_chain():
            yield from norm_rope_g(T[0][:, 0:512], bT[0], 8, qn, bqn,
                                   qr[:, :, :, :].rearrange("p j g d -> p g j d"), bqr,
                                   T[2][:, :], bT[2], gj=(2, 4))
            b = npt()
            for j in range(4):
                tr(PT[:, b, j * 128:(j + 1) * 128], qr[:, j, :, :].rearrange("p g d -> p (g d)"), [bqr],
                   [bPTs[b]])
            yield
            for g in range(2):
                prow = slice(g * 64, (g + 1) * 64)
                cp(QTz[prow, g, :, :], PT[prow, b, 0:512].rearrange("p (j q) -> p j q", j=4),
                   [bPTs[b]], [bQTz])
            yield

        def gla_chain():
            dma(T[0][:, 512:1024], obwd[rows, :], [bob], [bT[0]])
            yield from gla_full_g(0, Sf, bSf, T[4], bT[4], T[1][:, 256:512], bT[1], T[1][:, 0:256], bT[1],
                                  vb, bvb, lrT[0:17, 0, :], blrT)
            osum = T[0][:, 512:1024]
            tt(osum, PS[:, 4, :], osum, ALU.add, [bPS[4], bT[0]], [bT[0]])
            yield
            tt(T[4][:, 0:512], osum, osum, ALU.mult, [bT[0]], [bT[4]])
            red(st4[:, 0:4], T[4][:, 0:512].rearrange("p (h d) -> p h d", h=4), [bT[4]], [bst4])
            yield
            rstd_from_ss(st4[:, 0:4], st5[:, 0:4], 128.0, bst4, bst5)
            yield
            o3 = osum.rearrange("p (h d) -> p h d", h=4)
            tt(o3, o3, st5[:, 0:4].unsqueeze(2).to_broadcast([128, 4, 128]), ALU.mult, [bT[0], bst5], [bT[0]])
            tt(o3, o3, gn[:, :].unsqueeze(1).to_broadcast([128, 4, 128]), ALU.mult, [bT[0], bgn], [bT[0]])
            tt(mg[:, :], osum, T[1][:, 512:1024], ALU.mult, [bT[0], bT[1]], [bmg])
            yield
            b = npt()
            for c in range(4):
                tr(PT[:, b, c * 128:(c + 1) * 128], mg[:, c * 128:(c + 1) * 128], [bmg], [bPTs[b]])
            cp(mixT_t[:, 4:8, :], PT[:, b, 0:512].rearrange("p (a b) -> p a b", a=4), [bPTs[b]], [bmixT_t],
               eng="act")
            yield

        yield from interleave_g(q_chain(), gla_chain())

    def b2_att(seg, t, par):
        nkb = segs[seg][0]
        mixT_t, bmixT_t = ((mixT, bmixT), (mixTb, bmixTb))[par]
        aTd = T[3][0:64, :].rearrange("p (h q) -> p h q", h=8)

        def s_mm(kb):
            sb_ = kb % 2
            for g in range(2):
                mm(PS[:, 2 * sb_ + g, :], KT[:, kb * 128:(kb + 1) * 128],
                   QTz[:, g, :, :].rearrange("p j q -> p (j q)"), True, True,
                   [bKT, bQTz], [bPS[2 * sb_ + g]])
        s_mm(0)
        if nkb > 1:
            s_mm(1)
        for kb in range(nkb):
            sb_ = kb % 2
            act(PTb[sb_][:, :, :], PS[:, 2 * sb_:2 * sb_ + 2, :], AF.Exp,
                [bPS[2 * sb_], bPS[2 * sb_ + 1]], [bPTb[sb_]], scale=0.125)
            for g in range(2):
                mm(PS[0:65, 4 + g, :], VO[:, kb, g, 0:65], PTb[sb_][:, g, :], kb == 0, kb == nkb - 1,
                   [bVO, bPTb[sb_]], [bPS[4 + g]])
            if kb + 2 < nkb:
                s_mm(kb + 2)
        Oe = T[2]
        cp(Oe[0:65, :].rearrange("p (g n) -> p g n", g=2), PS[0:65, 4:6, :], [bPS[4], bPS[5]], [bT[2]],
           eng="act")
        act(Oe[64:65, :], Oe[64:65, :], AF.Ln, [bT[2]], [bT[2]])
        act(Oe[64:65, :], Oe[64:65, :], AF.Exp, [bT[2]], [bT[2]], scale=-1.0)
        for g in range(2):
            bb = g
            mm(PS[0:64, bb, :], ones_f[64:65, 0:64], Oe[64:65, g * 512:(g + 1) * 512], True, True,
               [bones_f, bT[2]], [bPS[bb]])
            tt(Oe[0:64, g * 512:(g + 1) * 512], Oe[0:64, g * 512:(g + 1) * 512], PS[0:64, bb, :], ALU.mult,
               [bT[2], bPS[bb]], [bT[2]])
            On = Oe[0:64, g * 512:(g + 1) * 512].rearrange("p (j q) -> p j q", j=4)
            for r in range(2):
                tt(mixT_t[r * 64:(r + 1) * 64, 2 * g:2 * g + 2, :], On[:, r::2, :],
                   aTd[:, 4 * g + r:4 * g + 4:2, :], ALU.mult, [bT[2], bT[3]], [bmixT_t])

    def b2_post(seg, t, par):
        mixT_t, bmixT_t = ((mixT, bmixT), (mixTb, bmixTb))[par]
        pT_t, bpT_t = ((pT, bpT), (pTb, bpTb))[par]
        rows = slice(t * 128, (t + 1) * 128)
        bh = [nps(), nps()]
        for hh in range(2):
            for c in range(8):
                mm(PS[:, bh[hh], :], mixT_t[:, c, :], Wout[:, c, hh * 512:(hh + 1) * 512], c == 0, c == 7,
                   [bmixT_t, bWout], [bPS[bh[hh]]])
            yield
        for hh in range(2):
            cs_ = slice(hh * 512, (hh + 1) * 512)
            tt(H2[:, cs_], PS[:, bh[hh], :], H2[:, cs_], ALU.add, [bPS[bh[hh]], bH2], [bH2])
        yield
        stt(G2[:, :], H2[:, :], 1.0, H2[:, :], ALU.mult, ALU.mult, [bH2], [bG2, bstP], accum=stP[:, 0:1])
        yield
        rstd_from_ss(stP[:, 0:1], stP[:, 1:2], float(D), bstP, bstP)
        yield
        act(vbA[0][:, :], H2[:, 0:512], AF.Copy, [bH2, bstP], [bvbA[0]], scale=stP[:, 1:2])
        ts(vbA[1][:, :], H2[:, 512:1024], stP[:, 1:2], None, ALU.mult, None, [bH2, bstP], [bvbA[1]])
        yield
        b = npt()
        for c in range(8):
            tr(PT[:, b, c * 128:(c + 1) * 128], vbA[c // 4][:, (c % 4) * 128:(c % 4 + 1) * 128], [bvbA[c // 4]],
               [bPTs[b]])
        yield
        cp(HT[:, :, :], PT[:, b, :].rearrange("p (a b) -> p a b", a=8), [bPTs[b]], [bHT], eng="act")
        yield
        for hh in range(2):
            cs_ = slice(hh * 512, (hh + 1) * 512)
            bgp = nps()
            for c in range(8):
                mm(PS[:, bgp, :], HT[:, c, :], Wpg[:, c, cs_], c == 0, c == 7, [bHT, bWpg], [bPS[bgp]])
            yield
            bpp = nps()
            for c in range(2):
                mm(PS[:, bpp, :], pT_t[:, c, :], Wpp[:, c, cs_], c == 0, c == 1, [bpT_t, bWpp], [bPS[bpp]])
            act(G2[:, cs_], PS[:, bgp, :], AF.Tanh, [bPS[bgp]], [bG2], scale=0.5)
            yield
            ts(G2[:, cs_], G2[:, cs_], 0.5, 0.5, ALU.mult, ALU.add, [bG2], [bG2])
            tt(G2[:, cs_], G2[:, cs_], PS[:, bpp, :], ALU.mult, [bG2, bPS[bpp]], [bG2])
            yield
            tt(H2[:, cs_], G2[:, cs_], H2[:, cs_], ALU.add, [bG2, bH2], [bH2])
            yield
        stt(G2[:, :], H2[:, :], 1.0, H2[:, :], ALU.mult, ALU.mult, [bH2], [bG2, bstP], accum=stP[:, 2:3])
        yield
        rstd_from_ss(stP[:, 2:3], stP[:, 3:4], float(D), bstP, bstP)
        yield
        stt(H2[:, :], H2[:, :], stP[:, 3:4], FNt[:, :], ALU.mult, ALU.mult, [bH2, bstP, bFNt], [bH2])
        dma(y_out[rows, :], H2[:, :], [bH2], [Buf("yout")])
        yield

    def phase_b2(seg):
        t0 = sum(b for _, b in segs[:seg])
        n_own = segs[seg][1]
        interleave(b2_pre(seg, t0, 0, True))
        for i in range(n_own):
            t = t0 + i
            rows = slice(t * 128, (t + 1) * 128)
            dma(H2[:, :], x_own[rows, :], [], [bH2])
            if i + 1 < n_own:
                dma(xt[:, :], x_own[(t + 1) * 128:(t + 2) * 128, :], [], [bxt])
            b2_att(seg, t, i % 2)
            nxt = b2_pre(seg, t + 1, (i + 1) % 2, False) if i + 1 < n_own else None
            interleave(b2_post(seg, t, i % 2), nxt)

    for seg in range(len(segs)):
        phase_a(seg)
        phase_b1(seg)
        phase_b2(seg)

    S.emit(nc, st)
    st.close()
    return nc


def _rope_tab(pos):
    half = 32
    inv = (10000.0 ** (-np.arange(0, half, 2, dtype=np.float32) / half)).astype(np.float32)
    row = (pos // 64).astype(np.float32)
    col = (pos % 64).astype(np.float32)
    ar = row[:, None] * inv[None, :]
    ac = col[:, None] * inv[None, :]
    cr, sr, cc, sc = np.cos(ar), np.sin(ar), np.cos(ac), np.sin(ac)
    cos = np.concatenate([cr, cr, cc, cc], axis=1)
    sin = np.concatenate([-sr, sr, -sc, sc], axis=1)
    return np.concatenate([cos, sin], axis=1).astype(np.float32)


_NC_CACHE = {}


def kernel(x_prompt, x_sample, p_prompt, p_sample, mix_norm, w_in, q_norm, k_norm,
           w_gate_up_fwd, b_gate_fwd, w_gate_up_bwd, b_gate_bwd, gla_norm, w_out,
           ple_norm, w_ple_gate, w_ple_proj, final_norm):
    f32 = np.float32
    xp = np.asarray(x_prompt, f32)[0]
    xsm = np.asarray(x_sample, f32)
    pp = np.asarray(p_prompt, f32)[0, 0]
    psm = np.asarray(p_sample, f32)[0]
    tab_p = _rope_tab(np.arange(16384))
    tab_s = _rope_tab(np.arange(4096))
    rep = lambda v, n: np.ascontiguousarray(np.broadcast_to(np.asarray(v, f32).reshape(1, -1), (128, n)))
    tri = np.zeros((128, 4, 128), f32)
    s_, t_ = np.meshgrid(np.arange(128), np.arange(128), indexing="ij")
    tri[:, 0] = (s_ <= t_)
    tri[:, 1] = (s_ > t_)
    tri[:, 2] = (s_ >= t_)
    tri[:, 3] = (s_ < t_)
    common = {
        "w_in": np.ascontiguousarray(np.asarray(w_in, f32)[0]),
        "w_out": np.ascontiguousarray(np.asarray(w_out, f32)[0]),
        "w_pg": np.ascontiguousarray(np.asarray(w_ple_gate, f32)[0]),
        "w_pp": np.ascontiguousarray(np.asarray(w_ple_proj, f32)[0]),
        "wup_f": np.ascontiguousarray(np.asarray(w_gate_up_fwd, f32)[0]),
        "wup_b": np.ascontiguousarray(np.asarray(w_gate_up_bwd, f32)[0]),
        "bg_f": np.ascontiguousarray(np.asarray(b_gate_fwd, f32)[0].reshape(1, 256)),
        "bg_b": np.ascontiguousarray(np.asarray(b_gate_bwd, f32)[0].reshape(1, 256)),
        "mixn": np.ascontiguousarray(np.asarray(mix_norm, f32)[0].reshape(8, 128).T),
        "plen": np.ascontiguousarray(np.asarray(ple_norm, f32)[0].reshape(8, 128).T),
        "qn": rep(np.asarray(q_norm)[0], 64),
        "kn": rep(np.asarray(k_norm)[0], 64),
        "gn": rep(np.asarray(gla_norm)[0], 128),
        "fn": rep(np.asarray(final_norm), 1024),
        "ident": np.eye(128, dtype=f32).astype(ml_dtypes.bfloat16),
        "tri": np.ascontiguousarray(tri.reshape(128, 512)),
    }
    in_maps = []
    for c in range(NCORES):
        sq, hf = c // 2, c % 2
        own_p = slice(2048 * c, 2048 * (c + 1))
        own_s = slice(2048 * hf, 2048 * (hf + 1))
        m = np.zeros((NT_CTX, 2), f32)
        m[0:16 * c, 0] = 1.0
        m[16 * (c + 1):128, 1] = 1.0
        if hf == 1:
            m[128:144, 0] = 1.0
        else:
            m[144:160, 1] = 1.0
        d = dict(common)
        d["x_all"] = np.ascontiguousarray(np.concatenate([xp, xsm[sq]], axis=0))
        d["x_own"] = np.ascontiguousarray(np.concatenate([xp[own_p], xsm[sq][own_s]], axis=0))
        d["p_own"] = np.ascontiguousarray(np.concatenate([pp[own_p], psm[sq][own_s]], axis=0))
        d["tab_all"] = np.ascontiguousarray(np.concatenate([tab_p, tab_s], axis=0))
        d["tab_own"] = np.ascontiguousarray(np.concatenate([tab_p[own_p], tab_s[own_s]], axis=0))
        d["masks"] = np.ascontiguousarray(np.broadcast_to(m.reshape(1, -1), (128, NT_CTX * 2)))
        in_maps.append(d)
    if "nc" not in _NC_CACHE:
        _NC_CACHE["nc"] = build_program()
    res = run_bass_kernel_spmd(_NC_CACHE["nc"], in_maps, core_ids=list(range(NCORES)))
    ys = [np.asarray(r["y"], f32) for r in res.results]
    y_prompt = np.concatenate([y[0:2048] for y in ys], axis=0)[None]
    y_sample = np.stack([np.concatenate([ys[2 * s][2048:], ys[2 * s + 1][2048:]], axis=0) for s in range(4)])
    return (y_prompt.astype(f32), y_sample.astype(f32))
```

```python
import sys
import numpy as np
import ml_dtypes
from contextlib import ExitStack
import concourse.bass as bass
import concourse.mybir as mybir
from concourse.bass_utils import run_bass_kernel_spmd

F32 = mybir.dt.float32
BF16 = mybir.dt.bfloat16
ALU = mybir.AluOpType
AF = mybir.ActivationFunctionType
AX = mybir.AxisListType

NCORES = 8
D = 1024
TOK_OWN = 4096
NT_OWN = 32
NT_CTX = 160
EPS = 1e-6
ND = 8
MAX_OPS = None
WARMUP_MM = 32
DMA_SCRATCH = 512


class Buf:
    __slots__ = ("name", "lw", "rd", "excl")

    def __init__(self, name, excl=False):
        self.name = name
        self.excl = excl
        self.lw = None
        self.rd = []


class Op:
    __slots__ = ("eng", "fn", "deps", "signal", "sigval", "dma", "dma_idx", "line")


class Sched:
    ENGS = ("sp", "pe", "dve", "act", "pool")

    def __init__(self):
        self.ops = []
        self.ndma = 0

    def add(self, eng, fn, R=(), W=(), dma=False):
        idx = len(self.ops)
        xr = [b for b in R if b.excl]
        if xr:
            R = [b for b in R if not b.excl]
            W = list(W) + [b for b in xr if b not in W]
        raw, other = set(), set()
        for b in R:
            if b.lw is not None:
                raw.add(b.lw)
        for b in W:
            if b.lw is not None:
                other.add(b.lw)
            for r in b.rd:
                other.add(r)
        deps = set()
        for d in raw | other:
            p = self.ops[d]
            if p.dma or dma:
                deps.add(d)
            elif p.eng != eng:
                deps.add(d)
            elif eng != "pe":
                deps.add(d)
        op = Op()
        op.eng, op.fn, op.deps, op.signal, op.sigval, op.dma = eng, fn, deps, False, 0, dma
        op.dma_idx = -1
        op.line = sys._getframe(2).f_lineno
        if dma:
            op.dma_idx = self.ndma
            self.ndma += 1
        self.ops.append(op)
        for b in R:
            b.rd.append(idx)
        for b in W:
            b.lw = idx
            b.rd = []
        return idx

    def emit(self, nc, stack):
        if MAX_OPS is not None:
            self.ops = self.ops[:MAX_OPS]
            self.ndma = sum(1 for o in self.ops if o.dma)
        ops = self.ops
        for op in ops:
            best = {}
            keep = set()
            for d in op.deps:
                p = ops[d]
                if p.dma:
                    keep.add(d)
                else:
                    if p.eng not in best or best[p.eng] < d:
                        best[p.eng] = d
            keep |= set(best.values())
            op.deps = keep
            for d in keep:
                ops[d].signal = True
        cnt = {e: 0 for e in self.ENGS}
        for op in ops:
            if op.dma:
                op.signal = True
            elif op.signal:
                cnt[op.eng] += 1
                op.sigval = cnt[op.eng]
        sems = {e: stack.enter_context(nc.semaphore("s_" + e)) for e in self.ENGS}
        dsems = [stack.enter_context(nc.semaphore("d%d" % i)) for i in range(ND)]
        block = stack.enter_context(nc.Block())
        ndma = self.ndma

        def run(engname):
            def body(e):
                waited = {}

                def wait(sem, val):
                    k = id(sem)
                    if waited.get(k, 0) < val:
                        e.wait_ge(sem, val)
                        waited[k] = val

                for op in ops:
                    if op.eng != engname:
                        continue
                    for d in sorted(op.deps):
                        p = ops[d]
                        if p.dma:
                            wait(dsems[p.dma_idx % ND], 16 * (p.dma_idx // ND + 1))
                        else:
                            wait(sems[p.eng], p.sigval)
                    if op.dma:
                        j = op.dma_idx
                        if j >= ND:
                            wait(dsems[j % ND], 16 * (j // ND))
                        op.fn(e).then_inc(dsems[j % ND], 16)
                    else:
                        ins = op.fn(e)
                        if op.signal:
                            ins.then_inc(sems[engname], 1)
                if engname == "sp":
                    for i in range(ND):
                        n = (ndma - i + ND - 1) // ND if ndma > i else 0
                        if n > 0:
                            wait(dsems[i], 16 * n)
            return body

        block.sync(run("sp"))
        block.tensor(run("pe"))
        block.vector(run("dve"))
        block.scalar(run("act"))
        block.gpsimd(run("pool"))


def build_program(segs=((128, 16), (32, 16))):
    NT_CTX = sum(a for a, _ in segs)
    NT_OWN = sum(b for _, b in segs)
    TOK_OWN = NT_OWN * 128
    MAXC = max(a for a, _ in segs)
    nc = bass.Bass("TRN2", target_bir_lowering=False, dynamic_dma_scratch_size=DMA_SCRATCH)
    S = Sched()
    st = ExitStack()

    def din(name, shape, dt=F32):
        return nc.dram_tensor(name, list(shape), dt, kind="ExternalInput").ap()

    x_all = din("x_all", [NT_CTX * 128, D])
    x_own = din("x_own", [TOK_OWN, D])
    p_own = din("p_own", [TOK_OWN, 256])
    tab_all = din("tab_all", [NT_CTX * 128, 128])
    tab_own = din("tab_own", [TOK_OWN, 128])
    masks_d = din("masks", [128, NT_CTX * 2])
    w_in = din("w_in", [D, 2848])
    w_out = din("w_out", [D, D])
    w_pg = din("w_pg", [D, D])
    w_pp = din("w_pp", [256, D])
    wup_d = [din("wup_f", [16, 256]), din("wup_b", [16, 256])]
    bg_d = [din("bg_f", [1, 256]), din("bg_b", [1, 256])]
    mixn_d = din("mixn", [128, 8])
    plen_d = din("plen", [128, 8])
    qn_d = din("qn", [128, 64])
    kn_d = din("kn", [128, 64])
    gn_d = din("gn", [128, 128])
    fn_d = din("fn", [128, D])
    ident_d = din("ident", [128, 128], BF16)
    tri_d = din("tri", [128, 4 * 128])
    y_out = nc.dram_tensor("y", [TOK_OWN, D], F32, kind="ExternalOutput").ap()
    obwd = nc.dram_tensor("obwd", [TOK_OWN, 512], F32)

    bufs = {}

    def sb(name, shape, dt=F32):
        t = st.enter_context(nc.sbuf_tensor("sb_" + name, list(shape), dt))
        bufs[name] = Buf(name)
        return t, bufs[name]

    Win, bWin = sb("Win", [128, 8, 2848], BF16)
    Wout, bWout = sb("Wout", [128, 8, D], BF16)
    Wpg, bWpg = sb("Wpg", [128, 8, D], BF16)
    Wpp, bWpp = sb("Wpp", [128, 2, D], BF16)
    Wup, bWup = sb("Wup", [17, 2, 256], BF16)
    ident, bident = sb("ident", [128, 128], BF16)
    tri, btri = sb("tri", [128, 4, 128], F32)
    ones_bf, bones_bf = sb("ones_bf", [128, 128], BF16)
    ones_f, bones_f = sb("ones_f", [128, 64], F32)
    qn, bqn = sb("qn", [128, 64])
    kn, bkn = sb("kn", [128, 64])
    gn, bgn = sb("gn", [128, 128])
    masks, bmasks = sb("masks", [128, NT_CTX * 2])
    mixn, bmixn = sb("mixn", [128, 8])
    plen, bplen = sb("plen", [128, 8])
    KT, bKT = sb("KT", [128, MAXC * 128], BF16)
    VO, bVO = sb("VO", [128, MAXC, 2, 66], BF16)
    T = []
    bT = []
    for i in range(5):
        t, b = sb("T%d" % i, [128, 1024], F32)
        T.append(t)
        bT.append(b)
    xt, bxt = sb("xt", [128, D])
    xs, bxs = sb("xs", [128, D], BF16)
    xT, bxT = sb("xT", [128, 8, 128], BF16)
    tab, btab = sb("tab", [128, 128])
    pt, bpt = sb("pt", [128, 256])
    pb, bpb = sb("pb", [128, 256], BF16)
    pT, bpT = sb("pT", [128, 2, 128], BF16)
    kr, bkr = sb("kr", [128, 128], BF16)
    lkA, blkA, vbA, bvbA, lrTA, blrTA = [], [], [], [], [], []
    for i in range(2):
        t, b = sb("lkA%d" % i, [128, 256]); lkA.append(t); blkA.append(b)
        t, b = sb("vbA%d" % i, [128, 512], BF16); vbA.append(t); bvbA.append(b)
        t, b = sb("lrTA%d" % i, [17, 2, 128], BF16); lrTA.append(t); blrTA.append(b)
    dcy2, bdcy2, acf2, bacf2 = [], [], [], []
    for i in range(2):
        t, b = sb("dcyA%d" % i, [128, 2]); dcy2.append(t); bdcy2.append(b)
        t, b = sb("acfA%d" % i, [128, 2]); acf2.append(t); bacf2.append(b)
    vb, bvb = sb("vb", [128, 512], BF16)
    lrT, blrT = sb("lrT", [17, 2, 128], BF16)
    gq, bgq = sb("gq", [128, 3, 256], BF16)
    qkT, bqkT = sb("qkT", [128, 4, 128], BF16)
    AT, bAT = sb("AT", [128, 4, 128], BF16)
    Sbf, bSbf = sb("Sbf", [128, 2, 2, 128], BF16)
    qdz, bqdz = sb("qdz", [128, 2, 2, 128], BF16)
    Sf, bSf = sb("Sf", [128, 2, 128])
    Lb, bLb = sb("Lb", [128, 2, 128])
    Pb, bPb = sb("Pb", [128, 2])
    dcy, bdcy = sb("dcy", [128, 2])
    acf, bacf = sb("acf", [128, 2])
    st1, bst1 = sb("st1", [128, 8])
    st2, bst2 = sb("st2", [128, 8])
    st3, bst3 = sb("st3", [128, 8])
    st4, bst4 = sb("st4", [128, 8])
    stA, bstA = [], []
    for i in range(2):
        t, b = sb("stA%d" % i, [128, 2]); stA.append(t); bstA.append(b)
    st5, bst5 = sb("st5", [128, 8])
    mg, bmg = sb("mg", [128, 512], BF16)
    mixT, bmixT = sb("mixT", [128, 8, 128], BF16)
    qr, bqr = sb("qr", [128, 4, 2, 64], BF16)
    QTz, bQTz = sb("QTz", [128, 2, 4, 128], BF16)
    H2, bH2 = sb("H2", [128, D])
    FNt, bFNt = sb("FNt", [128, D])
    G2, bG2 = sb("G2", [128, D])
    HT, bHT = sb("HT", [128, 8, 128], BF16)
    mixTb, bmixTb = sb("mixTb", [128, 8, 128], BF16)
    pTb, bpTb = sb("pTb", [128, 2, 128], BF16)
    stP, bstP = sb("stP", [128, 8])
    PTb = []
    bPTb = []
    for i in range(2):
        t, b = sb("PTb%d" % i, [128, 2, 512], BF16)
        PTb.append(t)
        bPTb.append(b)

    PS = st.enter_context(nc.psum_tensor("PS", [128, 6, 512], F32))
    PT = st.enter_context(nc.psum_tensor("PT", [128, 2, 1024], BF16))
    bPS = [Buf("PS%d" % i, True) for i in range(6)]
    bPTs = [Buf("PT%d" % i, True) for i in range(2)]
    rr = {"ps": 0, "pt": 0, "dq": 0}

    def nps():
        i = rr["ps"]
        rr["ps"] = (i + 1) % 6
        return i

    def npt():
        i = rr["pt"]
        rr["pt"] = (i + 1) % 2
        return i

    def dma(out, in_, R, W):
        S.add("sp", lambda e, o=out, i=in_: e.dma_start(out=o, in_=i), R, W, dma=True)

    def mm(out, lhsT, rhs, start, stop, R, W):
        S.add("pe", lambda e, o=out, l=lhsT, r=rhs, s0=start, s1=stop:
              e.matmul(o, lhsT=l, rhs=r, start=s0, stop=s1), R, W)

    def tr(out, in_, R, W):
        S.add("pe", lambda e, o=out, i=in_: e.transpose(o, i, ident[:, :]), list(R) + [bident], W)

    def act(out, in_, func, R, W, scale=1.0, bias=0.0):
        S.add("act", lambda e, o=out, i=in_, f=func, s=scale, b=bias:
              e.activation(out=o, in_=i, func=f, bias=b, scale=s), R, W)

    def tt(out, in0, in1, op, R, W, eng="dve"):
        S.add(eng, lambda e, o=out, a=in0, b=in1, p=op: e.tensor_tensor(out=o, in0=a, in1=b, op=p), R, W)

    def ts(out, in0, s1, s2, op0, op1, R, W, eng="dve"):
        if s2 is None:
            S.add(eng, lambda e, o=out, a=in0, x=s1, p0=op0:
                  e.tensor_scalar(out=o, in0=a, scalar1=x, scalar2=None, op0=p0), R, W)
        else:
            S.add(eng, lambda e, o=out, a=in0, x=s1, y=s2, p0=op0, p1=op1:
                  e.tensor_scalar(out=o, in0=a, scalar1=x, scalar2=y, op0=p0, op1=p1), R, W)

    def stt(out, in0, scalar, in1, op0, op1, R, W, accum=None):
        if accum is None:
            S.add("dve", lambda e, o=out, a=in0, s=scalar, b=in1, p0=op0, p1=op1:
                  e.scalar_tensor_tensor(out=o, in0=a, scalar=s, in1=b, op0=p0, op1=p1), R, W)
        else:
            S.add("dve", lambda e, o=out, a=in0, s=scalar, b=in1, p0=op0, p1=op1, ac=accum:
                  e.scalar_tensor_tensor(out=o, in0=a, scalar=s, in1=b, op0=p0, op1=p1, accum_out=ac), R, W)

    def cp(out, in_, R, W, eng="dve"):
        if eng == "act":
            S.add("act", lambda e, o=out, i=in_: e.copy(out=o, in_=i), R, W)
        else:
            S.add(eng, lambda e, o=out, i=in_: e.tensor_copy(out=o, in_=i), R, W)

    def red(out, in_, R, W):
        S.add("dve", lambda e, o=out, i=in_: e.tensor_reduce(out=o, in_=i, axis=AX.X, op=ALU.add), R, W)

    def recip(out, in_, R, W):
        S.add("dve", lambda e, o=out, i=in_: e.reciprocal(out=o, in_=i), R, W)

    def memset(ap, val, W, eng="dve"):
        S.add(eng, lambda e, a=ap, v=val: e.memset(a, v), (), W)

    def rstd_from_ss(ss_ap, out_ap, n, bss, bout):
        act(out_ap, ss_ap, AF.Ln, [bss], [bout], scale=1.0 / n, bias=EPS)
        act(out_ap, out_ap, AF.Exp, [bout], [bout], scale=-0.5)

    dma(ident[:, :], ident_d[:, :], [], [bident])
    dma(tri[:, :, :], tri_d.rearrange("p (a b) -> p a b", a=4), [], [btri])
    dma(qn[:, :], qn_d[:, :], [], [bqn])
    dma(kn[:, :], kn_d[:, :], [], [bkn])
    dma(gn[:, :], gn_d[:, :], [], [bgn])
    dma(masks[:, :], masks_d[:, :], [], [bmasks])
    dma(mixn[:, :], mixn_d[:, :], [], [bmixn])
    dma(plen[:, :], plen_d[:, :], [], [bplen])
    dma(FNt[:, :], fn_d[:, :], [], [bFNt])
    memset(ones_bf[:, :], 1.0, [bones_bf])
    memset(ones_f[:, :], 1.0, [bones_f])
    memset(VO[:, :, :, 64:66], 1.0, [bVO])
    memset(QTz[:, :, :, :], 0.0, [bQTz])
    memset(Sbf[:, :, :, :], 0.0, [bSbf])
    memset(qdz[:, :, :, :], 0.0, [bqdz])

    k = 0
    for kc in range(8):
        for (c0, c1) in ((0, 1024), (1024, 2048), (2048, 2848)):
            n = c1 - c0
            ti = k % 2
            k += 1
            dma(T[ti][:, 0:n], w_in[kc * 128:(kc + 1) * 128, c0:c1], [], [bT[ti]])
            ts(Win[:, kc, c0:c1], T[ti][:, 0:n], mixn[:, kc:kc + 1], None, ALU.mult, None,
               [bT[ti], bmixn], [bWin])
    for kc in range(8):
        ti = k % 2
        k += 1
        dma(T[ti][:, :], w_pg[kc * 128:(kc + 1) * 128, :], [], [bT[ti]])
        ts(Wpg[:, kc, :], T[ti][:, :], plen[:, kc:kc + 1], None, ALU.mult, None, [bT[ti], bplen], [bWpg])
    for kc in range(8):
        ti = k % 2
        k += 1
        dma(T[ti][:, :], w_out[kc * 128:(kc + 1) * 128, :], [], [bT[ti]])
        cp(Wout[:, kc, :], T[ti][:, :], [bT[ti]], [bWout])
    for kc in range(2):
        ti = k % 2
        k += 1
        dma(T[ti][:, :], w_pp[kc * 128:(kc + 1) * 128, :], [], [bT[ti]])
        cp(Wpp[:, kc, :], T[ti][:, :], [bT[ti]], [bWpp])
    for d in range(2):
        dma(T[2][0:16, d * 256:(d + 1) * 256], wup_d[d][:, :], [], [bT[2]])
        dma(T[2][16:17, d * 256:(d + 1) * 256], bg_d[d][:, :], [], [bT[2]])
    cp(Wup[:, :, :], T[2][0:17, 0:512].rearrange("p (a b) -> p a b", a=2), [bT[2]], [bWup])
    memset(lrT[:, :, :], 1.0, [blrT])
    for i in range(2):
        memset(lrTA[i][:, :, :], 1.0, [blrTA[i]])

    def front_g(x_rows_ap, load=True):
        if load:
            dma(xt[:, :], x_rows_ap, [], [bxt])
        stt(xs[:, :], xt[:, :], 1.0, xt[:, :], ALU.mult, ALU.mult, [bxt], [bxs, bst1], accum=st1[:, 0:1])
        yield
        rstd_from_ss(st1[:, 0:1], st1[:, 1:2], float(D), bst1, bst1)
        act(xs[:, :], xt[:, :], AF.Copy, [bxt, bst1], [bxs], scale=st1[:, 1:2])
        yield
        transpose8(xs, bxs, xT, bxT)
        yield

    def front(x_rows_ap):
        for _ in front_g(x_rows_ap):
            pass

    def transpose8(src, bsrc, dst, bdst):
        b = npt()
        for c in range(8):
            tr(PT[:, b, c * 128:(c + 1) * 128], src[:, c * 128:(c + 1) * 128], [bsrc], [bPTs[b]])
        cp(dst[:, :, :], PT[:, b, :].rearrange("p (a b) -> p a b", a=8), [bPTs[b]], [bdst], eng="act")

    def proj_tok(bank, off, c0, c1):
        n = c1 - c0
        for kc in range(8):
            mm(PS[:, bank, off:off + n], xT[:, kc, :], Win[:, kc, c0:c1], kc == 0, kc == 7,
               [bxT, bWin], [bPS[bank]])

    def proj_feat(bank, off, c0, c1, p0=0):
        m = c1 - c0
        for kc in range(8):
            mm(PS[p0:p0 + m, bank, off:off + 128], Win[:, kc, c0:c1], xT[:, kc, :], kc == 0, kc == 7,
               [bxT, bWin], [bPS[bank]])

    def norm_rope_g(src, bsrc, nh, wn, bwn, dst, bdst, tmp, btmp, gj=None, tabs=None):
        n = nh * 64
        tab_t, btab_t = tabs if tabs is not None else (tab, btab)
        sq = tmp[:, 0:n]
        yv = tmp[:, n:2 * n]
        tt(sq, src, src, ALU.mult, [bsrc], [btmp])
        red(st2[:, 0:nh], sq.rearrange("p (h d) -> p h d", h=nh), [btmp], [bst2])
        yield
        rstd_from_ss(st2[:, 0:nh], st3[:, 0:nh], 64.0, bst2, bst3)
        yield
        y3 = yv.rearrange("p (h d) -> p h d", h=nh)
        tt(y3, src.rearrange("p (h d) -> p h d", h=nh),
           st3[:, 0:nh].unsqueeze(2).to_broadcast([128, nh, 64]), ALU.mult, [bsrc, bst3], [btmp])
        tt(y3, y3, wn[:, :].unsqueeze(1).to_broadcast([128, nh, 64]), ALU.mult, [btmp, bwn], [btmp])
        yield
        y5 = yv.rearrange("p (h a b c) -> p h a b c", h=nh, a=2, b=2)
        s5 = sq.rearrange("p (h a b c) -> p h a b c", h=nh, a=2, b=2)
        cos4 = tab_t[:, 0:64].rearrange("p (a b c) -> p a b c", a=2, b=2)
        sin4 = tab_t[:, 64:128].rearrange("p (a b c) -> p a b c", a=2, b=2)
        for hf in range(2):
            tt(s5[:, :, :, hf, :], y5[:, :, :, 1 - hf, :],
               sin4[:, :, hf, :].unsqueeze(1).to_broadcast([128, nh, 2, 16]), ALU.mult,
               [btmp, btab_t], [btmp])
        tt(y3, y3, tab_t[:, 0:64].unsqueeze(1).to_broadcast([128, nh, 64]), ALU.mult, [btmp, btab_t], [btmp])
        if gj is None:
            tt(dst, y3, sq.rearrange("p (h d) -> p h d", h=nh), ALU.add, [btmp], [bdst])
        else:
            g_, j_ = gj
            tt(dst, yv.rearrange("p (g j d) -> p g j d", g=g_, j=j_),
               sq.rearrange("p (g j d) -> p g j d", g=g_, j=j_), ALU.add, [btmp], [bdst])

    def norm_rope(*a, **k):
        for _ in norm_rope_g(*a, **k):
            pass

    TRI_CS = (0, 2)
    TRI_REM = (1, 3)
    TRI_MASK = (0, 1)

    def kvblk(bk, p, hf):
        return PS[hf * 64:(hf + 1) * 64, bk, p * 256 + hf * 128:p * 256 + hf * 128 + 128]

    def gla_full_g(d, St, bSt, E, bE, lk_ap, blk, lq_ap, blq, vb_t, bvb_t, lr_ap, blr):
        bx, bc, ba = 3, 4, 5
        for r in range(2):
            pr = slice(r * 64, (r + 1) * 64)
            cp(Sbf[pr, :, r, :], St[pr, :, :], [bSt], [bSbf])
        mm(PS[:, bx, 0:256], lr_ap, Wup[0:17, d, :], True, True, [blr, bWup], [bPS[bx]])
        yield
        act(E[:, 0:256], PS[:, bx, 0:256], AF.Exp, [bPS[bx]], [bE], scale=-1.0)
        act(E[:, 0:256], E[:, 0:256], AF.Ln, [bE], [bE], bias=1.0)
        yield
        mm(PS[:, bc, 0:256], tri[:, TRI_CS[d], :], E[:, 0:256], True, True, [btri, bE], [bPS[bc]])
        mm(PS[:, bc, 256:512], tri[:, TRI_REM[d], :], E[:, 0:256], True, True, [btri, bE], [bPS[bc]])
        for p in range(2):
            mm(PS[:, bx, 256 + p:257 + p], E[:, p * 128:(p + 1) * 128], ones_f[:, 0:1], True, True,
               [bE, bones_f], [bPS[bx]])
        yield
        act(E[:, 256:512], PS[:, bc, 0:256], AF.Exp, [bPS[bc]], [bE], scale=-1.0 / 16)
        act(E[:, 768:1024], PS[:, bc, 0:256], AF.Exp, [bPS[bc]], [bE], scale=1.0 / 16)
        act(E[:, 512:768], PS[:, bc, 256:512], AF.Exp, [bPS[bc]], [bE], scale=-1.0 / 16)
        act(dcy[:, 0:2], PS[:, bx, 256:258], AF.Exp, [bPS[bx]], [bdcy], scale=-1.0 / 16)
        yield
        stt(gq[:, 0, :], lq_ap, 0.125, E[:, 256:512], ALU.mult, ALU.mult, [blq, bE], [bgq])
        tt(gq[:, 1, :], lk_ap, E[:, 768:1024], ALU.mult, [blk, bE], [bgq])
        tt(gq[:, 2, :], lk_ap, E[:, 512:768], ALU.mult, [blk, bE], [bgq])
        yield
        b = npt()
        for i in range(2):
            for p in range(2):
                tr(PT[:, b, (i * 2 + p) * 128:(i * 2 + p + 1) * 128], gq[:, i, p * 128:(p + 1) * 128],
                   [bgq], [bPTs[b]])
        bk = bx
        for p in range(2):
            mm(PS[:, bk, p * 256:(p + 1) * 256], gq[:, 2, p * 128:(p + 1) * 128], vb_t[:, p * 256:(p + 1) * 256],
               True, True, [bgq, bvb_t], [bPS[bk]])
        yield
        cp(qkT[:, :, :], PT[:, b, 0:512].rearrange("p (a b) -> p a b", a=4), [bPTs[b]], [bqkT], eng="act")
        for r in range(2):
            pr = slice(r * 64, (r + 1) * 64)
            cp(qdz[pr, :, r, :], PT[pr, b, 0:256].rearrange("p (a b) -> p a b", a=2), [bPTs[b]], [bqdz])
        yield
        for p in range(2):
            mm(PS[:, ba, p * 256:(p + 1) * 256], qkT[:, 2 + p, :],
               qdz[:, p, :, :].rearrange("p r c -> p (r c)"), True, True, [bqkT, bqdz], [bPS[ba]])
        yield
        tt(AT[:, :, :], PS[:, ba, :].rearrange("p (a b) -> p a b", a=4),
           tri[:, TRI_MASK[d], :].unsqueeze(1).to_broadcast([128, 4, 128]), ALU.mult,
           [bPS[ba], btri], [bAT])
        yield
        bo = bc
        for p in range(2):
            mm(PS[:, bo, p * 256:(p + 1) * 256], qkT[:, p, :], Sbf[:, p, :, :].rearrange("p r c -> p (r c)"),
               True, False, [bqkT, bSbf], [bPS[bo]])
            for r in range(2):
                h = 2 * p + r
                mm(PS[:, bo, h * 128:(h + 1) * 128], AT[:, h, :], vb_t[:, h * 128:(h + 1) * 128], False, r == 1,
                   [bAT, bvb_t], [bPS[bo]])
        yield
        for p in range(2):
            for hf in range(2):
                rows = slice(hf * 64, (hf + 1) * 64)
                stt(St[rows, p, :], St[rows, p, :], dcy[rows, p:p + 1], kvblk(bk, p, hf), ALU.mult, ALU.add,
                    [bSt, bdcy, bPS[bk]], [bSt])
        yield

    def a_pre(seg, n, par):
        xt2, bxt2 = ((xt, bxt), (T[4], bT[4]))[par]
        rows = slice(n * 128, (n + 1) * 128)
        dma(xt2[:, :], x_all[rows, :], [], [bxt2])
        stt(T[1][:, :], xt2[:, :], 1.0, xt2[:, :], ALU.mult, ALU.mult, [bxt2], [bT[1], bstA[par]],
            accum=stA[par][:, 0:1])
        yield
        rstd_from_ss(stA[par][:, 0:1], stA[par][:, 1:2], float(D), bstA[par], bstA[par])
        yield

    def a_headA(seg, n, par):
        xt2, bxt2 = ((xt, bxt), (T[4], bT[4]))[par]
        tb2, btb2 = ((tab, btab), (pt, bpt))[par]
        rows = slice(n * 128, (n + 1) * 128)
        dma(tb2[:, 0:128], tab_all[rows, :], [], [btb2])
        act(xs[:, :], xt2[:, :], AF.Copy, [bxt2, bstA[par]], [bxs], scale=stA[par][:, 1:2])
        yield
        transpose8(xs, bxs, xT, bxT)
        yield
        ba, bv, bl = 0, 1, 2
        proj_tok(ba, 0, 512, 768)
        proj_tok(ba, 256, 1536, 1792)
        yield
        proj_tok(bv, 0, 1792, 2304)
        yield
        proj_feat(bl, 0, 2816, 2832)
        proj_feat(bl, 128, 2832, 2848)
        yield

    def a_headB(seg, n, par):
        n0 = sum(a for a, _ in segs[:seg])
        j = n - n0
        tb2, btb2 = ((tab, btab), (pt, bpt))[par]
        ba, bv, bl = 0, 1, 2
        cp(T[0][:, 0:128], PS[:, ba, 0:128], [bPS[ba]], [bT[0]], eng="act")
        cp(VO[:, j, :, 0:64], PS[:, ba, 128:256].rearrange("p (g d) -> p g d", g=2), [bPS[ba]], [bVO],
           eng="act")
        cp(lkA[par][:, :], PS[:, ba, 256:512], [bPS[ba]], [blkA[par]])
        yield
        cp(vbA[par][:, :], PS[:, bv, :], [bPS[bv]], [bvbA[par]], eng="act")
        cp(lrTA[par][0:16, :, :], PS[0:16, bl, 0:256].rearrange("p (a b) -> p a b", a=2), [bPS[bl]], [blrTA[par]])
        yield
        yield from norm_rope_g(T[0][:, 0:128], bT[0], 2, kn, bkn, kr[:, :].rearrange("p (h d) -> p h d", h=2),
                               bkr, T[0][:, 256:768], bT[0], tabs=(tb2, btb2))
        yield
        b = npt()
        tr(PT[:, b, 0:128], kr[:, :], [bkr], [bPTs[b]])
        cp(KT[:, j * 128:(j + 1) * 128], PT[:, b, 0:128], [bPTs[b]], [bKT])
        yield

    def a_tail(seg, n, par):
        E2 = (T[2], T[3])
        bE2 = (bT[2], bT[3])
        lk_ap, blk = lkA[par][:, :], blkA[par]
        mcol = [masks[:, 2 * n + d:2 * n + d + 1] for d in range(2)]
        bx = [3, 4]
        bc = 5
        for d in range(2):
            mm(PS[:, bx[d], 0:256], lrTA[par][0:17, d, :], Wup[0:17, d, :], True, True,
               [blrTA[par], bWup], [bPS[bx[d]]])
        yield
        for d in range(2):
            act(E2[d][:, 0:256], PS[:, bx[d], 0:256], AF.Exp, [bPS[bx[d]]], [bE2[d]], scale=-1.0)
        yield
        for d in range(2):
            act(E2[d][:, 0:256], E2[d][:, 0:256], AF.Ln, [bE2[d]], [bE2[d]], bias=1.0)
        yield
        for d in range(2):
            mm(PS[:, bc, d * 256:(d + 1) * 256], tri[:, TRI_REM[d], :], E2[d][:, 0:256], True, True,
               [btri, bE2[d]], [bPS[bc]])
            for p in range(2):
                mm(PS[:, bx[d], 256 + p:257 + p], E2[d][:, p * 128:(p + 1) * 128], ones_f[:, 0:1], True, True,
                   [bE2[d], bones_f], [bPS[bx[d]]])
        yield
        for d in range(2):
            act(E2[d][:, 512:768], PS[:, bc, d * 256:(d + 1) * 256], AF.Exp, [bPS[bc]], [bE2[d]], scale=-1.0 / 16)
            act(dcy2[d][:, 0:2], PS[:, bx[d], 256:258], AF.Exp, [bPS[bx[d]]], [bdcy2[d]], scale=-1.0 / 16)
        yield
        for d in range(2):
            stt(gq[:, d, :], lk_ap, mcol[d], E2[d][:, 512:768], ALU.mult, ALU.mult, [blk, bE2[d], bmasks], [bgq])
            ts(acf2[d][:, :], dcy2[d][:, :], -1.0, mcol[d], ALU.add, ALU.mult, [bdcy2[d], bmasks], [bacf2[d]])
            ts(acf2[d][:, :], acf2[d][:, :], 1.0, None, ALU.add, None, [bacf2[d]], [bacf2[d]])
        yield
        bk = bx
        for d in range(2):
            for p in range(2):
                mm(PS[:, bk[d], p * 256:(p + 1) * 256], gq[:, d, p * 128:(p + 1) * 128],
                   vbA[par][:, p * 256:(p + 1) * 256], True, True, [bgq, bvbA[par]], [bPS[bk[d]]])
        yield
        for d in range(2):
            for p in range(2):
                for hf in range(2):
                    r = slice(hf * 64, (hf + 1) * 64)
                    if d == 0:
                        stt(Sf[r, p, :], Sf[r, p, :], acf2[0][r, p:p + 1], kvblk(bk[0], p, hf), ALU.mult, ALU.add,
                            [bSf, bacf2[0], bPS[bk[0]]], [bSf])
                    else:
                        stt(Lb[r, p, :], kvblk(bk[1], p, hf), Pb[r, p:p + 1], Lb[r, p, :], ALU.mult, ALU.add,
                            [bLb, bPb, bPS[bk[1]]], [bLb])
            yield
        tt(Pb[:, :], Pb[:, :], acf2[1][:, :], ALU.mult, [bPb, bacf2[1]], [bPb])
        yield

    def interleave(*gens):
        gens = [g for g in gens if g is not None]
        while gens:
            for g in list(gens):
                try:
                    next(g)
                except StopIteration:
                    gens.remove(g)

    def phase_a(seg):
        n0 = sum(a for a, _ in segs[:seg])
        n1 = n0 + segs[seg][0]
        memset(Sf[:, :, :], 0.0, [bSf])
        memset(Lb[:, :, :], 0.0, [bLb])
        memset(Pb[:, :], 1.0, [bPb])
        tiles = list(range(n0, n1))
        N_ = len(tiles)

        def G(fn, k):
            return fn(seg, tiles[k], k % 2) if 0 <= k < N_ else None
        for k in range(-3, N_):
            interleave(G(a_headA, k + 2), G(a_headB, k + 1), G(a_tail, k), G(a_pre, k + 3))

    LQK = (T[1], T[4])
    bLQK = (bT[1], bT[4])

    def b1_head(t, par):
        rows = slice(t * 128, (t + 1) * 128)
        yield from front_g(x_own[rows, :])
        ba, bv, bl = 0, 1, 2
        proj_tok(ba, 0, 1280, 1792)
        yield
        proj_tok(bv, 0, 1792, 2304)
        proj_feat(bl, 128, 2832, 2848)
        yield
        cp(LQK[par][:, 0:512], PS[:, ba, :], [bPS[ba]], [bLQK[par]])
        cp(vbA[par][:, :], PS[:, bv, :], [bPS[bv]], [bvbA[par]], eng="act")
        cp(lrTA[par][0:16, 1, :], PS[0:16, bl, 128:256], [bPS[bl]], [blrTA[par]])
        yield

    def b1_tail(t, par):
        rows = slice(t * 128, (t + 1) * 128)
        yield from gla_full_g(1, Lb, bLb, T[2], bT[2], LQK[par][:, 256:512], bLQK[par], LQK[par][:, 0:256],
                              bLQK[par], vbA[par], bvbA[par], lrTA[par][0:17, 1, :], blrTA[par])
        hs_ = slice(par * 512, (par + 1) * 512)
        cp(T[0][:, hs_], PS[:, 4, :], [bPS[4]], [bT[0]])
        dma(obwd[rows, :], T[0][:, hs_], [bT[0]], [bufs.setdefault("obwd", Buf("obwd"))])
        yield

    def phase_b1(seg):
        t0 = sum(b for _, b in segs[:seg])
        tiles = list(range(t0 + segs[seg][1] - 1, t0 - 1, -1))
        interleave(b1_head(tiles[0], 0))
        for i, t in enumerate(tiles):
            h = b1_head(tiles[i + 1], (i + 1) % 2) if i + 1 < len(tiles) else None
            interleave(h, b1_tail(t, i % 2))

    def interleave_g(*gens):
        gens = [g for g in gens if g is not None]
        while gens:
            for g in list(gens):
                try:
                    next(g)
                except StopIteration:
                    gens.remove(g)
            yield

    def b2_pre(seg, t, par, first):
        bob = bufs.setdefault("obwd", Buf("obwd"))
        mixT_t, bmixT_t = ((mixT, bmixT), (mixTb, bmixTb))[par]
        pT_t, bpT_t = ((pT, bpT), (pTb, bpTb))[par]
        rows = slice(t * 128, (t + 1) * 128)
        dma(tab[:, :], tab_own[rows, :], [], [btab])
        dma(pt[:, :], p_own[rows, :], [], [bpt])
        yield from front_g(x_own[rows, :], load=first)
        bq = nps()
        proj_tok(bq, 0, 0, 512)
        yield
        ba = nps()
        proj_tok(ba, 0, 1280, 1792)
        yield
        cp(T[0][:, 0:512], PS[:, bq, :], [bPS[bq]], [bT[0]], eng="act")
        bv = nps()
        proj_tok(bv, 0, 1792, 2304)
        yield
        cp(T[1][:, 0:512], PS[:, ba, :], [bPS[ba]], [bT[1]])
        bl = nps()
        proj_feat(bl, 0, 2816, 2832)
        yield
        cp(vb[:, :], PS[:, bv, :], [bPS[bv]], [bvb], eng="act")
        cp(lrT[0:16, 0, :], PS[0:16, bl, 0:128], [bPS[bl]], [blrT])
        bg = nps()
        proj_tok(bg, 0, 2304, 2816)
        yield
        bag = nps()
        for c in range(4):
            proj_feat(bag, c * 128, 768 + c * 128, 768 + (c + 1) * 128)
        yield
        cp(pb[:, :], pt[:, :], [bpt], [bpb])
        b = npt()
        for c in range(2):
            tr(PT[:, b, c * 128:(c + 1) * 128], pb[:, c * 128:(c + 1) * 128], [bpb], [bPTs[b]])
        cp(pT_t[:, :, :], PT[:, b, 0:256].rearrange("p (a b) -> p a b", a=2), [bPTs[b]], [bpT_t], eng="act")
        yield
        act(T[4][:, 0:512], PS[:, bg, :], AF.Tanh, [bPS[bg]], [bT[4]], scale=0.5)
        act(T[4][:, 512:1024], PS[:, bag, :], AF.Tanh, [bPS[bag]], [bT[4]], scale=0.5)
        yield
        ts(T[4][:, 0:512], T[4][:, 0:512], 0.5, 0.5, ALU.mult, ALU.add, [bT[4]], [bT[4]])
        tt(T[1][:, 512:1024], PS[:, bg, :], T[4][:, 0:512], ALU.mult, [bPS[bg], bT[4]], [bT[1]])
        yield
        ts(T[4][:, 512:1024], T[4][:, 512:1024], 0.5, 0.5, ALU.mult, ALU.add, [bT[4]], [bT[4]])
        aTd = T[3][0:64, :].rearrange("p (h q) -> p h q", h=8)
        for r in range(2):
            prow = slice(r * 64, (r + 1) * 64)
            tt(aTd[:, r::2, :], PS[prow, bag, :].rearrange("p (c q) -> p c q", c=4),
               T[4][prow, 512:1024].rearrange("p (c q) -> p c q", c=4), ALU.mult,
               [bPS[bag], bT[4]], [bT[3]])
        yield

        def q_chain():
            yield from norm_rope_g(T[0][:, 0:512], bT[0], 8, qn, bqn,
                                   qr[:, :, :, :].rearrange("p j g d -> p g j d"), bqr,
                                   T[2][:, :], bT[2], gj=(2, 4))
            b = npt()
            for j in range(4):
                tr(PT[:, b, j * 128:(j + 1) * 128], qr[:, j, :, :].rearrange("p g d -> p (g d)"), [bqr],
                   [bPTs[b]])
            yield
            for g in range(2):
                prow = slice(g * 64, (g + 1) * 64)
                cp(QTz[prow, g, :, :], PT[prow, b, 0:512].rearrange("p (j q) -> p j q", j=4),
                   [bPTs[b]], [bQTz])
            yield

        def gla_chain():
            dma(T[0][:, 512:1024], obwd[rows, :], [bob], [bT[0]])
            yield from gla_full_g(0, Sf, bSf, T[4], bT[4], T[1][:, 256:512], bT[1], T[1][:, 0:256], bT[1],
                                  vb, bvb, lrT[0:17, 0, :], blrT)
            osum = T[0][:, 512:1024]
            tt(osum, PS[:, 4, :], osum, ALU.add, [bPS[4], bT[0]], [bT[0]])
            yield
            tt(T[4][:, 0:512], osum, osum, ALU.mult, [bT[0]], [bT[4]])
            red(st4[:, 0:4], T[4][:, 0:512].rearrange("p (h d) -> p h d", h=4), [bT[4]], [bst4])
            yield
            rstd_from_ss(st4[:, 0:4], st5[:, 0:4], 128.0, bst4, bst5)
            yield
            o3 = osum.rearrange("p (h d) -> p h d", h=4)
            tt(o3, o3, st5[:, 0:4].unsqueeze(2).to_broadcast([128, 4, 128]), ALU.mult, [bT[0], bst5], [bT[0]])
            tt(o3, o3, gn[:, :].unsqueeze(1).to_broadcast([128, 4, 128]), ALU.mult, [bT[0], bgn], [bT[0]])
            tt(mg[:, :], osum, T[1][:, 512:1024], ALU.mult, [bT[0], bT[1]], [bmg])
            yield
            b = npt()
            for c in range(4):
                tr(PT[:, b, c * 128:(c + 1) * 128], mg[:, c * 128:(c + 1) * 128], [bmg], [bPTs[b]])
            cp(mixT_t[:, 4:8, :], PT[:, b, 0:512].rearrange("p (a b) -> p a b", a=4), [bPTs[b]], [bmixT_t],
               eng="act")
            yield

        yield from interleave_g(q_chain(), gla_chain())

    def b2_att(seg, t, par):
        nkb = segs[seg][0]
        mixT_t, bmixT_t = ((mixT, bmixT), (mixTb, bmixTb))[par]
        aTd = T[3][0:64, :].rearrange("p (h q) -> p h q", h=8)

        def s_mm(kb):
            sb_ = kb % 2
            for g in range(2):
                mm(PS[:, 2 * sb_ + g, :], KT[:, kb * 128:(kb + 1) * 128],
                   QTz[:, g, :, :].rearrange("p j q -> p (j q)"), True, True,
                   [bKT, bQTz], [bPS[2 * sb_ + g]])
        s_mm(0)
        if nkb > 1:
            s_mm(1)
        for kb in range(nkb):
            sb_ = kb % 2
            act(PTb[sb_][:, :, :], PS[:, 2 * sb_:2 * sb_ + 2, :], AF.Exp,
                [bPS[2 * sb_], bPS[2 * sb_ + 1]], [bPTb[sb_]], scale=0.125)
            for g in range(2):
                mm(PS[0:65, 4 + g, :], VO[:, kb, g, 0:65], PTb[sb_][:, g, :], kb == 0, kb == nkb - 1,
                   [bVO, bPTb[sb_]], [bPS[4 + g]])
            if kb + 2 < nkb:
                s_mm(kb + 2)
        Oe = T[2]
        cp(Oe[0:65, :].rearrange("p (g n) -> p g n", g=2), PS[0:65, 4:6, :], [bPS[4], bPS[5]], [bT[2]],
           eng="act")
        act(Oe[64:65, :], Oe[64:65, :], AF.Ln, [bT[2]], [bT[2]])
        act(Oe[64:65, :], Oe[64:65, :], AF.Exp, [bT[2]], [bT[2]], scale=-1.0)
        for g in range(2):
            bb = g
            mm(PS[0:64, bb, :], ones_f[64:65, 0:64], Oe[64:65, g * 512:(g + 1) * 512], True, True,
               [bones_f, bT[2]], [bPS[bb]])
            tt(Oe[0:64, g * 512:(g + 1) * 512], Oe[0:64, g * 512:(g + 1) * 512], PS[0:64, bb, :], ALU.mult,
               [bT[2], bPS[bb]], [bT[2]])
            On = Oe[0:64, g * 512:(g + 1) * 512].rearrange("p (j q) -> p j q", j=4)
            for r in range(2):
                tt(mixT_t[r * 64:(r + 1) * 64, 2 * g:2 * g + 2, :], On[:, r::2, :],
                   aTd[:, 4 * g + r:4 * g + 4:2, :], ALU.mult, [bT[2], bT[3]], [bmixT_t])

    def b2_post(seg, t, par):
        mixT_t, bmixT_t = ((mixT, bmixT), (mixTb, bmixTb))[par]
        pT_t, bpT_t = ((pT, bpT), (pTb, bpTb))[par]
        rows = slice(t * 128, (t + 1) * 128)
        bh = [nps(), nps()]
        for hh in range(2):
            for c in range(8):
                mm(PS[:, bh[hh], :], mixT_t[:, c, :], Wout[:, c, hh * 512:(hh + 1) * 512], c == 0, c == 7,
                   [bmixT_t, bWout], [bPS[bh[hh]]])
            yield
        for hh in range(2):
            cs_ = slice(hh * 512, (hh + 1) * 512)
            tt(H2[:, cs_], PS[:, bh[hh], :], H2[:, cs_], ALU.add, [bPS[bh[hh]], bH2], [bH2])
        yield
        stt(G2[:, :], H2[:, :], 1.0, H2[:, :], ALU.mult, ALU.mult, [bH2], [bG2, bstP], accum=stP[:, 0:1])
        yield
        rstd_from_ss(stP[:, 0:1], stP[:, 1:2], float(D), bstP, bstP)
        yield
        act(vbA[0][:, :], H2[:, 0:512], AF.Copy, [bH2, bstP], [bvbA[0]], scale=stP[:, 1:2])
        ts(vbA[1][:, :], H2[:, 512:1024], stP[:, 1:2], None, ALU.mult, None, [bH2, bstP], [bvbA[1]])
        yield
        b = npt()
        for c in range(8):
            tr(PT[:, b, c * 128:(c + 1) * 128], vbA[c // 4][:, (c % 4) * 128:(c % 4 + 1) * 128], [bvbA[c // 4]],
               [bPTs[b]])
        yield
        cp(HT[:, :, :], PT[:, b, :].rearrange("p (a b) -> p a b", a=8), [bPTs[b]], [bHT], eng="act")
        yield
        for hh in range(2):
            cs_ = slice(hh * 512, (hh + 1) * 512)
            bgp = nps()
            for c in range(8):
                mm(PS[:, bgp, :], HT[:, c, :], Wpg[:, c, cs_], c == 0, c == 7, [bHT, bWpg], [bPS[bgp]])
            yield
            bpp = nps()
            for c in range(2):
                mm(PS[:, bpp, :], pT_t[:, c, :], Wpp[:, c, cs_], c == 0, c == 1, [bpT_t, bWpp], [bPS[bpp]])
            act(G2[:, cs_], PS[:, bgp, :], AF.Tanh, [bPS[bgp]], [bG2], scale=0.5)
            yield
            ts(G2[:, cs_], G2[:, cs_], 0.5, 0.5, ALU.mult, ALU.add, [bG2], [bG2])
            tt(G2[:, cs_], G2[:, cs_], PS[:, bpp, :], ALU.mult, [bG2, bPS[bpp]], [bG2])
            yield
            tt(H2[:, cs_], G2[:, cs_], H2[:, cs_], ALU.add, [bG2, bH2], [bH2])
            yield
        stt(G2[:, :], H2[:, :], 1.0, H2[:, :], ALU.mult, ALU.mult, [bH2], [bG2, bstP], accum=stP[:, 2:3])
        yield
        rstd_from_ss(stP[:, 2:3], stP[:, 3:4], float(D), bstP, bstP)
        yield
        stt(H2[:, :], H2[:, :], stP[:, 3:4], FNt[:, :], ALU.mult, ALU.mult, [bH2, bstP, bFNt], [bH2])
        dma(y_out[rows, :], H2[:, :], [bH2], [Buf("yout")])
        yield

    def phase_b2(seg):
        t0 = sum(b for _, b in segs[:seg])
        n_own = segs[seg][1]
        interleave(b2_pre(seg, t0, 0, True))
        for i in range(n_own):
            t = t0 + i
            rows = slice(t * 128, (t + 1) * 128)
            dma(H2[:, :], x_own[rows, :], [], [bH2])
            if i + 1 < n_own:
                dma(xt[:, :], x_own[(t + 1) * 128:(t + 2) * 128, :], [], [bxt])
            b2_att(seg, t, i % 2)
            nxt = b2_pre(seg, t + 1, (i + 1) % 2, False) if i + 1 < n_own else None
            interleave(b2_post(seg, t, i % 2), nxt)

    for _ in range(WARMUP_MM):
        mm(PS[:, 5, :], ident[:, :], Win[:, 0, 0:512], True, True, [bident, bWin], [bPS[5]])

    for seg in range(len(segs)):
        phase_a(seg)
        phase_b1(seg)
        phase_b2(seg)

    S.emit(nc, st)
    st.close()
    return nc


def _rope_tab(pos):
    half = 32
    inv = (10000.0 ** (-np.arange(0, half, 2, dtype=np.float32) / half)).astype(np.float32)
    row = (pos // 64).astype(np.float32)
    col = (pos % 64).astype(np.float32)
    ar = row[:, None] * inv[None, :]
    ac = col[:, None] * inv[None, :]
    cr, sr, cc, sc = np.cos(ar), np.sin(ar), np.cos(ac), np.sin(ac)
    cos = np.concatenate([cr, cr, cc, cc], axis=1)
    sin = np.concatenate([-sr, sr, -sc, sc], axis=1)
    return np.concatenate([cos, sin], axis=1).astype(np.float32)


_NC_CACHE = {}


def kernel(x_prompt, x_sample, p_prompt, p_sample, mix_norm, w_in, q_norm, k_norm,
           w_gate_up_fwd, b_gate_fwd, w_gate_up_bwd, b_gate_bwd, gla_norm, w_out,
           ple_norm, w_ple_gate, w_ple_proj, final_norm):
    f32 = np.float32
    xp = np.asarray(x_prompt, f32)[0]
    xsm = np.asarray(x_sample, f32)
    pp = np.asarray(p_prompt, f32)[0, 0]
    psm = np.asarray(p_sample, f32)[0]
    tab_p = _rope_tab(np.arange(16384))
    tab_s = _rope_tab(np.arange(4096))
    rep = lambda v, n: np.ascontiguousarray(np.broadcast_to(np.asarray(v, f32).reshape(1, -1), (128, n)))
    tri = np.zeros((128, 4, 128), f32)
    s_, t_ = np.meshgrid(np.arange(128), np.arange(128), indexing="ij")
    tri[:, 0] = (s_ <= t_)
    tri[:, 1] = (s_ > t_)
    tri[:, 2] = (s_ >= t_)
    tri[:, 3] = (s_ < t_)
    common = {
        "w_in": np.ascontiguousarray(np.asarray(w_in, f32)[0]),
        "w_out": np.ascontiguousarray(np.asarray(w_out, f32)[0]),
        "w_pg": np.ascontiguousarray(np.asarray(w_ple_gate, f32)[0]),
        "w_pp": np.ascontiguousarray(np.asarray(w_ple_proj, f32)[0]),
        "wup_f": np.ascontiguousarray(np.asarray(w_gate_up_fwd, f32)[0]),
        "wup_b": np.ascontiguousarray(np.asarray(w_gate_up_bwd, f32)[0]),
        "bg_f": np.ascontiguousarray(np.asarray(b_gate_fwd, f32)[0].reshape(1, 256)),
        "bg_b": np.ascontiguousarray(np.asarray(b_gate_bwd, f32)[0].reshape(1, 256)),
        "mixn": np.ascontiguousarray(np.asarray(mix_norm, f32)[0].reshape(8, 128).T),
        "plen": np.ascontiguousarray(np.asarray(ple_norm, f32)[0].reshape(8, 128).T),
        "qn": rep(np.asarray(q_norm)[0], 64),
        "kn": rep(np.asarray(k_norm)[0], 64),
        "gn": rep(np.asarray(gla_norm)[0], 128),
        "fn": rep(np.asarray(final_norm), 1024),
        "ident": np.eye(128, dtype=f32).astype(ml_dtypes.bfloat16),
        "tri": np.ascontiguousarray(tri.reshape(128, 512)),
    }
    in_maps = []
    for c in range(NCORES):
        sq, hf = c // 2, c % 2
        own_p = slice(2048 * c, 2048 * (c + 1))
        own_s = slice(2048 * hf, 2048 * (hf + 1))
        m = np.zeros((NT_CTX, 2), f32)
        m[0:16 * c, 0] = 1.0
        m[16 * (c + 1):128, 1] = 1.0
        if hf == 1:
            m[128:144, 0] = 1.0
        else:
            m[144:160, 1] = 1.0
        d = dict(common)
        d["x_all"] = np.ascontiguousarray(np.concatenate([xp, xsm[sq]], axis=0))
        d["x_own"] = np.ascontiguousarray(np.concatenate([xp[own_p], xsm[sq][own_s]], axis=0))
        d["p_own"] = np.ascontiguousarray(np.concatenate([pp[own_p], psm[sq][own_s]], axis=0))
        d["tab_all"] = np.ascontiguousarray(np.concatenate([tab_p, tab_s], axis=0))
        d["tab_own"] = np.ascontiguousarray(np.concatenate([tab_p[own_p], tab_s[own_s]], axis=0))
        d["masks"] = np.ascontiguousarray(np.broadcast_to(m.reshape(1, -1), (128, NT_CTX * 2)))
        in_maps.append(d)
    if "nc" not in _NC_CACHE:
        _NC_CACHE["nc"] = build_program()
    res = run_bass_kernel_spmd(_NC_CACHE["nc"], in_maps, core_ids=list(range(NCORES)))
    ys = [np.asarray(r["y"], f32) for r in res.results]
    y_prompt = np.concatenate([y[0:2048] for y in ys], axis=0)[None]
    y_sample = np.stack([np.concatenate([ys[2 * s][2048:], ys[2 * s + 1][2048:]], axis=0) for s in range(4)])
    return (y_prompt.astype(f32), y_sample.astype(f32))
```
